# Optimizing a Trainium2 kernel written in Bass

```python
import math
import jax
import jax.numpy as jnp
from jax import lax
import numpy as np

D_MODEL = 2048
BATCH = 1
SEQ = 8192
DEPTH = 2

GRID_W = 64
CTX_LEN = 256
D_MIX = D_MODEL
Q_BLOCK = 128
ROPE_THETA = 10000.0
EPS = 1e-6
GROUP_W = D_MIX // 4

A_HEAD_DIM = 64
A_Q_HEADS = GROUP_W // A_HEAD_DIM
A_KV_HEADS = 2
A_COLS = GROUP_W + 2 * A_KV_HEADS * A_HEAD_DIM

HY_WIDTH = GROUP_W
HY_SHORT = 3
HY_EMB = 33
HY_BANDS = (HY_EMB - 1) // 2
HY_FFN = 64
HY_FAST_DECAY = 0.3
HY_SLOW_DECAY = 1.5
HY_TARGET = 1e-2
HY_COLS = 3 * HY_WIDTH

DN_WIDTH = GROUP_W
DN_HEAD_DIM = 128
DN_HEADS = DN_WIDTH // DN_HEAD_DIM
DN_SHORT = 3
DN_CHUNK = 64
DN_COLS = 4 * DN_WIDTH + 4 * DN_HEADS

DF_HEAD_DIM = 64
DF_HEADS = GROUP_W // (2 * DF_HEAD_DIM)
DF_COLS = 3 * GROUP_W

D_FF = 5632
FFN_CONV = 3

IN_COLS = A_COLS + HY_COLS + DN_COLS + DF_COLS

kernel_name = 'hybrid_parallel_heads_flow_block'


def rmsnorm(x, gain):
    xf = x.astype(jnp.float32)
    y = xf * lax.rsqrt(jnp.mean(xf * xf, axis=-1, keepdims=True) + EPS)
    return (y * gain.astype(jnp.float32)).astype(x.dtype)


def l2norm(x):
    xf = x.astype(jnp.float32)
    return (xf * lax.rsqrt(jnp.sum(xf * xf, axis=-1, keepdims=True) + EPS)).astype(x.dtype)


def modulate(h, shift, scale):
    return h * (1 + scale) + shift


def dwconv_centred(x, w):
    k = w.shape[0]
    return lax.conv_general_dilated(
        x, w[:, None, :].astype(x.dtype), window_strides=(1,),
        padding=[(k // 2, k // 2)], dimension_numbers=('NWC', 'WIO', 'NWC'),
        feature_group_count=x.shape[-1])


def axial_rope(rows, dim):
    row = jnp.repeat(jnp.arange(rows, dtype=jnp.float32), GRID_W)
    col = jnp.tile(jnp.arange(GRID_W, dtype=jnp.float32), rows)
    n_freq = dim // 4
    inv = ROPE_THETA ** (-jnp.arange(n_freq, dtype=jnp.float32) / n_freq)
    ang = jnp.concatenate([row[:, None] * inv, col[:, None] * inv], axis=-1)
    return jnp.cos(ang), jnp.sin(ang)


def apply_rope(x, cos, sin):
    shape = (cos.shape[0],) + (1,) * (x.ndim - 3) + (cos.shape[1],)
    cos = cos.reshape(shape)
    sin = sin.reshape(shape)
    x1, x2 = jnp.split(x.astype(jnp.float32), 2, axis=-1)
    return jnp.concatenate([x1 * cos - x2 * sin, x1 * sin + x2 * cos], axis=-1).astype(x.dtype)


def gqa_blocks(q, k, v):
    b, s, hq, d = q.shape
    hkv = k.shape[2]
    nb = s // Q_BLOCK
    qb = q.reshape(b, nb, Q_BLOCK, hkv, hq // hkv, d).transpose(1, 0, 3, 4, 2, 5)
    scale = d ** -0.5

    def one(qblk):
        sc = jnp.einsum('bhgqd,bkhd->bhgqk', qblk, k).astype(jnp.float32) * scale
        p = jax.nn.softmax(sc, axis=-1).astype(v.dtype)
        return jnp.einsum('bhgqk,bkhd->bhgqd', p, v)

    o = lax.map(one, qb)
    return o.transpose(1, 0, 4, 2, 3, 5).reshape(b, s, hq * d)


def diff_blocks(q, k, v, lam):
    b, s, h, _, d = q.shape
    nb = s // Q_BLOCK
    qb = q.reshape(b, nb, Q_BLOCK, h, 2, d).transpose(1, 0, 3, 4, 2, 5)
    scale = d ** -0.5

    def one(qblk):
        sc = jnp.einsum('bhmqd,bkhmd->bhmqk', qblk, k).astype(jnp.float32) * scale
        p = jax.nn.softmax(sc, axis=-1)
        a = (p[:, :, 0] - lam * p[:, :, 1]).astype(v.dtype)
        return jnp.einsum('bhqk,bkhe->bhqe', a, v)

    o = lax.map(one, qb)
    return o.transpose(1, 0, 3, 2, 4).reshape(b, s, h, v.shape[-1])


def mixer_gqa(p_lat, p_ctx, q_gain, k_gain, rope, update_ctx):
    def heads(p):
        b, l, _ = p.shape
        q, k, v = jnp.split(p, [GROUP_W, GROUP_W + A_KV_HEADS * A_HEAD_DIM], axis=-1)
        q = rmsnorm(q.reshape(b, l, A_Q_HEADS, A_HEAD_DIM), q_gain)
        k = rmsnorm(k.reshape(b, l, A_KV_HEADS, A_HEAD_DIM), k_gain)
        return q, k, v.reshape(b, l, A_KV_HEADS, A_HEAD_DIM)

    q, k, v = heads(p_lat)
    qc, kc, vc = heads(p_ctx)
    cos, sin = rope
    q = apply_rope(q, cos, sin)
    k = apply_rope(k, cos, sin)
    y = gqa_blocks(q, jnp.concatenate([kc, k], axis=1), jnp.concatenate([vc, v], axis=1))
    y_ctx = gqa_blocks(qc, kc, vc) if update_ctx else None
    return y, y_ctx


def mixer_diff(p_lat, p_ctx, lam_vecs, sub_gain, lam_init, rope, update_ctx):
    def heads(p):
        b, l, _ = p.shape
        q, k, v = jnp.split(p, 3, axis=-1)
        return (q.reshape(b, l, DF_HEADS, 2, DF_HEAD_DIM),
                k.reshape(b, l, DF_HEADS, 2, DF_HEAD_DIM),
                v.reshape(b, l, DF_HEADS, 2 * DF_HEAD_DIM))

    lv = lam_vecs.astype(jnp.float32)
    lam = jnp.exp(jnp.sum(lv[0] * lv[1])) - jnp.exp(jnp.sum(lv[2] * lv[3])) + lam_init

    def finish(o):
        b, l = o.shape[:2]
        return (rmsnorm(o, sub_gain) * (1.0 - lam_init)).reshape(b, l, GROUP_W)

    q, k, v = heads(p_lat)
    qc, kc, vc = heads(p_ctx)
    cos, sin = rope
    q = apply_rope(q, cos, sin)
    k = apply_rope(k, cos, sin)
    y = finish(diff_blocks(q, jnp.concatenate([kc, k], axis=1), jnp.concatenate([vc, v], axis=1), lam))
    y_ctx = finish(diff_blocks(qc, kc, vc, lam)) if update_ctx else None
    return y, y_ctx


def hyena_filters(l, w1, b1, w2, b2, w3, b3, w4, freq):
    f32 = jnp.float32
    t = jnp.linspace(0.0, 1.0, l, dtype=f32)[:, None]
    w = 2.0 * math.pi * jnp.arange(l, dtype=f32)[:, None] / l
    f = jnp.linspace(1e-4, HY_BANDS - 1, HY_BANDS, dtype=f32)[None, :]
    z = jnp.concatenate([t, jnp.cos(f * w), -jnp.sin(f * w)], axis=-1)
    fr = freq.astype(f32)
    h = jnp.sin(fr * (z @ w1.astype(f32) + b1.astype(f32)))
    h = jnp.sin(fr * (h @ w2.astype(f32) + b2.astype(f32)))
    h = jnp.sin(fr * (h @ w3.astype(f32) + b3.astype(f32)))
    h = h @ w4.astype(f32)
    min_decay = math.log(HY_TARGET) / HY_SLOW_DECAY
    max_decay = math.log(HY_TARGET) / HY_FAST_DECAY
    deltas = jnp.linspace(min_decay, max_decay, HY_WIDTH, dtype=f32)
    window = jnp.exp(-t * jnp.abs(deltas))
    h = h.reshape(l, 2, HY_WIDTH) * window[:, None, :]
    return h[:, 0], h[:, 1]


def bidir_long_conv(u, h_fwd, h_bwd, d_skip):
    l, ch = u.shape[1], u.shape[2]
    kern = jnp.concatenate([h_fwd, jnp.zeros((1, ch), jnp.float32), h_bwd[:0:-1]], axis=0)
    uf = u.astype(jnp.float32)
    uk = jnp.fft.rfft(uf, n=2 * l, axis=1) * jnp.fft.rfft(kern, n=2 * l, axis=0)[None]
    y = jnp.fft.irfft(uk, n=2 * l, axis=1)[:, :l]
    return (y + uf * d_skip.astype(jnp.float32)).astype(u.dtype)


def mixer_hyena(p_lat, p_ctx, short_w, filt, d_skip, update_ctx):
    def run(p):
        uc = dwconv_centred(p, short_w)
        x0, x1, v = jnp.split(uc, 3, axis=-1)
        h_fwd, h_bwd = hyena_filters(p.shape[1], *filt)
        return x0 * bidir_long_conv(x1 * v, h_fwd, h_bwd, d_skip)

    return run(p_lat), (run(p_ctx) if update_ctx else None)


def gated_delta_chunks(q, k, v, beta, g, state0):
    b, l, h, dk = q.shape
    dv = v.shape[-1]
    n = l // DN_CHUNK
    f32 = jnp.float32

    def to_chunks(t):
        t = t.astype(f32).reshape((b, n, DN_CHUNK) + t.shape[2:])
        return jnp.swapaxes(t, 2, 3)

    q, k, v, beta, g = (to_chunks(t) for t in (q, k, v, beta, g))
    gc = jnp.cumsum(g, axis=-1)
    idx = jnp.arange(DN_CHUNK)
    incl = idx[:, None] >= idx[None, :]
    strict = idx[:, None] > idx[None, :]
    gamma = jnp.exp(jnp.where(incl, gc[..., :, None] - gc[..., None, :], -jnp.inf))
    kb = k * beta[..., None]
    m = jnp.where(strict, jnp.einsum('bnhid,bnhjd->bnhij', kb, k) * gamma, 0.0)
    a = m + jnp.eye(DN_CHUNK, dtype=f32)
    rhs = jnp.concatenate([v * beta[..., None], kb * jnp.exp(gc)[..., None]], axis=-1)
    sol = lax.linalg.triangular_solve(a, rhs, left_side=True, lower=True, unit_diagonal=True)
    u, w = sol[..., :dv], sol[..., dv:]
    qk = jnp.einsum('bnhid,bnhjd->bnhij', q, k) * gamma
    q_dec = q * jnp.exp(gc)[..., None]
    k_dec = k * jnp.exp(gc[..., -1:] - gc)[..., None]
    last = jnp.exp(gc[..., -1])

    def step(s, xs):
        u_c, w_c, qk_c, qd_c, kd_c, l_c = xs
        v_new = u_c - jnp.einsum('bhcd,bhde->bhce', w_c, s)
        o = jnp.einsum('bhcd,bhde->bhce', qd_c, s) + jnp.einsum('bhij,bhje->bhie', qk_c, v_new)
        s = s * l_c[..., None, None] + jnp.einsum('bhcd,bhce->bhde', kd_c, v_new)
        return s, o

    xs = tuple(jnp.moveaxis(t, 1, 0) for t in (u, w, qk, q_dec, k_dec, last))
    s_final, o = lax.scan(step, state0, xs)
    o = o.transpose(1, 0, 3, 2, 4).reshape(b, l, h, dv)
    return o, s_final


def mixer_deltanet(p_lat, p_ctx, short_w, a_log, dt_bias, o_gain, update_ctx):
    w_, h_ = DN_WIDTH, DN_HEADS

    def prep(p):
        b, l, _ = p.shape
        qkv, gate, bf, bb, af, ab = jnp.split(
            p, [3 * w_, 4 * w_, 4 * w_ + h_, 4 * w_ + 2 * h_, 4 * w_ + 3 * h_], axis=-1)
        qkv = jax.nn.silu(dwconv_centred(qkv, short_w))
        q, k, v = (t.reshape(b, l, h_, DN_HEAD_DIM) for t in jnp.split(qkv, 3, axis=-1))
        q = l2norm(q) * (DN_HEAD_DIM ** -0.5)
        k = l2norm(k)
        beta = jax.nn.sigmoid(jnp.stack([bf, bb]).astype(jnp.float32))
        g = -jnp.exp(a_log.astype(jnp.float32))[:, None, None, :] * jax.nn.softplus(
            jnp.stack([af, ab]).astype(jnp.float32) + dt_bias.astype(jnp.float32)[:, None, None, :])
        return q, k, v, gate, beta, g

    q, k, v, gate, beta, g = prep(p_lat)
    qc, kc, vc, gate_c, beta_c, g_c = prep(p_ctx)
    s0 = jnp.zeros((p_lat.shape[0], h_, DN_HEAD_DIM, DN_HEAD_DIM), jnp.float32)
    rev = lambda t: jnp.flip(t, axis=1)
    oc_f, s_f = gated_delta_chunks(qc, kc, vc, beta_c[0], g_c[0], s0)
    o_f, _ = gated_delta_chunks(q, k, v, beta[0], g[0], s_f)
    oc_b, s_b = gated_delta_chunks(rev(qc), rev(kc), rev(vc), rev(beta_c[1]), rev(g_c[1]), s0)
    o_b, _ = gated_delta_chunks(rev(q), rev(k), rev(v), rev(beta[1]), rev(g[1]), s_b)

    def finish(o_fwd, o_bwd_rev, gt):
        b, l = gt.shape[:2]
        o = (o_fwd + rev(o_bwd_rev)).astype(gt.dtype)
        return rmsnorm(o, o_gain).reshape(b, l, w_) * jax.nn.silu(gt)

    y = finish(o_f, o_b, gate)
    y_ctx = finish(oc_f, oc_b, gate_c) if update_ctx else None
    return y, y_ctx


def conv_ffn(h, w_up, w_conv, w_down):
    u = dwconv_centred(h @ w_up, w_conv)
    a, b = jnp.split(u, 2, axis=-1)
    return (jax.nn.silu(a) * b) @ w_down


def setup_inputs(seed: int = 0) -> dict:
    key = jax.random.key(seed)
    ks = iter(jax.random.split(key, 40))
    f32 = jnp.float32

    def nrm(shape, scale):
        return jax.random.normal(next(ks), shape, f32) * scale

    def gain(shape):
        return 1.0 + nrm(shape, 0.05)

    L_ = DEPTH
    dt = jnp.exp(jax.random.uniform(next(ks), (L_, 2, DN_HEADS), f32, math.log(1e-3), math.log(1e-1)))
    return {
        'x': nrm((BATCH, SEQ, D_MODEL), 1.0),
        'c': nrm((BATCH, D_MODEL), 1.0),
        'ctx': nrm((BATCH, CTX_LEN, D_MODEL), 1.0),
        'c_ctx': nrm((D_MODEL,), 1.0),
        'w_ada': nrm((L_, D_MODEL, 6 * D_MODEL), 0.5 * D_MODEL ** -0.5),
        'b_ada': nrm((L_, 6 * D_MODEL), 0.02),
        'norm_mix_pre': gain((L_, D_MODEL)),
        'norm_mix_post': gain((L_, D_MODEL)),
        'norm_ffn_pre': gain((L_, D_MODEL)),
        'norm_ffn_post': gain((L_, D_MODEL)),
        'w_in': nrm((L_, D_MODEL, IN_COLS), D_MODEL ** -0.5),
        'w_out': nrm((L_, D_MIX, D_MODEL), D_MIX ** -0.5),
        'attn_q_norm': gain((L_, A_HEAD_DIM)),
        'attn_k_norm': gain((L_, A_HEAD_DIM)),
        'hy_short': nrm((L_, HY_SHORT, HY_COLS), HY_SHORT ** -0.5),
        'hy_w1': nrm((L_, HY_EMB, HY_FFN), HY_EMB ** -0.5),
        'hy_b1': nrm((L_, HY_FFN), 0.02),
        'hy_w2': nrm((L_, HY_FFN, HY_FFN), HY_FFN ** -0.5),
        'hy_b2': nrm((L_, HY_FFN), 0.02),
        'hy_w3': nrm((L_, HY_FFN, HY_FFN), HY_FFN ** -0.5),
        'hy_b3': nrm((L_, HY_FFN), 0.02),
        'hy_w4': nrm((L_, HY_FFN, 2 * HY_WIDTH), 0.03 * HY_FFN ** -0.5),
        'hy_freq': gain((L_, HY_FFN)),
        'hy_skip': nrm((L_, HY_WIDTH), 0.5),
        'dn_short': nrm((L_, DN_SHORT, 3 * DN_WIDTH), DN_SHORT ** -0.5),
        'dn_a_log': jnp.log(jax.random.uniform(next(ks), (L_, 2, DN_HEADS), f32, 1.0, 16.0)),
        'dn_dt_bias': dt + jnp.log(-jnp.expm1(-dt)),
        'dn_norm': gain((L_, DN_HEAD_DIM)),
        'df_lambda': nrm((L_, 4, DF_HEAD_DIM), 0.1),
        'df_norm': gain((L_, 2 * DF_HEAD_DIM)),
        'ffn_up': nrm((L_, D_MODEL, 2 * D_FF), D_MODEL ** -0.5),
        'ffn_conv': nrm((L_, FFN_CONV, 2 * D_FF), FFN_CONV ** -0.5),
        'ffn_down': nrm((L_, D_FF, D_MODEL), D_FF ** -0.5),
    }


def reference(x, c, ctx, c_ctx, w_ada, b_ada, norm_mix_pre, norm_mix_post, norm_ffn_pre,
              norm_ffn_post, w_in, w_out, attn_q_norm, attn_k_norm, hy_short, hy_w1, hy_b1,
              hy_w2, hy_b2, hy_w3, hy_b3, hy_w4, hy_freq, hy_skip, dn_short, dn_a_log,
              dn_dt_bias, dn_norm, df_lambda, df_norm, ffn_up, ffn_conv, ffn_down):
    rows = x.shape[1] // GRID_W
    rope = axial_rope(rows, A_HEAD_DIM)
    splits = [A_COLS, A_COLS + HY_COLS, A_COLS + HY_COLS + DN_COLS]
    xc = ctx
    for i in range(DEPTH):
        update_ctx = i < DEPTH - 1
        lam_init = 0.8 - 0.6 * math.exp(-0.3 * i)
        mod = jax.nn.silu(c) @ w_ada[i] + b_ada[i]
        mod_c = jax.nn.silu(c_ctx) @ w_ada[i] + b_ada[i]
        sh1, sc1, gt1, sh2, sc2, gt2 = jnp.split(mod[:, None, :], 6, axis=-1)
        sh1c, sc1c, gt1c, sh2c, sc2c, gt2c = jnp.split(mod_c, 6, axis=-1)

        p_lat = modulate(rmsnorm(x, norm_mix_pre[i]), sh1, sc1) @ w_in[i]
        p_ctx = modulate(rmsnorm(xc, norm_mix_pre[i]), sh1c, sc1c) @ w_in[i]
        a_l, b_l, c_l, d_l = jnp.split(p_lat, splits, axis=-1)
        a_c, b_c, c_c, d_c = jnp.split(p_ctx, splits, axis=-1)
        ya, ya_c = mixer_gqa(a_l, a_c, attn_q_norm[i], attn_k_norm[i], rope, update_ctx)
        yb, yb_c = mixer_hyena(b_l, b_c, hy_short[i],
                               (hy_w1[i], hy_b1[i], hy_w2[i], hy_b2[i], hy_w3[i], hy_b3[i],
                                hy_w4[i], hy_freq[i]), hy_skip[i], update_ctx)
        yc, yc_c = mixer_deltanet(c_l, c_c, dn_short[i], dn_a_log[i], dn_dt_bias[i], dn_norm[i], update_ctx)
        yd, yd_c = mixer_diff(d_l, d_c, df_lambda[i], df_norm[i], lam_init, rope, update_ctx)
        y = jnp.concatenate([ya, yb, yc, yd], axis=-1) @ w_out[i]
        x = x + gt1 * rmsnorm(y, norm_mix_post[i])
        if update_ctx:
            y_c = jnp.concatenate([ya_c, yb_c, yc_c, yd_c], axis=-1) @ w_out[i]
            xc = xc + gt1c * rmsnorm(y_c, norm_mix_post[i])

        h = modulate(rmsnorm(x, norm_ffn_pre[i]), sh2, sc2)
        x = x + gt2 * rmsnorm(conv_ffn(h, ffn_up[i], ffn_conv[i], ffn_down[i]), norm_ffn_post[i])
        if update_ctx:
            hc = modulate(rmsnorm(xc, norm_ffn_pre[i]), sh2c, sc2c)
            xc = xc + gt2c * rmsnorm(conv_ffn(hc, ffn_up[i], ffn_conv[i], ffn_down[i]), norm_ffn_post[i])
    return x
```

```python
import math
import numpy as np
import contextlib
import concourse.bass as bass
import concourse.mybir as mybir
from concourse.bass_utils import run_bass_kernel_spmd

F32 = mybir.dt.float32
BF16 = mybir.dt.bfloat16
AF = mybir.ActivationFunctionType
ALU = mybir.AluOpType
AX = mybir.AxisListType


class View:
    def __init__(self, buf, ap):
        self.buf = buf
        self.ap = ap


class Buf:
    def __init__(self, t, name):
        self.t = t
        self.name = name
        self.wr = {}
        self.rd = {}

    def __getitem__(self, idx):
        return View(self, self.t[idx])

    def v(self, ap):
        return View(self, ap)

    def re(self, pat, **kw):
        return ReView(self, self.t.rearrange(pat, **kw))


class ReView:
    def __init__(self, buf, ap):
        self.buf = buf
        self.ap = ap

    def __getitem__(self, idx):
        return View(self.buf, self.ap[idx])


def _aps(x):
    return x.ap if isinstance(x, View) else x


class Prog:
    NDMASEM = 6

    def __init__(self, nc, stack):
        self.nc = nc
        self.stack = stack
        self.streams = ['pe', 'dve', 'act', 'pool', 'sp']
        self.sems = {}
        self.cnt = {}
        for k in ['pe', 'dve', 'act', 'pool']:
            self.sems[k] = stack.enter_context(nc.semaphore('s_' + k))
            self.cnt[k] = 0
        self.dq = {}
        for q in ['sp', 'act', 'pool']:
            keys = []
            for i in range(self.NDMASEM):
                k = 'd_%s_%d' % (q, i)
                self.sems[k] = stack.enter_context(nc.semaphore(k))
                self.cnt[k] = 0
                keys.append(k)
            self.dq[q] = [keys, 0]
        self.seen = {e: {} for e in self.streams}
        self.rec = {e: [] for e in self.streams}
        self.nbuf = 0
        self.dmarr = 0

    def sbuf(self, shape, dt, name=None):
        self.nbuf += 1
        name = name or ('sb%d' % self.nbuf)
        t = self.stack.enter_context(self.nc.sbuf_tensor(name, list(shape), dt))
        return Buf(t, name)

    def psum(self, shape, dt, name=None):
        self.nbuf += 1
        name = name or ('ps%d' % self.nbuf)
        t = self.stack.enter_context(self.nc.psum_tensor(name, list(shape), dt))
        return Buf(t, name)

    def dram(self, name, shape, dt, kind):
        t = self.nc.dram_tensor(name, list(shape), dt, kind=kind)
        return Buf(t.ap(), name)

    def _waits(self, e, reads, writes, extra=(), nowaw=False):
        need = {}

        def add(dep):
            if dep is None:
                return
            k, c = dep
            if need.get(k, 0) < c:
                need[k] = c
        raw_self = 0
        for b in reads:
            for k, c in b.wr.items():
                add((k, c))
                if k == e:
                    raw_self = max(raw_self, c)
        for b in writes:
            if not nowaw:
                for k, c in b.wr.items():
                    add((k, c))
            for k, c in b.rd.items():
                add((k, c))
        for d in extra:
            add(d)
        ws = []
        if e in need:
            del need[e]
        if raw_self > 0 and e != 'pe':
            need[e] = raw_self
        for k, c in need.items():
            if self.seen[e].get(k, 0) >= c:
                continue
            ws.append((self.sems[k], c))
            self.seen[e][k] = c
        return ws

    def _mark(self, k, c, reads, writes, nowaw):
        for b in reads:
            b.rd[k] = c
        for b in writes:
            if nowaw:
                b.wr[k] = c
            else:
                b.wr = {k: c}
                b.rd = {}

    def op(self, e, fn, reads=(), writes=(), nowaw=False):
        reads = [r.buf if isinstance(r, View) else r for r in reads if r is not None and not isinstance(r, (int, float))]
        writes = [w.buf if isinstance(w, View) else w for w in writes]
        ws = self._waits(e, reads, writes, nowaw=nowaw)
        self.cnt[e] += 1
        self.rec[e].append((ws, fn, self.sems[e], 1))
        self._mark(e, self.cnt[e], reads, writes, nowaw)

    def dma(self, out, in_, q=None, nowaw=False, **kw):
        if q is None:
            q = ['sp', 'pool'][self.dmarr % 2]
            self.dmarr += 1
        reads = [in_.buf]
        writes = [out.buf]
        keys, idx = self.dq[q]
        k = keys[idx % len(keys)]
        self.dq[q][1] += 1
        prev = (k, self.cnt[k]) if self.cnt[k] > 0 else None
        ws = self._waits(q, reads, writes, extra=(prev,) if prev else (), nowaw=nowaw)
        self.cnt[k] += 16
        oa, ia = out.ap, in_.ap
        self.rec[q].append((ws, (lambda e: e.dma_start(out=oa, in_=ia, **kw)), self.sems[k], 16))
        self._mark(k, self.cnt[k], reads, writes, nowaw)

    def mm(self, out, lhsT, rhs, start=True, stop=True):
        o, l, r = out.ap, lhsT.ap, rhs.ap
        self.op('pe', lambda e: e.matmul(o, lhsT=l, rhs=r, start=start, stop=stop), reads=[lhsT, rhs], writes=[out])

    def transpose(self, out, in_, ident):
        o, i, d = out.ap, in_.ap, ident.ap
        self.op('pe', lambda e: e.transpose(o, i, d), reads=[in_, ident], writes=[out])

    def act(self, out, in_, func, scale=1.0, bias=None, eng='act', accum_out=None, nowaw=False):
        o, i = out.ap, in_.ap
        s = _aps(scale)
        b = _aps(bias)
        kw = {}
        if bias is not None:
            kw['bias'] = b
        if accum_out is not None:
            kw['accum_out'] = accum_out.ap
        wr = [out] + ([accum_out] if accum_out is not None else [])
        self.op('act', lambda e: e.activation(out=o, in_=i, func=func, scale=s, **kw),
                reads=[in_, scale if isinstance(scale, View) else None, bias if isinstance(bias, View) else None], writes=wr, nowaw=nowaw)

    def tt(self, out, in0, in1, op, eng='dve'):
        o, a, b = out.ap, in0.ap, in1.ap
        self.op(eng, lambda e: e.tensor_tensor(out=o, in0=a, in1=b, op=op), reads=[in0, in1], writes=[out])

    def ts(self, out, in0, s1, op0, s2=None, op1=None, eng='dve', accum_out=None, nowaw=False):
        o, a = out.ap, in0.ap
        x1, x2 = _aps(s1), _aps(s2)
        kw = {}
        if op1 is not None:
            kw['op1'] = op1
        if accum_out is not None:
            kw['accum_out'] = accum_out.ap
        wr = [out] + ([accum_out] if accum_out is not None else [])
        self.op(eng, lambda e: e.tensor_scalar(out=o, in0=a, scalar1=x1, scalar2=x2, op0=op0, **kw),
                reads=[in0, s1 if isinstance(s1, View) else None, s2 if isinstance(s2, View) else None], writes=wr, nowaw=nowaw)

    def stt(self, out, in0, scalar, in1, op0, op1):
        o, a, b = out.ap, in0.ap, in1.ap
        s = _aps(scalar)
        self.op('dve', lambda e: e.scalar_tensor_tensor(out=o, in0=a, scalar=s, in1=b, op0=op0, op1=op1),
                reads=[in0, in1, scalar if isinstance(scalar, View) else None], writes=[out])

    def copy(self, out, in_, eng='dve', nowaw=False):
        o, i = out.ap, in_.ap
        self.op(eng, lambda e: e.tensor_copy(out=o, in_=i), reads=[in_], writes=[out], nowaw=nowaw)

    def recip(self, out, in_):
        o, i = out.ap, in_.ap
        self.op('dve', lambda e: e.reciprocal(out=o, in_=i), reads=[in_], writes=[out])

    def memset(self, out, val, eng='dve'):
        o = out.ap
        self.op(eng, lambda e: e.memset(o, val), reads=[], writes=[out])

    def finish(self, bufs, e='sp'):
        ws = self._waits(e, bufs, [])
        self.rec[e].append((ws, None, None, 0))
        rec = self.rec

        def replay(lst):
            def f(eng):
                for ws, fn, sem, inc in lst:
                    for (s_, c_) in ws:
                        eng.wait_ge(s_, c_)
                    if fn is not None:
                        fn(eng).then_inc(sem, inc)
            return f
        with self.nc.Block() as block:
            if rec['sp']:
                block.sync(replay(rec['sp']))
            if rec['pe']:
                block.tensor(replay(rec['pe']))
            if rec['dve']:
                block.vector(replay(rec['dve']))
            if rec['act']:
                block.scalar(replay(rec['act']))
            if rec['pool']:
                block.gpsimd(replay(rec['pool']))


def new_prog():
    nc = bass.Bass("TRN2", target_bir_lowering=False)
    st = contextlib.ExitStack()
    return nc, st, Prog(nc, st)


D = 2048
NT = 1056
NL = 1024
IN_COLS = 5904
EPS = 1e-6


def build_k0():
    nc, st, P = new_prog()
    with st:
        cc = P.dram("cc", [128, 32], F32, "ExternalInput")
        w = P.dram("w", [2048, 3072], F32, "ExternalInput")
        b2 = P.dram("b2", [2, 3072], F32, "ExternalInput")
        mod = P.dram("mod", [2, 3072], F32, "ExternalOutput")
        cs = P.sbuf([128, 32], F32)
        ca = P.sbuf([128, 32], F32)
        bs = P.sbuf([2, 3072], F32)
        ms = P.sbuf([2, 3072], F32)
        wb = [P.sbuf([128, 3072], F32) for _ in range(4)]
        ps = [P.psum([128, 512], F32) for _ in range(6)]
        P.dma(cs[:], cc[:])
        P.dma(bs[:], b2[:])
        P.act(ca[:], cs[:], AF.Silu)
        for kc in range(16):
            wt = wb[kc % 4]
            P.dma(wt[:], w[kc * 128:(kc + 1) * 128, :])
            for g in range(6):
                P.mm(ps[g][0:2, :], ca[:, 2 * kc:2 * kc + 2], wt[:, g * 512:(g + 1) * 512], start=(kc == 0), stop=(kc == 15))
        for g in range(6):
            P.tt(ms[0:2, g * 512:(g + 1) * 512], ps[g][0:2, :], bs[0:2, g * 512:(g + 1) * 512], ALU.add)
        P.dma(mod[:], ms[:])
        P.finish([mod])
    return nc


def k1_groups():
    g = []
    for m in range(34):
        c0 = m * 128
        kind = None
        if m < 4:
            kind = 'gq_q'
        elif m == 4:
            kind = 'gq_k'
        g.append((c0, 128, kind))
    g.append((4352, 16, None))
    for m in range(12):
        c0 = 4368 + m * 128
        g.append((c0, 128, 'rope' if m < 8 else None))
    return g


def build_k1():
    nc, st, P = new_prog()
    with st:
        xT = P.dram("xT", [D, NT], F32, "ExternalInput")
        modT = P.dram("modT", [128, 96 * 2], F32, "ExternalInput")
        gain = P.dram("gain", [128, 16], F32, "ExternalInput")
        w_in = P.dram("w_in", [D, IN_COLS], F32, "ExternalInput")
        qkg = P.dram("qkg", [128, 2], F32, "ExternalInput")
        cosT = P.dram("cosT", [128, NL], F32, "ExternalInput")
        sinT = P.dram("sinT", [128, NL], F32, "ExternalInput")
        cmat = P.dram("cmat", [128, 3 * 128], F32, "ExternalInput")
        pT = P.dram("pT", [IN_COLS, NT], F32, "ExternalOutput")

        xs = P.sbuf([128, 16, NT], F32, "xs")
        hT = P.sbuf([128, 16, NT], BF16, "hT")
        mods = P.sbuf([128, 96, 2], F32, "mods")
        gs = P.sbuf([128, 16], F32, "gs")
        qk = P.sbuf([128, 2], F32, "qk")
        cs_ = P.sbuf([128, NL], F32, "cos")
        sn_ = P.sbuf([128, NL], F32, "sin")
        cm = P.sbuf([128, 384], F32, "cm")
        A = P.sbuf([128, 16, 2], F32, "A")
        rstd = P.sbuf([128, NT], F32, "rstd")
        tmp = [P.sbuf([128, NT], F32, "tmp%d" % i) for i in range(2)]
        banks = [P.psum([128, 512], F32, "bank%d" % i) for i in range(8)]

        xTr = xT.re("(kc k) n -> k kc n", k=128)
        for kc in range(16):
            P.dma(xs[:, kc, :], xTr[:, kc, :])
        P.dma(mods[:], modT.re("k (c r) -> k c r", r=2)[:, :, :])
        P.dma(gs[:], gain[:])
        P.dma(qk[:], qkg[:])
        P.dma(cs_[:], cosT[:])
        P.dma(sn_[:], sinT[:])
        P.dma(cm[:], cmat[:])
        ones = cm[:, 0:128]
        bo = cm[:, 128:256]
        RT = cm[:, 256:384]

        for r in range(2):
            P.ts(A[:, :, r], mods[:, 16:32, r], 1.0, ALU.add)
            P.tt(A[:, :, r], A[:, :, r], gs[:], ALU.mult)
        for kc in range(16):
            sq = tmp[kc % 2]
            P.act(sq[:], xs[:, kc, :], AF.Square)
            for tg in range(3):
                P.mm(banks[tg][:, 0:352], ones, sq[:, tg * 352:(tg + 1) * 352], start=(kc == 0), stop=(kc == 15))
        for tg in range(3):
            P.act(rstd[:, tg * 352:(tg + 1) * 352], banks[tg][:, 0:352], AF.Sqrt, scale=1.0 / D, bias=EPSB(P))
        P.recip(rstd[:], rstd[:])
        for kc in range(16):
            t = tmp[kc % 2]
            P.tt(t[:], xs[:, kc, :], rstd[:], ALU.mult)
            P.act(hT[:, kc, 0:NL], t[:, 0:NL], AF.Identity, scale=A[:, kc, 0:1], bias=mods[:, kc, 0:1])
            P.ts(hT[:, kc, NL:NT], t[:, NL:NT], A[:, kc, 1:2], ALU.mult, mods[:, kc, 1:2], ALU.add)

        wst = [P.sbuf([128, 16, 256], F32, "wst%d" % i) for i in range(2)]
        wbf = [P.sbuf([128, 16, 256], BF16, "wbf%d" % i) for i in range(2)]
        pout = [P.sbuf([128, NT], F32, "pout%d" % i) for i in range(3)]
        w_r = w_in.re("(kc k) c -> k kc c", k=128)
        groups = k1_groups()
        slabs = []
        i = 0
        while i < len(groups):
            c0, n, _ = groups[i]
            if n == 128 and i + 1 < len(groups) and groups[i + 1][1] == 128 and groups[i + 1][0] == c0 + 128:
                slabs.append((c0, 256, [groups[i], groups[i + 1]]))
                i += 2
            else:
                slabs.append((c0, n, [groups[i]]))
                i += 1
        bi = 0
        gi = 0
        for si, (c0, wn, grs) in enumerate(slabs):
            ws_, wb_ = wst[si % 2], wbf[si % 2]
            for kh in range(2):
                P.dma(ws_[:, kh * 8:(kh + 1) * 8, 0:wn], w_r[:, kh * 8:(kh + 1) * 8, c0:c0 + wn])
            P.copy(wb_[:, :, 0:wn], ws_[:, :, 0:wn], eng=('dve' if si % 2 == 0 else 'pool'))
            for (gc0, gn, kind) in grs:
                off = gc0 - c0
                po = pout[gi % 3]
                gi += 1
                for tg in range(3):
                    bk = banks[3 + (bi % 5)]
                    bi += 1
                    for kc in range(16):
                        P.mm(bk[0:gn, 0:352], wb_[:, kc, off:off + gn], hT[:, kc, tg * 352:(tg + 1) * 352], start=(kc == 0), stop=(kc == 15))
                    P.act(po[0:gn, tg * 352:(tg + 1) * 352], bk[0:gn, 0:352], AF.Copy)
                if kind in ('gq_q', 'gq_k'):
                    gcol = qk[:, 0:1] if kind == 'gq_q' else qk[:, 1:2]
                    sq = tmp[0]
                    P.act(sq[:], po[:], AF.Square)
                    r2 = tmp[1]
                    for tg in range(3):
                        P.mm(banks[tg][:, 0:352], bo, sq[:, tg * 352:(tg + 1) * 352])
                        P.act(r2[:, tg * 352:(tg + 1) * 352], banks[tg][:, 0:352], AF.Sqrt, scale=1.0 / 64, bias=EPSB(P))
                    P.recip(r2[:], r2[:])
                    P.stt(po[:], po[:], gcol, r2[:], ALU.mult, ALU.mult)
                if kind is not None:
                    t1 = tmp[0]
                    for hh in range(2):
                        P.mm(banks[hh][:, 0:512], RT, po[:, hh * 512:(hh + 1) * 512])
                    P.tt(t1[:, 0:NL], po[:, 0:NL], cs_[:], ALU.mult)
                    for hh in range(2):
                        P.tt(po[:, hh * 512:(hh + 1) * 512], banks[hh][:, 0:512], sn_[:, hh * 512:(hh + 1) * 512], ALU.mult)
                    P.tt(po[:, 0:NL], po[:, 0:NL], t1[:, 0:NL], ALU.add)
                P.dma(pT[gc0:gc0 + gn, :], po[0:gn, :])
        P.finish([pT])
    return nc


def EPSB(P):
    if not hasattr(P, '_epsb'):
        P._epsb = P.sbuf([128, 1], F32, "epsb")
        P.memset(P._epsb[:], EPS)
    return P._epsb[:]


def rope_tables():
    rows = 8192 // 64
    row = np.repeat(np.arange(rows, dtype=np.float32), 64)
    col = np.tile(np.arange(64, dtype=np.float32), rows)
    n_freq = 16
    inv = (np.float32(10000.0) ** (-np.arange(n_freq, dtype=np.float32) / n_freq)).astype(np.float32)
    ang = np.concatenate([row[:, None] * inv, col[:, None] * inv], axis=-1).astype(np.float32)
    return np.cos(ang).astype(np.float32), np.sin(ang).astype(np.float32)


def const_mats():
    ones = np.ones((128, 128), np.float32)
    bo = np.zeros((128, 128), np.float32)
    bo[:64, :64] = 1
    bo[64:, 64:] = 1
    RT = np.zeros((128, 128), np.float32)
    for m in range(128):
        if (m % 64) < 32:
            RT[m + 32, m] = -1.0
        else:
            RT[m - 32, m] = 1.0
    return np.concatenate([ones, bo, RT], axis=1)


NT = 1056
NL = 1024
NK = 8448
KT = 66


def build_k2():
    nc, st, P = new_prog()
    with st:
        qT = P.dram("qT", [16, 64, NT], F32, "ExternalInput")
        kT = P.dram("kT", [10, 64, NK], F32, "ExternalInput")
        vv = P.dram("vv", [2, 128, KT * 64], F32, "ExternalInput")
        vd = P.dram("vd", [4, 128, KT * 128], F32, "ExternalInput")
        lamv = P.dram("lamv", [128, 256], F32, "ExternalInput")
        misc = P.dram("misc", [128, 4], F32, "ExternalInput")
        onesd = P.dram("ones", [128, 128], F32, "ExternalInput")
        yaT = P.dram("yaT", [512, NT], F32, "ExternalOutput")
        ydT = P.dram("ydT", [512, NT], F32, "ExternalOutput")

        stage = [P.sbuf([128, 2112], F32, "stage%d" % i) for i in range(2)]
        kbf = [P.sbuf([64, NK], BF16, "kbf%d" % i) for i in range(2)]
        qbf = [P.sbuf([64, NT], BF16, "qbf%d" % i) for i in range(2)]
        vbf = [P.sbuf([128, KT * 128], BF16, "vbf%d" % i) for i in range(2)]
        pb = [P.sbuf([128, 512], BF16, "pb%d" % i) for i in range(3)]
        ones_f = P.sbuf([128, 128], F32, "ones_f")
        ones_b = P.sbuf([128, 128], BF16, "ones_b")
        lv = P.sbuf([128, 256], F32, "lv")
        ms = P.sbuf([128, 4], F32, "ms")
        lam = P.sbuf([128, 4], F32, "lam")
        rd = P.sbuf([128, 512], F32, "rd")
        osb = [P.sbuf([128, NT], F32, "osb%d" % i) for i in range(3)]
        sq = P.sbuf([128, NT], F32, "sq")
        sbank = [P.psum([128, 512], F32, "sbank%d" % i) for i in range(3)]
        obank = [P.psum([128, 512], F32, "obank%d" % i) for i in range(2)]
        dbank = [P.psum([128, 512], F32, "dbank%d" % i) for i in range(2)]
        nbank = P.psum([128, 512], F32, "nbank")

        P.dma(ones_f[:], onesd[:])
        P.copy(ones_b[:], ones_f[:])
        P.dma(lv[:], lamv[:])
        P.dma(ms[:], misc[:])
        pr = P.sbuf([128, 128], F32, "pr")
        P.tt(pr[:, 0:64], lv[:, 0:64], lv[:, 64:128], ALU.mult)
        P.tt(pr[:, 64:128], lv[:, 128:192], lv[:, 192:256], ALU.mult)
        o0, i0 = lam[:, 0:1].ap, pr[:, 0:64].ap
        P.op('dve', lambda e: e.tensor_reduce(out=o0, in_=i0, axis=AX.X, op=ALU.add), reads=[pr], writes=[lam])
        o1, i1 = lam[:, 1:2].ap, pr[:, 64:128].ap
        P.op('dve', lambda e: e.tensor_reduce(out=o1, in_=i1, axis=AX.X, op=ALU.add), reads=[pr], writes=[lam])
        P.act(lam[:, 0:2], lam[:, 0:2], AF.Exp)
        P.tt(lam[:, 2:3], lam[:, 0:1], lam[:, 1:2], ALU.subtract)
        P.tt(lam[:, 2:3], lam[:, 2:3], ms[:, 1:2], ALU.add)
        P.ts(lam[:, 3:4], lam[:, 2:3], -1.0, ALU.mult)
        gsc = P.sbuf([128, 2], F32, "gsc")
        P.tt(gsc[:, 0:1], ms[:, 0:1], ms[:, 2:3], ALU.mult)

        cnt = {'s': 0, 'p': 0, 'a': 0, 'st': 0, 'o': 0}

        def load_cast(dst_view_fn, src_view_fn, np_, ncols_total, piece=2112):
            c = 0
            while c < ncols_total:
                n = min(piece, ncols_total - c)
                sg = stage[cnt['st'] % 2]
                cnt['st'] += 1
                P.dma(sg[0:np_, 0:n], src_view_fn(c, n))
                P.copy(dst_view_fn(c, n), sg[0:np_, 0:n], eng='pool')
                c += n

        def attend(kb, qb, vb, dv, ob):
            for (q0, qn, nkt) in [(0, 512, KT), (512, 512, KT), (1024, 32, 2)]:
                a = cnt['a'] % 2
                cnt['a'] += 1
                ob_, db_ = obank[a], dbank[a]
                for kt in range(nkt):
                    sb_ = sbank[cnt['s'] % 3]
                    cnt['s'] += 1
                    pt = pb[cnt['p'] % 3]
                    cnt['p'] += 1
                    P.mm(sb_[:, 0:qn], kb[:, kt * 128:(kt + 1) * 128], qb[:, q0:q0 + qn])
                    P.act(pt[:, 0:qn], sb_[:, 0:qn], AF.Exp, scale=0.125)
                    P.mm(ob_[0:dv, 0:qn], vb[:, kt * dv:(kt + 1) * dv], pt[:, 0:qn], start=(kt == 0), stop=(kt == nkt - 1))
                    P.mm(db_[0:dv, 0:qn], ones_b[:, 0:dv], pt[:, 0:qn], start=(kt == 0), stop=(kt == nkt - 1))
                P.recip(rd[0:dv, 0:qn], db_[0:dv, 0:qn])
                P.tt(ob[0:dv, q0:q0 + qn], ob_[0:dv, 0:qn], rd[0:dv, 0:qn], ALU.mult)

        for g in range(2):
            kb = kbf[g % 2]
            load_cast(lambda c, n: kb[:, c:c + n], lambda c, n: kT[g, :, c:c + n], 64, NK)
            vb = vbf[g % 2]
            load_cast(lambda c, n: vb[:, c:c + n], lambda c, n: vv[g, :, c:c + n], 128, KT * 64)
            for hh in range(4):
                h = g * 4 + hh
                qb = qbf[h % 2]
                load_cast(lambda c, n: qb[:, c:c + n], lambda c, n: qT[h, :, c:c + n], 64, NT)
                ob = osb[cnt['o'] % 3]
                cnt['o'] += 1
                attend(kb, qb, vb, 64, ob)
                P.dma(yaT[h * 64:(h + 1) * 64, :], ob[0:64, :])
        for h in range(4):
            vb = vbf[h % 2]
            load_cast(lambda c, n: vb[:, c:c + n], lambda c, n: vd[h, :, c:c + n], 128, KT * 128)
            obs = []
            for m in range(2):
                u = h * 2 + m
                kb = kbf[u % 2]
                load_cast(lambda c, n: kb[:, c:c + n], lambda c, n: kT[2 + u, :, c:c + n], 64, NK)
                qb = qbf[u % 2]
                load_cast(lambda c, n: qb[:, c:c + n], lambda c, n: qT[8 + u, :, c:c + n], 64, NT)
                ob = osb[cnt['o'] % 3]
                cnt['o'] += 1
                attend(kb, qb, vb, 128, ob)
                obs.append(ob)
            o0_, o1_ = obs
            P.stt(o0_[:], o1_[:], lam[:, 3:4], o0_[:], ALU.mult, ALU.add)
            P.act(sq[:], o0_[:], AF.Square)
            for tg in range(3):
                P.mm(nbank[:, 0:352], ones_f[:], sq[:, tg * 352:(tg + 1) * 352])
                P.act(o1_[:, tg * 352:(tg + 1) * 352], nbank[:, 0:352], AF.Sqrt, scale=1.0 / 128, bias=EPSB(P))
            P.recip(o1_[:], o1_[:])
            P.stt(o0_[:], o0_[:], gsc[:, 0:1], o1_[:], ALU.mult, ALU.mult)
            P.dma(ydT[h * 128:(h + 1) * 128, :], o0_[:])
        P.finish([yaT, ydT])
    return nc


LL = 8192
LC = 256


def hy_consts(L, core):
    nb = L // 128
    cb = np.arange(2 * nb)[:, None]
    jp = np.arange(128)[None, :]
    b = cb - nb
    n_f = 128 * b + 127 - jp
    n_b = 128 * (-b) - 127 + jp
    n = np.where(b >= 0, n_f, n_b)
    valid = (n >= 0) & (n < L)
    n = np.where(valid, n, 0).astype(np.float32)
    t = (n / np.float32(L - 1)).astype(np.float32)
    w = (np.float32(2.0 * math.pi) * n / np.float32(L)).astype(np.float32)
    f = np.linspace(1e-4, 15, 16, dtype=np.float32)
    z = np.concatenate([t[..., None], np.cos(f * w[..., None]), -np.sin(f * w[..., None])], axis=-1).astype(np.float32)
    zT = np.ascontiguousarray(z.reshape(2 * nb * 128, 33).T)
    min_decay = math.log(1e-2) / 1.5
    max_decay = math.log(1e-2) / 0.3
    deltas = np.linspace(min_decay, max_decay, 512, dtype=np.float32)[core * 64:(core + 1) * 64]
    win = np.exp(-t[..., None] * np.abs(deltas)).astype(np.float32) * valid[..., None]
    win = np.ascontiguousarray(win.transpose(1, 0, 2).reshape(128, 2 * nb * 64)).astype(np.float32)
    return zT, win


def build_k3():
    nc, st, P = new_prog()
    with st:
        pl = P.dram("pl", [3, 64, LL], F32, "ExternalInput")
        pc = P.dram("pc", [3, 64, LC], F32, "ExternalInput")
        swd = P.dram("sw", [64, 9], F32, "ExternalInput")
        w1d = P.dram("w1", [33, 64], F32, "ExternalInput")
        w23d = P.dram("w23", [64, 128], F32, "ExternalInput")
        w4d = P.dram("w4s", [64, 128], F32, "ExternalInput")
        vecd = P.dram("vec", [64, 8], F32, "ExternalInput")
        zld = P.dram("zl", [33, 2 * 64 * 128], F32, "ExternalInput")
        zcd = P.dram("zc", [33, 2 * 2 * 128], F32, "ExternalInput")
        wld = P.dram("wl", [128, 128 * 64], F32, "ExternalInput")
        wcd = P.dram("wc", [128, 4 * 64], F32, "ExternalInput")
        identd = P.dram("ident", [128, 128], F32, "ExternalInput")
        yb = P.dram("yb", [64, LL + LC], F32, "ExternalOutput")
        upl_h = nc.dram_tensor("upl", [64, LL + 256], BF16, kind="Internal")
        upc_h = nc.dram_tensor("upc", [64, LC + 256], BF16, kind="Internal")
        upl = Buf(upl_h.ap(), "upl")
        upc = Buf(upc_h.ap(), "upc")

        sw = P.sbuf([64, 9], F32, "sw_s")
        w1 = P.sbuf([33, 64], F32, "w1_s")
        w23 = P.sbuf([64, 128], F32, "w23_s")
        w4 = P.sbuf([64, 128], F32, "w4_s")
        vec = P.sbuf([64, 8], F32, "vec_s")
        sc = P.sbuf([64, 4], F32, "sc_s")
        ident = P.sbuf([128, 128], F32, "ident_s")
        zero = P.sbuf([64, 136], BF16, "zero_s")
        for d, s in [(sw, swd), (w1, w1d), (w23, w23d), (w4, w4d), (vec, vecd), (ident, identd)]:
            P.dma(d[:], s[:])
        P.memset(zero[:], 0.0)
        P.ts(sc[:, 0:1], vec[:, 3:4], 1.0 / 3.0, ALU.mult)
        for k in range(3):
            P.tt(sc[:, 1 + k:2 + k], vec[:, k:k + 1], sc[:, 0:1], ALU.mult)

        PIECE = 2048
        pg = P.sbuf([64, 3, PIECE + 2], F32, "pg")
        cg = P.sbuf([64, 3, PIECE], F32, "cg")
        ub = P.sbuf([64, PIECE], BF16, "ub")
        Hm = P.sbuf([128, 64, 128], BF16, "Hm")
        ush = [P.sbuf([128, 65 * 128], BF16, "ush%d" % i) for i in range(2)]
        ytok = P.sbuf([128, 64, 64], F32, "ytok")
        yconv = P.sbuf([64, LL], F32, "yconv")
        zt = [P.sbuf([33, 512], F32, "zt%d" % i) for i in range(2)]
        hs = [P.sbuf([64, 512], F32, "hs%d" % i) for i in range(3)]
        s2 = P.sbuf([64, 512], F32, "s2")
        wp = [P.sbuf([128, 4, 64], F32, "wp%d" % i) for i in range(2)]
        fb = [P.psum([128, 512], F32, "fb%d" % i) for i in range(3)]
        hb = [P.psum([128, 512], F32, "hb%d" % i) for i in range(2)]
        yk = [P.psum([128, 512], F32, "yk%d" % i) for i in range(2)]
        tb = P.psum([128, 512], F32, "tb")

        def conv_piece(src, L, q, piece):
            lo = q * piece - 1
            hi = (q + 1) * piece + 1
            clo, chi = max(lo, 0), min(hi, L)
            if clo != lo or chi != hi:
                P.memset(pg[:], 0.0)
            for g in range(3):
                P.dma(pg[:, g, clo - lo:chi - lo], src[g, :, clo:chi])
            for g in range(3):
                P.ts(cg[:, g, 0:piece], pg[:, g, 1:1 + piece], sw[:, g * 3 + 1:g * 3 + 2], ALU.mult)
                P.stt(cg[:, g, 0:piece], pg[:, g, 0:piece], sw[:, g * 3:g * 3 + 1], cg[:, g, 0:piece], ALU.mult, ALU.add)
                P.stt(cg[:, g, 0:piece], pg[:, g, 2:2 + piece], sw[:, g * 3 + 2:g * 3 + 3], cg[:, g, 0:piece], ALU.mult, ALU.add)
            P.tt(cg[:, 1, 0:piece], cg[:, 1, 0:piece], cg[:, 2, 0:piece], ALU.mult)

        def sin3(out, ps, k):
            P.act(out[:], ps[0:64, :], AF.Sin, scale=sc[:, 0:1], bias=sc[:, 1 + k:2 + k])
            P.tt(s2[:], out[:], out[:], ALU.mult)
            P.ts(s2[:], s2[:], -4.0, ALU.mult, 3.0, ALU.add)
            P.tt(out[:], s2[:], out[:], ALU.mult)

        def run_seq(src, L, up, up_h, zd, wd, out0):
            nb = L // 128
            piece = min(L, PIECE)
            npieces = L // piece
            W = L + 256
            P.dma(up[:, 0:127], zero[:, 0:127])
            P.dma(up[:, 127 + L:W], zero[:, 0:129])
            for q in range(npieces):
                conv_piece(src, L, q, piece)
                P.copy(ub[:, 0:piece], cg[:, 1, 0:piece])
                P.dma(up[:, 127 + q * piece:127 + (q + 1) * piece], ub[:, 0:piece])
            ntile = (2 * nb * 128) // 512
            for ti in range(ntile):
                z_ = zt[ti % 2]
                P.dma(z_[:], zd[:, ti * 512:(ti + 1) * 512])
                P.mm(fb[0][0:64, :], w1[:, :], z_[:, :])
                sin3(hs[0], fb[0], 0)
                P.mm(fb[1][0:64, :], w23[:, 0:64], hs[0][:, :])
                sin3(hs[1], fb[1], 1)
                P.mm(fb[2][0:64, :], w23[:, 64:128], hs[1][:, :])
                sin3(hs[2], fb[2], 2)
                hbk = hb[ti % 2]
                for k in range(4):
                    cbi = ti * 4 + k
                    wcol = 0 if cbi >= nb else 64
                    P.mm(hbk[:, k * 64:(k + 1) * 64], hs[2][:, k * 128:(k + 1) * 128], w4[:, wcol:wcol + 64])
                wp_ = wp[ti % 2]
                P.dma(wp_[:], wd.re("p (b c) -> p b c", c=64)[:, ti * 4:(ti + 1) * 4, :])
                hv = View(hbk, hbk.t[:, 0:256].rearrange("p (k c) -> p k c", k=4))
                ov = View(Hm, Hm.t[:, :, ti * 4:(ti + 1) * 4].rearrange("p c k -> p k c"))
                P.tt(ov, hv, wp_[:], ALU.mult)
            ncol = (nb + 1) * 128
            for c in range(64):
                us = ush[c % 2]
                src_ap = bass.AP(up_h, c * W, [[1, 128], [1, ncol]])
                P.dma(us[:, 0:ncol], View(up, src_ap))
                ykb = yk[(c // 8) % 2]
                c8 = c % 8
                for m in range(nb + 1):
                    P.mm(ykb[:, c8 * nb:(c8 + 1) * nb], us[:, m * 128:(m + 1) * 128], Hm[:, c, nb - m:2 * nb - m],
                         start=(m == 0), stop=(m == nb))
                if c8 == 7:
                    c0 = c - 7
                    iv = View(ykb, ykb.t[:, 0:8 * nb].rearrange("p (c a) -> p c a", c=8))
                    ov = View(ytok, ytok.t[:, 0:nb, c0:c0 + 8].rearrange("p a c -> p c a"))
                    P.act(ov, iv, AF.Copy)
            for a in range(nb):
                k = a % 4
                P.transpose(tb[0:64, k * 128:(k + 1) * 128], ytok[:, a, :], ident[:])
                if k == 3 or a == nb - 1:
                    a0 = a - k
                    P.act(yconv[:, a0 * 128:(a + 1) * 128], tb[0:64, 0:(k + 1) * 128], AF.Copy)
            for q in range(npieces):
                conv_piece(src, L, q, piece)
                P.stt(cg[:, 1, 0:piece], cg[:, 1, 0:piece], vec[:, 4:5], yconv[:, q * piece:(q + 1) * piece], ALU.mult, ALU.add)
                P.tt(cg[:, 0, 0:piece], cg[:, 0, 0:piece], cg[:, 1, 0:piece], ALU.mult)
                P.dma(yb[:, out0 + q * piece:out0 + (q + 1) * piece], cg[:, 0, 0:piece])

        run_seq(pl, LL, upl, upl_h, zld, wld, 0)
        run_seq(pc, LC, upc, upc_h, zcd, wcd, LL)
        P.finish([yb])
    return nc


NS = 8448
NCH = 132


def dn_consts():
    U = np.triu(np.ones((64, 64), np.float32))
    Ls = np.tril(np.ones((64, 64), np.float32), -1)
    cm = np.zeros((128, 512), np.float32)
    cm[:64, 0:64] = U
    cm[:64, 64:128] = Ls
    cm[:, 128:256] = np.eye(128, dtype=np.float32)
    cm[:, 256:384] = 1.0
    return cm


def build_k4():
    nc, st, P = new_prog()
    with st:
        pqkv = P.dram("pqkv", [3, 128, NS], F32, "ExternalInput")
        tapsd = P.dram("taps", [128, 9], F32, "ExternalInput")
        grawd = P.dram("graw", [64, 2 * NCH], F32, "ExternalInput")
        scd = P.dram("scal", [128, 2], F32, "ExternalInput")
        cmd = P.dram("cm", [128, 512], F32, "ExternalInput")
        oT = P.dram("oT", [128, NS], F32, "ExternalOutput")

        cm = P.sbuf([128, 512], F32, "cm_s")
        taps = P.sbuf([128, 9], F32, "taps_s")
        graw = P.sbuf([64, 2 * NCH], F32, "graw_s")
        scl = P.sbuf([128, 2], F32, "scl_s")
        for d, s in [(cm, cmd), (taps, tapsd), (graw, grawd), (scl, scd)]:
            P.dma(d[:], s[:])
        U = cm[0:64, 0:64]
        Ls = cm[0:64, 64:128]
        I64 = cm[0:64, 128:192]
        I128 = cm[:, 128:256]
        ones = cm[:, 256:384]
        ones64 = cm[0:64, 256:384]
        eps = P.sbuf([128, 1], F32, "eps_s")
        P.memset(eps[:], 1e-6)

        qkv = [P.sbuf([128, NS], F32, "qkv%d" % g) for g in range(3)]
        PIECE = 2048
        pg = P.sbuf([128, PIECE + 2], F32, "pg")
        cgt = P.sbuf([128, PIECE], F32, "cgt")
        sg = P.sbuf([128, PIECE], F32, "sg")
        bank = [P.psum([128, 512], F32, "bk%d" % i) for i in range(8)]
        bctr = [0]

        def nb():
            b = bank[bctr[0] % 8]
            bctr[0] += 1
            return b

        segs = [(0, 256)] + [(256 + i * PIECE, PIECE) for i in range(4)]
        for g in range(3):
            for (s0, n) in segs:
                first = (s0 == 0 or s0 == 256)
                last = (s0 + n == 256 or s0 + n == NS)
                lo = s0 - (0 if first else 1)
                hi = s0 + n + (0 if last else 1)
                if first or last:
                    P.memset(pg[:], 0.0)
                P.dma(pg[:, 1 - (s0 - lo):1 + n + (hi - s0 - n)], pqkv[g, :, lo:hi])
                P.ts(cgt[:, 0:n], pg[:, 1:1 + n], taps[:, g * 3 + 1:g * 3 + 2], ALU.mult)
                P.stt(cgt[:, 0:n], pg[:, 0:n], taps[:, g * 3:g * 3 + 1], cgt[:, 0:n], ALU.mult, ALU.add)
                P.stt(cgt[:, 0:n], pg[:, 2:2 + n], taps[:, g * 3 + 2:g * 3 + 3], cgt[:, 0:n], ALU.mult, ALU.add)
                P.act(qkv[g][:, s0:s0 + n], cgt[:, 0:n], AF.Silu)
                if g < 2:
                    P.act(sg[:, 0:n], qkv[g][:, s0:s0 + n], AF.Square)
                    for c0 in range(0, n, 512):
                        w = min(512, n - c0)
                        b = nb()
                        P.mm(b[:, 0:w], ones, sg[:, c0:c0 + w])
                        P.act(sg[:, c0:c0 + w], b[:, 0:w], AF.Sqrt, bias=eps[:])
                    P.recip(sg[:, 0:n], sg[:, 0:n])
                    if g == 0:
                        P.stt(qkv[g][:, s0:s0 + n], qkv[g][:, s0:s0 + n], 128.0 ** -0.5, sg[:, 0:n], ALU.mult, ALU.mult)
                    else:
                        P.tt(qkv[g][:, s0:s0 + n], qkv[g][:, s0:s0 + n], sg[:, 0:n], ALU.mult)
        qT, kT, vT = qkv

        beta = P.sbuf([64, NCH], F32, "beta")
        nbeta = P.sbuf([64, NCH], F32, "nbeta")
        gg = P.sbuf([64, NCH], F32, "gg")
        ea = P.sbuf([128, 1], F32, "ea")
        P.act(beta[:], graw[:, 0:NCH], AF.Sigmoid)
        P.ts(nbeta[:], beta[:], -1.0, ALU.mult)
        P.act(gg[:], graw[:, NCH:2 * NCH], AF.Exp, bias=scl[0:64, 1:2])
        P.act(gg[:], gg[:], AF.Ln, bias=cm[0:64, 256:257])
        P.act(ea[:], scl[:, 0:1], AF.Exp)
        P.ts(gg[:], gg[:], ea[0:64, 0:1], ALU.mult, -1.0, ALU.mult)
        gc = P.sbuf([64, NCH], F32, "gc")
        egc = P.sbuf([64, NCH], F32, "egc")
        bke = P.sbuf([64, NCH], F32, "bke")
        kde = P.sbuf([64, NCH], F32, "kde")
        lastB = P.sbuf([128, NCH], F32, "lastB")
        b = nb()
        P.mm(b[0:64, 0:NCH], U, gg[:, :])
        P.act(gc[:], b[0:64, 0:NCH], AF.Copy)
        P.act(egc[:], gc[:], AF.Exp)
        P.tt(bke[:], beta[:], egc[:], ALU.mult)
        b = nb()
        P.mm(b[:, 0:NCH], ones64, gg[:, :])
        P.act(lastB[:], b[:, 0:NCH], AF.Exp)
        P.tt(kde[:], b[0:64, 0:NCH], gc[:], ALU.subtract)
        P.act(kde[:], kde[:], AF.Exp)

        S = [P.sbuf([128, 128], F32, "S%d" % i) for i in range(2)]
        P.memset(S[0][:], 0.0)
        oacc = P.sbuf([128, NS], F32, "oacc")

        def T(shape, name, n=2):
            return [P.sbuf(shape, F32, "%s%d" % (name, i)) for i in range(n)]
        gB = T([64, 128], "gB"); tt_ = T([128, 64], "tt"); a1 = T([64, 64], "a1"); gs = T([64, 64], "gs")
        gT_ = T([64, 64], "gT"); egr = T([128, 64], "egr"); X = T([64, 64], "X", 4); Z = T([64, 64], "Z", 4)
        W = T([64, 64], "W", 4); vb = T([64, 128], "vb"); kbd = T([64, 128], "kbd"); kd = T([64, 128], "kd")
        u = T([64, 128], "u"); wT = T([128, 64], "wT"); qd = T([128, 64], "qd"); qk = T([64, 64], "qk")
        vn = T([64, 128], "vn")
        for c in range(NCH):
            cols = slice(64 * c, 64 * c + 64)
            i2 = c % 2
            P.act(gB[i2][:], ones64, AF.Identity, scale=gg[:, c:c + 1])
            b1 = nb()
            P.mm(b1[:, 0:64], gB[i2][:], U)
            P.act(tt_[i2][:], b1[:, 0:64], AF.Copy)
            P.act(egr[i2][:], tt_[i2][:], AF.Exp)
            P.ts(a1[i2][:], tt_[i2][0:64, :], gc[:, c:c + 1], ALU.subtract, 0.0, ALU.min)
            P.act(gT_[i2][:], a1[i2][:], AF.Exp)
            P.tt(gT_[i2][:], gT_[i2][:], U, ALU.mult)
            P.ts(a1[i2][:], tt_[i2][0:64, :], gc[:, c:c + 1], ALU.subtract, -1.0, ALU.mult)
            P.ts(a1[i2][:], a1[i2][:], 0.0, ALU.min)
            P.act(gs[i2][:], a1[i2][:], AF.Exp)
            P.tt(gs[i2][:], gs[i2][:], Ls, ALU.mult)
            b2 = nb()
            P.mm(b2[0:64, 0:64], kT[:, cols], kT[:, cols])
            x0 = X[0]
            P.stt(x0[:], b2[0:64, 0:64], nbeta[:, c:c + 1], gs[i2][:], ALU.mult, ALU.mult)
            b3 = nb()
            P.transpose(b3[0:64, 0:64], x0[:], I64)
            z0 = Z[0]
            P.act(z0[:], b3[0:64, 0:64], AF.Copy)
            w0 = W[0]
            P.tt(w0[:], z0[:], I64, ALU.add)
            xk, zk, wk = x0, z0, w0
            for lev in range(5):
                xn, zn, wn = X[(lev + 1) % 4], Z[(lev + 1) % 4], W[(lev + 1) % 4]
                bx = nb()
                P.mm(bx[0:64, 0:64], zk[:], xk[:])
                P.act(xn[:], bx[0:64, 0:64], AF.Copy)
                if lev < 4:
                    bz = nb()
                    P.mm(bz[0:64, 0:64], xk[:], zk[:])
                    P.copy(zn[:], bz[0:64, 0:64])
                bw = nb()
                P.mm(bw[0:64, 0:64], xn[:], wk[:])
                P.tt(wn[:], bw[0:64, 0:64], wk[:], ALU.add)
                xk, zk, wk = xn, zn, wn
            Wf = wk
            bv = nb()
            P.transpose(bv[0:64, 0:128], vT[:, cols], I128)
            P.ts(vb[i2][:], bv[0:64, 0:128], beta[:, c:c + 1], ALU.mult)
            bk_ = nb()
            P.transpose(bk_[0:64, 0:128], kT[:, cols], I128)
            P.ts(kbd[i2][:], bk_[0:64, 0:128], bke[:, c:c + 1], ALU.mult)
            P.ts(kd[i2][:], bk_[0:64, 0:128], kde[:, c:c + 1], ALU.mult)
            bu = nb()
            P.mm(bu[0:64, 0:128], Wf[:], vb[i2][:])
            P.act(u[i2][:], bu[0:64, 0:128], AF.Copy)
            bwt = nb()
            P.mm(bwt[:, 0:64], kbd[i2][:], Wf[:])
            P.act(wT[i2][:], bwt[:, 0:64], AF.Copy)
            P.tt(qd[i2][:], qT[:, cols], egr[i2][:], ALU.mult)
            bq = nb()
            P.mm(bq[0:64, 0:64], kT[:, cols], qT[:, cols])
            P.tt(qk[i2][:], bq[0:64, 0:64], gT_[i2][:], ALU.mult)
            Sc, Sn = S[c % 2], S[(c + 1) % 2]
            bs = nb()
            P.mm(bs[0:64, 0:128], wT[i2][:], Sc[:])
            P.tt(vn[i2][:], u[i2][:], bs[0:64, 0:128], ALU.subtract)
            bo = nb()
            P.mm(bo[:, 0:64], Sc[:], qd[i2][:], start=True, stop=False)
            P.mm(bo[:, 0:64], vn[i2][:], qk[i2][:], start=False, stop=True)
            P.act(oacc[:, cols], bo[:, 0:64], AF.Copy)
            bn = nb()
            P.mm(bn[:, 0:128], kd[i2][:], vn[i2][:])
            P.stt(Sn[:], Sc[:], lastB[:, c:c + 1], bn[:, 0:128], ALU.mult, ALU.add)
        for i in range(4):
            P.dma(oT[:, i * 2112:(i + 1) * 2112], oacc[:, i * 2112:(i + 1) * 2112])
        P.finish([oT])
    return nc


D = 2048
NT = 1056
NL = 1024
DFF = 5632


def sumsq_rstd(P, src, nchunks, n, ones, bank, sqtmp, rstd, dim):
    for kc in range(nchunks):
        sq = sqtmp[kc % 2]
        P.act(sq[:, 0:n], src[:, kc, 0:n], AF.Square)
        P.mm(bank[:, 0:n], ones, sq[:, 0:n], start=(kc == 0), stop=(kc == nchunks - 1))
    P.act(rstd[:, 0:n], bank[:, 0:n], AF.Sqrt, scale=1.0 / dim, bias=EPSB(P))
    P.recip(rstd[:, 0:n], rstd[:, 0:n])


def build_k5a():
    nc, st, P = new_prog()
    with st:
        yT = P.dram("yT", [D, NT], F32, "ExternalInput")
        xT = P.dram("xT", [D, NT], F32, "ExternalInput")
        w_out = P.dram("w_out", [D, D], F32, "ExternalInput")
        modT = P.dram("modT", [128, 192], F32, "ExternalInput")
        gains = P.dram("gains", [128, 32], F32, "ExternalInput")
        onesd = P.dram("ones", [128, 128], F32, "ExternalInput")
        ofT = P.dram("ofT", [512, NT], F32, "ExternalInput")
        obT = P.dram("obT", [512, NT], F32, "ExternalInput")
        gtT = P.dram("gtT", [512, NT], F32, "ExternalInput")
        dngd = P.dram("dng", [128, 1], F32, "ExternalInput")
        xmT = P.dram("xmT", [D, NT], F32, "ExternalOutput")
        hT = P.dram("hT", [D, NT], F32, "ExternalOutput")
        dng = P.sbuf([128, 1], F32, "dng_s")
        P.dma(dng[:], dngd[:])
        dn_o = P.sbuf([128, 352], F32, "dn_o")
        dn_b = P.sbuf([128, 352], F32, "dn_b")
        dn_g = P.sbuf([128, 352], F32, "dn_g")

        wob = P.sbuf([128, 16, D], BF16, "wob")
        wst = [P.sbuf([128, 16, 256], F32, "wst%d" % i) for i in range(2)]
        mods = P.sbuf([128, 96, 2], F32, "mods")
        gs = P.sbuf([128, 32], F32, "gs")
        ones = P.sbuf([128, 128], F32, "ones_s")
        G1 = P.sbuf([128, 16, 2], F32, "G1")
        A2 = P.sbuf([128, 16, 2], F32, "A2")
        yb = P.sbuf([128, 16, 352], BF16, "yb")
        ystage = [P.sbuf([128, 352], F32, "ystage%d" % i) for i in range(2)]
        z = P.sbuf([128, 16, 352], F32, "z")
        xg = P.sbuf([128, 16, 352], F32, "xg")
        sqt = [P.sbuf([128, 352], F32, "sqt%d" % i) for i in range(2)]
        tmp = [P.sbuf([128, 352], F32, "tmp%d" % i) for i in range(2)]
        hout = [P.sbuf([128, 352], F32, "hout%d" % i) for i in range(2)]
        rstd = P.sbuf([128, 352], F32, "rstd")
        banks = [P.psum([128, 512], F32, "bank%d" % i) for i in range(6)]
        nbank = P.psum([128, 512], F32, "nbank")

        P.dma(mods[:], modT.re("k (c r) -> k c r", r=2)[:, :, :])
        P.dma(gs[:], gains[:])
        P.dma(ones[:], onesd[:])
        w_r = w_out.re("(kc k) c -> k kc c", k=128)
        for s in range(8):
            ws_ = wst[s % 2]
            for kh in range(2):
                P.dma(ws_[:, kh * 8:(kh + 1) * 8, :], w_r[:, kh * 8:(kh + 1) * 8, s * 256:(s + 1) * 256])
            P.copy(wob[:, :, s * 256:(s + 1) * 256], ws_[:], eng=('dve' if s % 2 == 0 else 'pool'))
        for r in range(2):
            P.tt(G1[:, :, r], mods[:, 32:48, r], gs[:, 0:16], ALU.mult)
            P.ts(A2[:, :, r], mods[:, 64:80, r], 1.0, ALU.add)
            P.tt(A2[:, :, r], A2[:, :, r], gs[:, 16:32], ALU.mult)
        yT_r = yT.re("(kc k) n -> k kc n", k=128)
        xT_r = xT.re("(kc k) n -> k kc n", k=128)
        xm_r = xmT.re("(kc k) n -> k kc n", k=128)
        hT_r = hT.re("(kc k) n -> k kc n", k=128)
        bi = 0
        for tg in range(3):
            c0 = tg * 352
            nl = 352 if tg < 2 else 320
            rngs = [(0, nl, 0)] + ([(nl, 352, 1)] if nl < 352 else [])
            for kc in range(16):
                ys_ = ystage[kc % 2]
                if 8 <= kc < 12:
                    hh = kc - 8
                    P.dma(dn_o[:], ofT[hh * 128:(hh + 1) * 128, c0:c0 + 352])
                    P.dma(dn_b[:], obT[hh * 128:(hh + 1) * 128, c0:c0 + 352])
                    P.dma(dn_g[:], gtT[hh * 128:(hh + 1) * 128, c0:c0 + 352])
                    P.tt(dn_o[:], dn_o[:], dn_b[:], ALU.add)
                    P.act(dn_b[:], dn_o[:], AF.Square)
                    P.mm(nbank[:, 0:352], ones[:], dn_b[:])
                    P.act(dn_b[:], nbank[:, 0:352], AF.Sqrt, scale=1.0 / 128, bias=EPSB(P))
                    P.recip(dn_b[:], dn_b[:])
                    P.stt(dn_o[:], dn_o[:], dng[:, 0:1], dn_b[:], ALU.mult, ALU.mult)
                    P.act(dn_g[:], dn_g[:], AF.Silu)
                    P.tt(ys_[:], dn_o[:], dn_g[:], ALU.mult)
                else:
                    P.dma(ys_[:], yT_r[:, kc, c0:c0 + 352])
                P.copy(yb[:, kc, :], ys_[:], eng=('dve' if kc % 2 == 0 else 'pool'))
            P.dma(xg[:, 0:8, :], xT_r[:, 0:8, c0:c0 + 352])
            P.dma(xg[:, 8:16, :], xT_r[:, 8:16, c0:c0 + 352])
            for m in range(16):
                bk = banks[bi % 6]
                bi += 1
                for kc in range(16):
                    P.mm(bk[:, 0:352], wob[:, kc, m * 128:(m + 1) * 128], yb[:, kc, :], start=(kc == 0), stop=(kc == 15))
                P.act(z[:, m, :], bk[:, 0:352], AF.Copy)
            sumsq_rstd(P, z, 16, 352, ones[:], nbank, sqt, rstd, D)
            for kc in range(16):
                t = tmp[kc % 2]
                P.tt(t[:], z[:, kc, :], rstd[:], ALU.mult)
                for (a, b, r) in rngs:
                    P.stt(xg[:, kc, a:b], t[:, a:b], G1[:, kc, r:r + 1], xg[:, kc, a:b], ALU.mult, ALU.add)
            sumsq_rstd(P, xg, 16, 352, ones[:], nbank, sqt, rstd, D)
            for kc in range(16):
                t = tmp[kc % 2]
                ho = hout[kc % 2]
                P.tt(t[:], xg[:, kc, :], rstd[:], ALU.mult)
                for (a, b, r) in rngs:
                    P.ts(ho[:, a:b], t[:, a:b], A2[:, kc, r:r + 1], ALU.mult, mods[:, 48 + kc, r:r + 1], ALU.add)
                P.dma(hT_r[:, kc, c0:c0 + 352], ho[:])
            P.dma(xm_r[:, 0:8, c0:c0 + 352], xg[:, 0:8, :])
            P.dma(xm_r[:, 8:16, c0:c0 + 352], xg[:, 8:16, :])
        P.finish([xmT, hT])
    return nc


NP5 = 1060


def build_k5b():
    nc, st, P = new_prog()
    with st:
        hp = P.dram("hp", [D, NP5], F32, "ExternalInput")
        xmT = P.dram("xmT", [D, NT], F32, "ExternalInput")
        w_up = P.dram("w_up", [D, 2 * DFF], F32, "ExternalInput")
        wcv = P.dram("wcv", [128, 88 * 3], F32, "ExternalInput")
        w_dn = P.dram("w_dn", [DFF, D], F32, "ExternalInput")
        modT = P.dram("modT", [128, 192], F32, "ExternalInput")
        gains = P.dram("gains", [128, 16], F32, "ExternalInput")
        onesd = P.dram("ones", [128, 128], F32, "ExternalInput")
        xoT = P.dram("xoT", [D, NT], F32, "ExternalOutput")

        stf = [P.sbuf([128, 5632], F32, "stf%d" % i) for i in range(2)]
        stb = [P.sbuf([128, 5632], BF16, "stb%d" % i) for i in range(2)]
        mods = P.sbuf([128, 96, 2], F32, "mods")
        gs = P.sbuf([128, 16], F32, "gs")
        wc = P.sbuf([128, 88, 3], F32, "wc")
        ones = P.sbuf([128, 128], F32, "ones_s")
        G2 = P.sbuf([128, 16, 2], F32, "G2")
        hg = P.sbuf([128, 16, 376], BF16, "hg")
        hst = [P.sbuf([128, 376], F32, "hst%d" % i) for i in range(2)]
        gT = P.sbuf([128, 44, 372], BF16, "gT")
        dn = P.sbuf([128, 16, 372], F32, "dn")
        xg = P.sbuf([128, 16, 372], F32, "xg")
        ca = [P.sbuf([128, 372], F32, "ca%d" % i) for i in range(2)]
        cb = [P.sbuf([128, 372], F32, "cb%d" % i) for i in range(2)]
        sqt = [P.sbuf([128, 372], F32, "sqt%d" % i) for i in range(2)]
        tmp = [P.sbuf([128, 372], F32, "tmp%d" % i) for i in range(2)]
        rstd = P.sbuf([128, 372], F32, "rstd")
        banks = [P.psum([128, 512], F32, "bank%d" % i) for i in range(6)]
        nbank = P.psum([128, 512], F32, "nbank")

        P.dma(mods[:], modT.re("k (c r) -> k c r", r=2)[:, :, :])
        P.dma(gs[:], gains[:])
        P.dma(ones[:], onesd[:])
        P.dma(wc[:], wcv.re("k (c t) -> k c t", t=3)[:, :, :])
        for r in range(2):
            P.tt(G2[:, :, r], mods[:, 80:96, r], gs[:], ALU.mult)
        hp_r = hp.re("(kc k) n -> k kc n", k=128)
        xm_r = xmT.re("(kc k) n -> k kc n", k=128)
        xo_r = xoT.re("(kc k) n -> k kc n", k=128)
        wu_r = w_up.re("(kc k) c -> k kc c", k=128)
        wd_r = w_dn.re("(f k) c -> k f c", k=128)
        groups = [(0, 344, [(0, 342)], 0, 342, 342),
                  (342, 344, [(0, 342)], 342, 342, 342),
                  (684, 376, [(0, 340), (342, 32)], 684, 372, 340)]
        si = 0
        bi = 0
        for (u0, un, segs, xc0, gn, nl) in groups:
            for kc in range(16):
                hs_ = hst[kc % 2]
                P.dma(hs_[:, 0:un], hp_r[:, kc, u0:u0 + un])
                P.copy(hg[:, kc, 0:un], hs_[:, 0:un], eng=('dve' if kc % 2 == 0 else 'pool'))
            P.dma(xg[:, 0:8, 0:gn], xm_r[:, 0:8, xc0:xc0 + gn])
            P.dma(xg[:, 8:16, 0:gn], xm_r[:, 8:16, xc0:xc0 + gn])
            for f in range(44):
                sf, sb_ = stf[si % 2], stb[si % 2]
                si += 1
                sfv = sf.v(sf.t[:, 0:4096].rearrange("p (a b c) -> p a b c", a=16, b=2))
                sbv = sb_.v(sb_.t[:, 0:4096].rearrange("p (a b c) -> p a b c", a=16, b=2))
                P.dma(View(sf, sfv.ap[:, :, 0, :]), wu_r[:, :, f * 128:(f + 1) * 128])
                P.dma(View(sf, sfv.ap[:, :, 1, :]), wu_r[:, :, DFF + f * 128:DFF + (f + 1) * 128])
                P.copy(sb_[:, 0:4096], sf[:, 0:4096], eng='pool')
                bka = banks[bi % 6]
                bkb = banks[(bi + 1) % 6]
                bi += 2
                for kc in range(16):
                    P.mm(bka[:, 0:un], View(sb_, sbv.ap[:, kc, 0, :]), hg[:, kc, 0:un], start=(kc == 0), stop=(kc == 15))
                for kc in range(16):
                    P.mm(bkb[:, 0:un], View(sb_, sbv.ap[:, kc, 1, :]), hg[:, kc, 0:un], start=(kc == 0), stop=(kc == 15))
                ca_, cb_ = ca[f % 2], cb[f % 2]
                goff = 0
                for (lo, n) in segs:
                    for (cc_, bk, ch) in [(ca_, bka, f), (cb_, bkb, 44 + f)]:
                        P.ts(cc_[:, goff:goff + n], bk[:, lo + 1:lo + 1 + n], wc[:, ch, 1:2], ALU.mult)
                        P.stt(cc_[:, goff:goff + n], bk[:, lo:lo + n], wc[:, ch, 0:1], cc_[:, goff:goff + n], ALU.mult, ALU.add)
                        P.stt(cc_[:, goff:goff + n], bk[:, lo + 2:lo + 2 + n], wc[:, ch, 2:3], cc_[:, goff:goff + n], ALU.mult, ALU.add)
                    goff += n
                P.act(ca_[:, 0:gn], ca_[:, 0:gn], AF.Silu)
                P.tt(gT[:, f, 0:gn], ca_[:, 0:gn], cb_[:, 0:gn], ALU.mult)
            for m in range(16):
                sf, sb_ = stf[si % 2], stb[si % 2]
                si += 1
                sfv = sf.v(sf.t[:, :].rearrange("p (f c) -> p f c", f=44))
                sbv = sb_.v(sb_.t[:, :].rearrange("p (f c) -> p f c", f=44))
                P.dma(View(sf, sfv.ap[:, 0:22, :]), wd_r[:, 0:22, m * 128:(m + 1) * 128])
                P.dma(View(sf, sfv.ap[:, 22:44, :]), wd_r[:, 22:44, m * 128:(m + 1) * 128])
                P.copy(sb_[:], sf[:], eng='pool')
                bk = banks[bi % 6]
                bi += 1
                for f in range(44):
                    P.mm(bk[:, 0:gn], View(sb_, sbv.ap[:, f, :]), gT[:, f, 0:gn], start=(f == 0), stop=(f == 43))
                P.act(dn[:, m, 0:gn], bk[:, 0:gn], AF.Copy)
            sumsq_rstd(P, dn, 16, gn, ones[:], nbank, sqt, rstd, D)
            rngs = [(0, nl, 0)] + ([(nl, gn, 1)] if nl < gn else [])
            for kc in range(16):
                t = tmp[kc % 2]
                P.tt(t[:, 0:gn], dn[:, kc, 0:gn], rstd[:, 0:gn], ALU.mult)
                for (a, b, r) in rngs:
                    P.stt(xg[:, kc, a:b], t[:, a:b], G2[:, kc, r:r + 1], xg[:, kc, a:b], ALU.mult, ALU.add)
            P.dma(xo_r[:, 0:8, xc0:xc0 + gn], xg[:, 0:8, 0:gn])
            P.dma(xo_r[:, 8:16, xc0:xc0 + gn], xg[:, 8:16, 0:gn])
        P.finish([xoT])
    return nc


_PROGS = {}


def _prog(name, fn):
    if name not in _PROGS:
        _PROGS[name] = fn()
    return _PROGS[name]


def _run(name, fn, maps):
    nc = fn()
    res = run_bass_kernel_spmd(nc, maps, core_ids=list(range(8)))
    return res.results


def _c(a):
    return np.ascontiguousarray(a, dtype=np.float32)


def kernel(x, c, ctx, c_ctx, w_ada, b_ada, norm_mix_pre, norm_mix_post, norm_ffn_pre,
           norm_ffn_post, w_in, w_out, attn_q_norm, attn_k_norm, hy_short, hy_w1, hy_b1,
           hy_w2, hy_b2, hy_w3, hy_b3, hy_w4, hy_freq, hy_skip, dn_short, dn_a_log,
           dn_dt_bias, dn_norm, df_lambda, df_norm, ffn_up, ffn_conv, ffn_down):
    f = lambda a: np.asarray(a, dtype=np.float32)
    x = f(x)[0]; ctxv = f(ctx)[0]; c = f(c); c_ctx = f(c_ctx)
    w_ada = f(w_ada); b_ada = f(b_ada); w_in = f(w_in); w_out = f(w_out)
    ffn_up = f(ffn_up); ffn_conv = f(ffn_conv); ffn_down = f(ffn_down)
    ones = np.ones((128, 128), np.float32)
    ident = np.eye(128, dtype=np.float32)
    cc = np.stack([c[0], c_ctx], axis=-1).reshape(16, 128, 2).transpose(1, 0, 2).reshape(128, 32)
    maps = []
    for j in range(8):
        l, q = j // 4, j % 4
        maps.append({"cc": _c(cc), "w": _c(w_ada[l][:, q * 3072:(q + 1) * 3072]),
                     "b2": _c(np.broadcast_to(b_ada[l][None, q * 3072:(q + 1) * 3072], (2, 3072)))})
    r = _run('k0', build_k0, maps)
    mod = np.zeros((2, 2, 12288), np.float32)
    for j in range(8):
        l, q = j // 4, j % 4
        mod[l][:, q * 3072:(q + 1) * 3072] = r[j]["mod"]
    cos, sin = rope_tables()
    cm1 = const_mats()
    cm4 = dn_consts()
    hyc = [(hy_consts(LL, j), hy_consts(LC, j)) for j in range(8)]

    def shard_T(lat, cx, j):
        return _c(np.concatenate([lat[j * 1024:(j + 1) * 1024], cx[j * 32:(j + 1) * 32]], axis=0).T)

    for L in range(2):
        modT = _c(mod[L].reshape(2, 96, 128).transpose(2, 1, 0).reshape(128, 192))
        gain = _c(f(norm_mix_pre)[L].reshape(16, 128).T)
        qkg = _c(np.stack([np.tile(f(attn_q_norm)[L], 2), np.tile(f(attn_k_norm)[L], 2)], axis=1))
        maps = []
        for j in range(8):
            cj = np.tile(cos[j * 1024:(j + 1) * 1024].T, (4, 1))
            sj = np.tile(sin[j * 1024:(j + 1) * 1024].T, (4, 1))
            maps.append({"xT": shard_T(x, ctxv, j), "modT": modT, "gain": gain, "w_in": _c(w_in[L]), "qkg": qkg,
                         "cosT": _c(cj), "sinT": _c(sj), "cmat": cm1})
        r = _run('k1', build_k1, maps)
        pT = [r[j]["pT"] for j in range(8)]
        lat = np.concatenate([p[:, :1024] for p in pT], axis=1)
        cxp = np.concatenate([p[:, 1024:] for p in pT], axis=1)
        full = np.concatenate([cxp, lat], axis=1)
        kT = np.concatenate([full[512:640].reshape(2, 64, 8448), full[4880:5392].reshape(8, 64, 8448)], axis=0)

        def vt(rows, nh, dv):
            v = full[rows].T.reshape(66, 128, nh, dv)
            return _c(v.transpose(2, 1, 0, 3).reshape(nh, 128, 66 * dv))
        vv = vt(slice(640, 768), 2, 64)
        vd = vt(slice(5392, 5904), 4, 128)
        lam_init = 0.8 - 0.6 * math.exp(-0.3 * L)
        misc = np.zeros((128, 4), np.float32)
        misc[:, 0] = f(df_norm)[L]; misc[:, 1] = lam_init; misc[:, 2] = 1.0 - lam_init
        lamv = _c(np.broadcast_to(f(df_lambda)[L].reshape(1, 256), (128, 256)))
        maps = []
        for j in range(8):
            q = np.concatenate([pT[j][0:512].reshape(8, 64, 1056), pT[j][4368:4880].reshape(8, 64, 1056)], axis=0)
            maps.append({"qT": _c(q), "kT": _c(kT), "vv": vv, "vd": vd, "lamv": lamv, "misc": misc, "ones": ones})
        r = _run('k2', build_k2, maps)
        yaT = [r[j]["yaT"] for j in range(8)]
        ydT = [r[j]["ydT"] for j in range(8)]
        maps = []
        hs_ = f(hy_short)[L]
        for j in range(8):
            rows = [768 + g * 512 + 64 * j for g in range(3)]
            pl = np.stack([lat[r0:r0 + 64] for r0 in rows])
            pc = np.stack([cxp[r0:r0 + 64] for r0 in rows])
            sw = np.stack([hs_[:, g * 512 + 64 * j:g * 512 + 64 * j + 64].T for g in range(3)], axis=1).reshape(64, 9)
            vec = np.zeros((64, 8), np.float32)
            vec[:, 0] = f(hy_b1)[L]; vec[:, 1] = f(hy_b2)[L]; vec[:, 2] = f(hy_b3)[L]; vec[:, 3] = f(hy_freq)[L]
            vec[:, 4] = f(hy_skip)[L][64 * j:64 * j + 64]
            w4 = f(hy_w4)[L]
            w4s = np.concatenate([w4[:, 64 * j:64 * j + 64], w4[:, 512 + 64 * j:512 + 64 * j + 64]], axis=1)
            (zl, wl), (zc, wc) = hyc[j]
            maps.append({"pl": _c(pl), "pc": _c(pc), "sw": _c(sw), "w1": _c(f(hy_w1)[L]),
                         "w23": _c(np.concatenate([f(hy_w2)[L], f(hy_w3)[L]], axis=1)), "w4s": _c(w4s), "vec": vec,
                         "zl": zl, "zc": zc, "wl": wl, "wc": wc, "ident": ident})
        r = _run('k3', build_k3, maps)
        ybf = np.concatenate([r[j]["yb"] for j in range(8)], axis=0)
        yb_lat, yb_ctx = ybf[:, :8192], ybf[:, 8192:]
        maps = []
        ds_ = f(dn_short)[L]
        for j in range(8):
            h, d = j % 4, j // 4
            rows = [2304 + g * 512 + h * 128 for g in range(3)]
            seq = full
            if d == 1:
                seq = np.concatenate([cxp[:, ::-1], lat[:, ::-1]], axis=1)
            pq = np.stack([seq[r0:r0 + 128] for r0 in rows])
            braw = seq[4352 + d * 4 + h].reshape(132, 64).T
            araw = seq[4352 + 8 + d * 4 + h].reshape(132, 64).T
            taps = np.stack([ds_[:, g * 512 + h * 128:g * 512 + (h + 1) * 128].T for g in range(3)], axis=1)
            if d == 1:
                taps = taps[:, :, ::-1]
            scal = np.zeros((128, 2), np.float32)
            scal[:, 0] = f(dn_a_log)[L, d, h]; scal[:, 1] = f(dn_dt_bias)[L, d, h]
            maps.append({"pqkv": _c(pq), "taps": _c(taps.reshape(128, 9)), "graw": _c(np.concatenate([braw, araw], axis=1)),
                         "scal": scal, "cm": cm4})
        r = _run('k4', build_k4, maps)
        of_full = np.concatenate([r[j]["oT"] for j in range(4)], axis=0)
        ob_s = [r[4 + j]["oT"] for j in range(4)]
        ob_full = np.concatenate([np.concatenate([o[:, :256][:, ::-1], o[:, 256:][:, ::-1]], axis=1) for o in ob_s], axis=0)
        gate_lat, gate_ctx = lat[3840:4352], cxp[3840:4352]
        gains = _c(np.concatenate([f(norm_mix_post)[L].reshape(16, 128).T, f(norm_ffn_pre)[L].reshape(16, 128).T], axis=1))
        dng = _c(f(dn_norm)[L].reshape(128, 1))
        maps = []
        for j in range(8):
            sl, sc_ = slice(j * 1024, (j + 1) * 1024), slice(j * 32, (j + 1) * 32)
            ybj = np.concatenate([yb_lat[:, sl], yb_ctx[:, sc_]], axis=1)
            yT = np.concatenate([yaT[j], ybj, np.zeros((512, 1056), np.float32), ydT[j]], axis=0)
            ofj = np.concatenate([of_full[:, 256:][:, sl], of_full[:, :256][:, sc_]], axis=1)
            obj = np.concatenate([ob_full[:, 256:][:, sl], ob_full[:, :256][:, sc_]], axis=1)
            gtj = np.concatenate([gate_lat[:, sl], gate_ctx[:, sc_]], axis=1)
            maps.append({"yT": _c(yT), "xT": shard_T(x, ctxv, j), "w_out": _c(w_out[L]), "modT": modT, "gains": gains,
                         "ones": ones, "ofT": _c(ofj), "obT": _c(obj), "gtT": _c(gtj), "dng": dng})
        r = _run('k5a', build_k5a, maps)
        xm = [r[j]["xmT"] for j in range(8)]
        hh = [r[j]["hT"] for j in range(8)]
        h_lat = np.concatenate([a[:, :1024] for a in hh], axis=1).T
        h_ctx = np.concatenate([a[:, 1024:] for a in hh], axis=1).T
        z1 = np.zeros((1, 2048), np.float32)

        def hp(j):
            la = np.concatenate([h_lat[j * 1024 - 1:j * 1024] if j > 0 else z1, h_lat[j * 1024:(j + 1) * 1024],
                                 h_lat[(j + 1) * 1024:(j + 1) * 1024 + 1] if j < 7 else z1], axis=0)
            cx_ = np.concatenate([h_ctx[j * 32 - 1:j * 32] if j > 0 else z1, h_ctx[j * 32:(j + 1) * 32],
                                  h_ctx[(j + 1) * 32:(j + 1) * 32 + 1] if j < 7 else z1], axis=0)
            return _c(np.concatenate([la, cx_], axis=0).T)
        wcv = _c(ffn_conv[L].reshape(3, 88, 128).transpose(2, 1, 0).reshape(128, 264))
        g5 = _c(f(norm_ffn_post)[L].reshape(16, 128).T)
        maps = [{"hp": hp(j), "xmT": xm[j], "w_up": _c(ffn_up[L]), "wcv": wcv, "w_dn": _c(ffn_down[L]), "modT": modT,
                 "gains": g5, "ones": ones} for j in range(8)]
        r = _run('k5b', build_k5b, maps)
        xo = [r[j]["xoT"] for j in range(8)]
        x = np.ascontiguousarray(np.concatenate([a[:, :1024] for a in xo], axis=1).T)
        ctxv = np.ascontiguousarray(np.concatenate([a[:, 1024:] for a in xo], axis=1).T)
    return x[None].astype(np.float32)
```

```python
import math
import numpy as np
import contextlib
import concourse.bass as bass
import concourse.mybir as mybir
from concourse.bass_utils import run_bass_kernel_spmd

F32 = mybir.dt.float32
BF16 = mybir.dt.bfloat16
AF = mybir.ActivationFunctionType
ALU = mybir.AluOpType
AX = mybir.AxisListType


class View:
    def __init__(self, buf, ap):
        self.buf = buf
        self.ap = ap


class Buf:
    def __init__(self, t, name):
        self.t = t
        self.name = name
        self.wr = {}
        self.rd = {}

    def __getitem__(self, idx):
        return View(self, self.t[idx])

    def v(self, ap):
        return View(self, ap)

    def re(self, pat, **kw):
        return ReView(self, self.t.rearrange(pat, **kw))


class ReView:
    def __init__(self, buf, ap):
        self.buf = buf
        self.ap = ap

    def __getitem__(self, idx):
        return View(self.buf, self.ap[idx])


def _aps(x):
    return x.ap if isinstance(x, View) else x


class Prog:
    NDMASEM = 8

    def __init__(self, nc, stack):
        self.nc = nc
        self.stack = stack
        self.streams = ['pe', 'dve', 'act', 'pool', 'sp']
        self.sems = {}
        self.cnt = {}
        for k in ['pe', 'dve', 'act', 'pool']:
            self.sems[k] = stack.enter_context(nc.semaphore('s_' + k))
            self.cnt[k] = 0
        self.dq = {}
        for q in ['sp', 'act', 'pool']:
            keys = []
            for i in range(self.NDMASEM):
                k = 'd_%s_%d' % (q, i)
                self.sems[k] = stack.enter_context(nc.semaphore(k))
                self.cnt[k] = 0
                keys.append(k)
            self.dq[q] = [keys, 0]
        self.seen = {e: {} for e in self.streams}
        self.rec = {e: [] for e in self.streams}
        self.nbuf = 0
        self.dmarr = 0

    def sbuf(self, shape, dt, name=None):
        self.nbuf += 1
        name = name or ('sb%d' % self.nbuf)
        t = self.stack.enter_context(self.nc.sbuf_tensor(name, list(shape), dt))
        return Buf(t, name)

    def psum(self, shape, dt, name=None):
        self.nbuf += 1
        name = name or ('ps%d' % self.nbuf)
        t = self.stack.enter_context(self.nc.psum_tensor(name, list(shape), dt))
        return Buf(t, name)

    def dram(self, name, shape, dt, kind):
        t = self.nc.dram_tensor(name, list(shape), dt, kind=kind)
        return Buf(t.ap(), name)

    def _waits(self, e, reads, writes, extra=(), nowaw=False):
        need = {}

        def add(dep):
            if dep is None:
                return
            k, c = dep
            if need.get(k, 0) < c:
                need[k] = c
        raw_self = 0
        for b in reads:
            for k, c in b.wr.items():
                add((k, c))
                if k == e:
                    raw_self = max(raw_self, c)
        for b in writes:
            if not nowaw:
                for k, c in b.wr.items():
                    add((k, c))
            for k, c in b.rd.items():
                add((k, c))
        for d in extra:
            add(d)
        ws = []
        if e in need:
            del need[e]
        if raw_self > 0 and e != 'pe':
            need[e] = raw_self
        for k, c in need.items():
            if self.seen[e].get(k, 0) >= c:
                continue
            ws.append((self.sems[k], c))
            self.seen[e][k] = c
        return ws

    def _mark(self, k, c, reads, writes, nowaw):
        for b in reads:
            b.rd[k] = c
        for b in writes:
            if nowaw:
                b.wr[k] = c
            else:
                b.wr = {k: c}
                b.rd = {}

    def op(self, e, fn, reads=(), writes=(), nowaw=False):
        reads = [r.buf if isinstance(r, View) else r for r in reads if r is not None and not isinstance(r, (int, float))]
        writes = [w.buf if isinstance(w, View) else w for w in writes]
        ws = self._waits(e, reads, writes, nowaw=nowaw)
        self.cnt[e] += 1
        self.rec[e].append((ws, fn, self.sems[e], 1))
        self._mark(e, self.cnt[e], reads, writes, nowaw)

    def dma(self, out, in_, q=None, nowaw=False, **kw):
        if q is None:
            q = 'sp'
        reads = [in_.buf]
        writes = [out.buf]
        keys, idx = self.dq[q]
        k = keys[idx % len(keys)]
        self.dq[q][1] += 1
        prev = (k, self.cnt[k]) if self.cnt[k] > 0 else None
        ws = self._waits(q, reads, writes, extra=(prev,) if prev else (), nowaw=nowaw)
        self.cnt[k] += 16
        oa, ia = out.ap, in_.ap
        self.rec[q].append((ws, (lambda e: e.dma_start(out=oa, in_=ia, **kw)), self.sems[k], 16))
        self._mark(k, self.cnt[k], reads, writes, nowaw)

    def mm(self, out, lhsT, rhs, start=True, stop=True):
        o, l, r = out.ap, lhsT.ap, rhs.ap
        self.op('pe', lambda e: e.matmul(o, lhsT=l, rhs=r, start=start, stop=stop), reads=[lhsT, rhs], writes=[out])

    def transpose(self, out, in_, ident):
        o, i, d = out.ap, in_.ap, ident.ap
        self.op('pe', lambda e: e.transpose(o, i, d), reads=[in_, ident], writes=[out])

    def act(self, out, in_, func, scale=1.0, bias=None, eng='act', accum_out=None, nowaw=False):
        o, i = out.ap, in_.ap
        s = _aps(scale)
        b = _aps(bias)
        kw = {}
        if bias is not None:
            kw['bias'] = b
        if accum_out is not None:
            kw['accum_out'] = accum_out.ap
        wr = [out] + ([accum_out] if accum_out is not None else [])
        self.op('act', lambda e: e.activation(out=o, in_=i, func=func, scale=s, **kw),
                reads=[in_, scale if isinstance(scale, View) else None, bias if isinstance(bias, View) else None], writes=wr, nowaw=nowaw)

    def tt(self, out, in0, in1, op, eng='dve'):
        o, a, b = out.ap, in0.ap, in1.ap
        self.op(eng, lambda e: e.tensor_tensor(out=o, in0=a, in1=b, op=op), reads=[in0, in1], writes=[out])

    def ts(self, out, in0, s1, op0, s2=None, op1=None, eng='dve', accum_out=None, nowaw=False):
        o, a = out.ap, in0.ap
        x1, x2 = _aps(s1), _aps(s2)
        kw = {}
        if op1 is not None:
            kw['op1'] = op1
        if accum_out is not None:
            kw['accum_out'] = accum_out.ap
        wr = [out] + ([accum_out] if accum_out is not None else [])
        self.op(eng, lambda e: e.tensor_scalar(out=o, in0=a, scalar1=x1, scalar2=x2, op0=op0, **kw),
                reads=[in0, s1 if isinstance(s1, View) else None, s2 if isinstance(s2, View) else None], writes=wr, nowaw=nowaw)

    def stt(self, out, in0, scalar, in1, op0, op1):
        o, a, b = out.ap, in0.ap, in1.ap
        s = _aps(scalar)
        self.op('dve', lambda e: e.scalar_tensor_tensor(out=o, in0=a, scalar=s, in1=b, op0=op0, op1=op1),
                reads=[in0, in1, scalar if isinstance(scalar, View) else None], writes=[out])

    def copy(self, out, in_, eng='dve', nowaw=False):
        o, i = out.ap, in_.ap
        self.op(eng, lambda e: e.tensor_copy(out=o, in_=i), reads=[in_], writes=[out], nowaw=nowaw)

    def recip(self, out, in_):
        o, i = out.ap, in_.ap
        self.op('dve', lambda e: e.reciprocal(out=o, in_=i), reads=[in_], writes=[out])

    def memset(self, out, val, eng='dve'):
        o = out.ap
        self.op(eng, lambda e: e.memset(o, val), reads=[], writes=[out])

    def finish(self, bufs, e='sp'):
        ws = self._waits(e, bufs, [])
        self.rec[e].append((ws, None, None, 0))
        rec = self.rec

        def replay(lst):
            def f(eng):
                for ws, fn, sem, inc in lst:
                    for (s_, c_) in ws:
                        eng.wait_ge(s_, c_)
                    if fn is not None:
                        fn(eng).then_inc(sem, inc)
            return f
        with self.nc.Block() as block:
            if rec['sp']:
                block.sync(replay(rec['sp']))
            if rec['pe']:
                block.tensor(replay(rec['pe']))
            if rec['dve']:
                block.vector(replay(rec['dve']))
            if rec['act']:
                block.scalar(replay(rec['act']))
            if rec['pool']:
                block.gpsimd(replay(rec['pool']))


def new_prog():
    nc = bass.Bass("TRN2", target_bir_lowering=False)
    st = contextlib.ExitStack()
    return nc, st, Prog(nc, st)


D = 2048
NT = 1056
NL = 1024
IN_COLS = 5904
EPS = 1e-6


def build_k0():
    nc, st, P = new_prog()
    with st:
        cc = P.dram("cc", [128, 32], F32, "ExternalInput")
        w = P.dram("w", [2048, 3072], F32, "ExternalInput")
        b2 = P.dram("b2", [2, 3072], F32, "ExternalInput")
        mod = P.dram("mod", [2, 3072], F32, "ExternalOutput")
        cs = P.sbuf([128, 32], F32)
        ca = P.sbuf([128, 32], F32)
        bs = P.sbuf([2, 3072], F32)
        ms = P.sbuf([2, 3072], F32)
        wb = [P.sbuf([128, 3072], F32) for _ in range(4)]
        ps = [P.psum([128, 512], F32) for _ in range(6)]
        P.dma(cs[:], cc[:])
        P.dma(bs[:], b2[:])
        P.act(ca[:], cs[:], AF.Silu)
        for kc in range(16):
            wt = wb[kc % 4]
            P.dma(wt[:], w[kc * 128:(kc + 1) * 128, :])
            for g in range(6):
                P.mm(ps[g][0:2, :], ca[:, 2 * kc:2 * kc + 2], wt[:, g * 512:(g + 1) * 512], start=(kc == 0), stop=(kc == 15))
        for g in range(6):
            P.tt(ms[0:2, g * 512:(g + 1) * 512], ps[g][0:2, :], bs[0:2, g * 512:(g + 1) * 512], ALU.add)
        P.dma(mod[:], ms[:])
        P.finish([mod])
    return nc


def k1_groups():
    g = []
    for m in range(34):
        c0 = m * 128
        kind = None
        if m < 4:
            kind = 'gq_q'
        elif m == 4:
            kind = 'gq_k'
        g.append((c0, 128, kind))
    g.append((4352, 16, None))
    for m in range(12):
        c0 = 4368 + m * 128
        g.append((c0, 128, 'rope' if m < 8 else None))
    return g


def build_k1():
    nc, st, P = new_prog()
    with st:
        xT = P.dram("xT", [D, NT], F32, "ExternalInput")
        modT = P.dram("modT", [128, 96 * 2], F32, "ExternalInput")
        gain = P.dram("gain", [128, 16], F32, "ExternalInput")
        w_in = P.dram("w_in", [D, IN_COLS], F32, "ExternalInput")
        qkg = P.dram("qkg", [128, 2], F32, "ExternalInput")
        cosT = P.dram("cosT", [128, NL], F32, "ExternalInput")
        sinT = P.dram("sinT", [128, NL], F32, "ExternalInput")
        cmat = P.dram("cmat", [128, 3 * 128], F32, "ExternalInput")
        pT = P.dram("pT", [IN_COLS, NT], F32, "ExternalOutput")

        xs = P.sbuf([128, 16, NT], F32, "xs")
        hT = P.sbuf([128, 16, NT], BF16, "hT")
        mods = P.sbuf([128, 96, 2], F32, "mods")
        gs = P.sbuf([128, 16], F32, "gs")
        qk = P.sbuf([128, 2], F32, "qk")
        cs_ = P.sbuf([128, NL], F32, "cos")
        sn_ = P.sbuf([128, NL], F32, "sin")
        cm = P.sbuf([128, 384], F32, "cm")
        A = P.sbuf([128, 16, 2], F32, "A")
        rstd = P.sbuf([128, NT], F32, "rstd")
        tmp = [P.sbuf([128, NT], F32, "tmp%d" % i) for i in range(2)]
        banks = [P.psum([128, 512], F32, "bank%d" % i) for i in range(8)]

        xTr = xT.re("(kc k) n -> k kc n", k=128)
        for kc in range(16):
            P.dma(xs[:, kc, :], xTr[:, kc, :], nowaw=True)
        P.dma(mods[:], modT.re("k (c r) -> k c r", r=2)[:, :, :])
        P.dma(gs[:], gain[:])
        P.dma(qk[:], qkg[:])
        P.dma(cs_[:], cosT[:])
        P.dma(sn_[:], sinT[:])
        P.dma(cm[:], cmat[:])
        ones = cm[:, 0:128]
        bo = cm[:, 128:256]
        RT = cm[:, 256:384]

        for r in range(2):
            P.ts(A[:, :, r], mods[:, 16:32, r], 1.0, ALU.add)
            P.tt(A[:, :, r], A[:, :, r], gs[:], ALU.mult)
        for kc in range(16):
            sq = tmp[kc % 2]
            P.act(sq[:], xs[:, kc, :], AF.Square)
            for tg in range(3):
                P.mm(banks[tg][:, 0:352], ones, sq[:, tg * 352:(tg + 1) * 352], start=(kc == 0), stop=(kc == 15))
        for tg in range(3):
            P.act(rstd[:, tg * 352:(tg + 1) * 352], banks[tg][:, 0:352], AF.Sqrt, scale=1.0 / D, bias=EPSB(P))
        P.recip(rstd[:], rstd[:])
        for kc in range(16):
            t = tmp[kc % 2]
            P.tt(t[:], xs[:, kc, :], rstd[:], ALU.mult)
            P.act(hT[:, kc, 0:NL], t[:, 0:NL], AF.Identity, scale=A[:, kc, 0:1], bias=mods[:, kc, 0:1])
            P.ts(hT[:, kc, NL:NT], t[:, NL:NT], A[:, kc, 1:2], ALU.mult, mods[:, kc, 1:2], ALU.add)

        wst = [P.sbuf([128, 16, 256], F32, "wst%d" % i) for i in range(2)]
        wbf = [P.sbuf([128, 16, 256], BF16, "wbf%d" % i) for i in range(2)]
        pout = [P.sbuf([128, NT], F32, "pout%d" % i) for i in range(3)]
        w_r = w_in.re("(kc k) c -> k kc c", k=128)
        groups = k1_groups()
        slabs = []
        i = 0
        while i < len(groups):
            c0, n, _ = groups[i]
            if n == 128 and i + 1 < len(groups) and groups[i + 1][1] == 128 and groups[i + 1][0] == c0 + 128:
                slabs.append((c0, 256, [groups[i], groups[i + 1]]))
                i += 2
            else:
                slabs.append((c0, n, [groups[i]]))
                i += 1
        bi = 0
        gi = 0
        for si, (c0, wn, grs) in enumerate(slabs):
            ws_, wb_ = wst[si % 2], wbf[si % 2]
            for kh in range(2):
                P.dma(ws_[:, kh * 8:(kh + 1) * 8, 0:wn], w_r[:, kh * 8:(kh + 1) * 8, c0:c0 + wn], nowaw=True)
            P.copy(wb_[:, :, 0:wn], ws_[:, :, 0:wn], eng=('dve' if si % 2 == 0 else 'pool'))
            for (gc0, gn, kind) in grs:
                off = gc0 - c0
                po = pout[gi % 3]
                gi += 1
                for tg in range(3):
                    bk = banks[3 + (bi % 5)]
                    bi += 1
                    for kc in range(16):
                        P.mm(bk[0:gn, 0:352], wb_[:, kc, off:off + gn], hT[:, kc, tg * 352:(tg + 1) * 352], start=(kc == 0), stop=(kc == 15))
                    P.act(po[0:gn, tg * 352:(tg + 1) * 352], bk[0:gn, 0:352], AF.Copy)
                if kind in ('gq_q', 'gq_k'):
                    gcol = qk[:, 0:1] if kind == 'gq_q' else qk[:, 1:2]
                    sq = tmp[0]
                    P.act(sq[:], po[:], AF.Square)
                    r2 = tmp[1]
                    for tg in range(3):
                        P.mm(banks[tg][:, 0:352], bo, sq[:, tg * 352:(tg + 1) * 352])
                        P.act(r2[:, tg * 352:(tg + 1) * 352], banks[tg][:, 0:352], AF.Sqrt, scale=1.0 / 64, bias=EPSB(P))
                    P.recip(r2[:], r2[:])
                    P.stt(po[:], po[:], gcol, r2[:], ALU.mult, ALU.mult)
                if kind is not None:
                    t1 = tmp[0]
                    for hh in range(2):
                        P.mm(banks[hh][:, 0:512], RT, po[:, hh * 512:(hh + 1) * 512])
                    P.tt(t1[:, 0:NL], po[:, 0:NL], cs_[:], ALU.mult)
                    for hh in range(2):
                        P.tt(po[:, hh * 512:(hh + 1) * 512], banks[hh][:, 0:512], sn_[:, hh * 512:(hh + 1) * 512], ALU.mult)
                    P.tt(po[:, 0:NL], po[:, 0:NL], t1[:, 0:NL], ALU.add)
                P.dma(pT[gc0:gc0 + gn, :], po[0:gn, :])
        P.finish([pT])
    return nc


def EPSB(P):
    if not hasattr(P, '_epsb'):
        P._epsb = P.sbuf([128, 1], F32, "epsb")
        P.memset(P._epsb[:], EPS)
    return P._epsb[:]


def rope_tables():
    rows = 8192 // 64
    row = np.repeat(np.arange(rows, dtype=np.float32), 64)
    col = np.tile(np.arange(64, dtype=np.float32), rows)
    n_freq = 16
    inv = (np.float32(10000.0) ** (-np.arange(n_freq, dtype=np.float32) / n_freq)).astype(np.float32)
    ang = np.concatenate([row[:, None] * inv, col[:, None] * inv], axis=-1).astype(np.float32)
    return np.cos(ang).astype(np.float32), np.sin(ang).astype(np.float32)


def const_mats():
    ones = np.ones((128, 128), np.float32)
    bo = np.zeros((128, 128), np.float32)
    bo[:64, :64] = 1
    bo[64:, 64:] = 1
    RT = np.zeros((128, 128), np.float32)
    for m in range(128):
        if (m % 64) < 32:
            RT[m + 32, m] = -1.0
        else:
            RT[m - 32, m] = 1.0
    return np.concatenate([ones, bo, RT], axis=1)


NT = 1056
NL = 1024
NK = 8448
KT = 66


def build_k2():
    nc, st, P = new_prog()
    with st:
        qT = P.dram("qT", [16, 64, NT], F32, "ExternalInput")
        kT = P.dram("kT", [10, 64, NK], F32, "ExternalInput")
        vv = P.dram("vv", [2, 128, KT * 64], F32, "ExternalInput")
        vd = P.dram("vd", [4, 128, KT * 128], F32, "ExternalInput")
        lamv = P.dram("lamv", [128, 256], F32, "ExternalInput")
        misc = P.dram("misc", [128, 4], F32, "ExternalInput")
        onesd = P.dram("ones", [128, 128], F32, "ExternalInput")
        yaT = P.dram("yaT", [512, NT], F32, "ExternalOutput")
        ydT = P.dram("ydT", [512, NT], F32, "ExternalOutput")

        stage = [P.sbuf([128, 2112], F32, "stage%d" % i) for i in range(2)]
        kbf = [P.sbuf([64, NK], BF16, "kbf%d" % i) for i in range(2)]
        qbf = [P.sbuf([64, NT], BF16, "qbf%d" % i) for i in range(2)]
        vbf = [P.sbuf([128, KT * 128], BF16, "vbf%d" % i) for i in range(2)]
        pb = [P.sbuf([128, 512], BF16, "pb%d" % i) for i in range(3)]
        ones_f = P.sbuf([128, 128], F32, "ones_f")
        ones_b = P.sbuf([128, 128], BF16, "ones_b")
        lv = P.sbuf([128, 256], F32, "lv")
        ms = P.sbuf([128, 4], F32, "ms")
        lam = P.sbuf([128, 4], F32, "lam")
        rd = P.sbuf([128, 512], F32, "rd")
        accA = P.sbuf([128, 512], F32, "accA")
        accB = P.sbuf([128, 512], F32, "accB")
        osb = [P.sbuf([128, NT], F32, "osb%d" % i) for i in range(3)]
        sq = P.sbuf([128, NT], F32, "sq")
        sbank = [P.psum([128, 512], F32, "sbank%d" % i) for i in range(3)]
        obank = [P.psum([128, 512], F32, "obank%d" % i) for i in range(2)]
        dbank = [P.psum([128, 512], F32, "dbank%d" % i) for i in range(2)]
        nbank = P.psum([128, 512], F32, "nbank")

        P.dma(ones_f[:], onesd[:])
        P.copy(ones_b[:], ones_f[:])
        P.dma(lv[:], lamv[:])
        P.dma(ms[:], misc[:])
        pr = P.sbuf([128, 128], F32, "pr")
        P.tt(pr[:, 0:64], lv[:, 0:64], lv[:, 64:128], ALU.mult)
        P.tt(pr[:, 64:128], lv[:, 128:192], lv[:, 192:256], ALU.mult)
        o0, i0 = lam[:, 0:1].ap, pr[:, 0:64].ap
        P.op('dve', lambda e: e.tensor_reduce(out=o0, in_=i0, axis=AX.X, op=ALU.add), reads=[pr], writes=[lam])
        o1, i1 = lam[:, 1:2].ap, pr[:, 64:128].ap
        P.op('dve', lambda e: e.tensor_reduce(out=o1, in_=i1, axis=AX.X, op=ALU.add), reads=[pr], writes=[lam])
        P.act(lam[:, 0:2], lam[:, 0:2], AF.Exp)
        P.tt(lam[:, 2:3], lam[:, 0:1], lam[:, 1:2], ALU.subtract)
        P.tt(lam[:, 2:3], lam[:, 2:3], ms[:, 1:2], ALU.add)
        P.ts(lam[:, 3:4], lam[:, 2:3], -1.0, ALU.mult)
        gsc = P.sbuf([128, 2], F32, "gsc")
        P.tt(gsc[:, 0:1], ms[:, 0:1], ms[:, 2:3], ALU.mult)

        cnt = {'s': 0, 'p': 0, 'a': 0, 'st': 0, 'o': 0}

        def load_cast(dst_view_fn, src_view_fn, np_, ncols_total, piece=2112):
            c = 0
            while c < ncols_total:
                n = min(piece, ncols_total - c)
                sg = stage[cnt['st'] % 2]
                cnt['st'] += 1
                P.dma(sg[0:np_, 0:n], src_view_fn(c, n))
                P.copy(dst_view_fn(c, n), sg[0:np_, 0:n], eng='pool')
                c += n

        def attend(kb, qb, vb, dv, ob):
            for (q0, qn, nkt) in [(0, 512, KT), (512, 512, KT), (1024, 32, 2)]:
                a = cnt['a'] % 2
                cnt['a'] += 1
                ob_, db_ = obank[a], dbank[a]
                for kt in range(nkt):
                    sb_ = sbank[cnt['s'] % 3]
                    cnt['s'] += 1
                    pt = pb[cnt['p'] % 3]
                    cnt['p'] += 1
                    P.mm(sb_[:, 0:qn], kb[:, kt * 128:(kt + 1) * 128], qb[:, q0:q0 + qn])
                    P.act(pt[:, 0:qn], sb_[:, 0:qn], AF.Exp, scale=0.125)
                    P.mm(ob_[0:dv, 0:qn], vb[:, kt * dv:(kt + 1) * dv], pt[:, 0:qn], start=(kt == 0), stop=(kt == nkt - 1))
                    if kt == 0:
                        P.copy(accA[:, 0:qn], pt[:, 0:qn])
                    elif kt == 1:
                        P.copy(accB[:, 0:qn], pt[:, 0:qn], eng='pool')
                    elif kt % 2 == 0:
                        P.tt(accA[:, 0:qn], accA[:, 0:qn], pt[:, 0:qn], ALU.add)
                    else:
                        P.tt(accB[:, 0:qn], accB[:, 0:qn], pt[:, 0:qn], ALU.add, eng='pool')
                P.tt(accA[:, 0:qn], accA[:, 0:qn], accB[:, 0:qn], ALU.add)
                P.mm(db_[0:dv, 0:qn], ones_f[:, 0:dv], accA[:, 0:qn])
                P.recip(rd[0:dv, 0:qn], db_[0:dv, 0:qn])
                P.tt(ob[0:dv, q0:q0 + qn], ob_[0:dv, 0:qn], rd[0:dv, 0:qn], ALU.mult)

        for g in range(2):
            kb = kbf[g % 2]
            load_cast(lambda c, n: kb[:, c:c + n], lambda c, n: kT[g, :, c:c + n], 64, NK)
            vb = vbf[g % 2]
            load_cast(lambda c, n: vb[:, c:c + n], lambda c, n: vv[g, :, c:c + n], 128, KT * 64)
            for hh in range(4):
                h = g * 4 + hh
                qb = qbf[h % 2]
                load_cast(lambda c, n: qb[:, c:c + n], lambda c, n: qT[h, :, c:c + n], 64, NT)
                ob = osb[cnt['o'] % 3]
                cnt['o'] += 1
                attend(kb, qb, vb, 64, ob)
                P.dma(yaT[h * 64:(h + 1) * 64, :], ob[0:64, :])
        for h in range(4):
            vb = vbf[h % 2]
            load_cast(lambda c, n: vb[:, c:c + n], lambda c, n: vd[h, :, c:c + n], 128, KT * 128)
            obs = []
            for m in range(2):
                u = h * 2 + m
                kb = kbf[u % 2]
                load_cast(lambda c, n: kb[:, c:c + n], lambda c, n: kT[2 + u, :, c:c + n], 64, NK)
                qb = qbf[u % 2]
                load_cast(lambda c, n: qb[:, c:c + n], lambda c, n: qT[8 + u, :, c:c + n], 64, NT)
                ob = osb[cnt['o'] % 3]
                cnt['o'] += 1
                attend(kb, qb, vb, 128, ob)
                obs.append(ob)
            o0_, o1_ = obs
            P.stt(o0_[:], o1_[:], lam[:, 3:4], o0_[:], ALU.mult, ALU.add)
            P.act(sq[:], o0_[:], AF.Square)
            for tg in range(3):
                P.mm(nbank[:, 0:352], ones_f[:], sq[:, tg * 352:(tg + 1) * 352])
                P.act(o1_[:, tg * 352:(tg + 1) * 352], nbank[:, 0:352], AF.Sqrt, scale=1.0 / 128, bias=EPSB(P))
            P.recip(o1_[:], o1_[:])
            P.stt(o0_[:], o0_[:], gsc[:, 0:1], o1_[:], ALU.mult, ALU.mult)
            P.dma(ydT[h * 128:(h + 1) * 128, :], o0_[:])
        P.finish([yaT, ydT])
    return nc


LL = 8192
LC = 256


def hy_consts(L, core):
    nb = L // 128
    cb = np.arange(2 * nb)[:, None]
    jp = np.arange(128)[None, :]
    b = cb - nb
    n_f = 128 * b + 127 - jp
    n_b = 128 * (-b) - 127 + jp
    n = np.where(b >= 0, n_f, n_b)
    valid = (n >= 0) & (n < L)
    n = np.where(valid, n, 0).astype(np.float32)
    t = (n / np.float32(L - 1)).astype(np.float32)
    w = (np.float32(2.0 * math.pi) * n / np.float32(L)).astype(np.float32)
    f = np.linspace(1e-4, 15, 16, dtype=np.float32)
    z = np.concatenate([t[..., None], np.cos(f * w[..., None]), -np.sin(f * w[..., None])], axis=-1).astype(np.float32)
    zT = np.ascontiguousarray(z.reshape(2 * nb * 128, 33).T)
    min_decay = math.log(1e-2) / 1.5
    max_decay = math.log(1e-2) / 0.3
    deltas = np.linspace(min_decay, max_decay, 512, dtype=np.float32)[core * 64:(core + 1) * 64]
    win = np.exp(-t[..., None] * np.abs(deltas)).astype(np.float32) * valid[..., None]
    win = np.ascontiguousarray(win.transpose(1, 0, 2).reshape(128, 2 * nb * 64)).astype(np.float32)
    return zT, win


def build_k3():
    nc, st, P = new_prog()
    with st:
        pl = P.dram("pl", [3, 64, LL], F32, "ExternalInput")
        pc = P.dram("pc", [3, 64, LC], F32, "ExternalInput")
        swd = P.dram("sw", [64, 9], F32, "ExternalInput")
        w1d = P.dram("w1", [33, 64], F32, "ExternalInput")
        w23d = P.dram("w23", [64, 128], F32, "ExternalInput")
        w4d = P.dram("w4s", [64, 128], F32, "ExternalInput")
        vecd = P.dram("vec", [64, 8], F32, "ExternalInput")
        zld = P.dram("zl", [33, 2 * 64 * 128], F32, "ExternalInput")
        zcd = P.dram("zc", [33, 2 * 2 * 128], F32, "ExternalInput")
        wld = P.dram("wl", [128, 128 * 64], F32, "ExternalInput")
        wcd = P.dram("wc", [128, 4 * 64], F32, "ExternalInput")
        identd = P.dram("ident", [128, 128], F32, "ExternalInput")
        yb = P.dram("yb", [64, LL + LC], F32, "ExternalOutput")
        upl_h = nc.dram_tensor("upl", [64, LL + 256], BF16, kind="Internal")
        upc_h = nc.dram_tensor("upc", [64, LC + 256], BF16, kind="Internal")
        upl = Buf(upl_h.ap(), "upl")
        upc = Buf(upc_h.ap(), "upc")

        sw = P.sbuf([64, 9], F32, "sw_s")
        w1 = P.sbuf([33, 64], F32, "w1_s")
        w23 = P.sbuf([64, 128], F32, "w23_s")
        w4 = P.sbuf([64, 128], F32, "w4_s")
        vec = P.sbuf([64, 8], F32, "vec_s")
        sc = P.sbuf([64, 4], F32, "sc_s")
        ident = P.sbuf([128, 128], F32, "ident_s")
        zero = P.sbuf([64, 136], BF16, "zero_s")
        for d, s in [(sw, swd), (w1, w1d), (w23, w23d), (w4, w4d), (vec, vecd), (ident, identd)]:
            P.dma(d[:], s[:])
        P.memset(zero[:], 0.0)
        P.ts(sc[:, 0:1], vec[:, 3:4], 1.0 / 3.0, ALU.mult)
        for k in range(3):
            P.tt(sc[:, 1 + k:2 + k], vec[:, k:k + 1], sc[:, 0:1], ALU.mult)

        PIECE = 2048
        pg = P.sbuf([64, 3, PIECE + 2], F32, "pg")
        cg = P.sbuf([64, 3, PIECE], F32, "cg")
        ub = P.sbuf([64, PIECE], BF16, "ub")
        Hm = P.sbuf([128, 64, 128], BF16, "Hm")
        ush = [P.sbuf([128, 65 * 128], BF16, "ush%d" % i) for i in range(2)]
        ytok = P.sbuf([128, 64, 64], F32, "ytok")
        yconv = P.sbuf([64, LL], F32, "yconv")
        zt = [P.sbuf([33, 512], F32, "zt%d" % i) for i in range(2)]
        hs = [P.sbuf([64, 512], F32, "hs%d" % i) for i in range(3)]
        s2 = P.sbuf([64, 512], F32, "s2")
        wp = [P.sbuf([128, 4, 64], F32, "wp%d" % i) for i in range(2)]
        fb = [P.psum([128, 512], F32, "fb%d" % i) for i in range(3)]
        hb = [P.psum([128, 512], F32, "hb%d" % i) for i in range(2)]
        yk = [P.psum([128, 512], F32, "yk%d" % i) for i in range(2)]
        tb = P.psum([128, 512], F32, "tb")

        def conv_piece(src, L, q, piece):
            lo = q * piece - 1
            hi = (q + 1) * piece + 1
            clo, chi = max(lo, 0), min(hi, L)
            if clo != lo or chi != hi:
                P.memset(pg[:], 0.0)
            for g in range(3):
                P.dma(pg[:, g, clo - lo:chi - lo], src[g, :, clo:chi])
            for g in range(3):
                P.ts(cg[:, g, 0:piece], pg[:, g, 1:1 + piece], sw[:, g * 3 + 1:g * 3 + 2], ALU.mult)
                P.stt(cg[:, g, 0:piece], pg[:, g, 0:piece], sw[:, g * 3:g * 3 + 1], cg[:, g, 0:piece], ALU.mult, ALU.add)
                P.stt(cg[:, g, 0:piece], pg[:, g, 2:2 + piece], sw[:, g * 3 + 2:g * 3 + 3], cg[:, g, 0:piece], ALU.mult, ALU.add)
            P.tt(cg[:, 1, 0:piece], cg[:, 1, 0:piece], cg[:, 2, 0:piece], ALU.mult)

        def sin3(out, ps, k):
            P.act(out[:], ps[0:64, :], AF.Sin, scale=sc[:, 0:1], bias=sc[:, 1 + k:2 + k])
            P.tt(s2[:], out[:], out[:], ALU.mult)
            P.ts(s2[:], s2[:], -4.0, ALU.mult, 3.0, ALU.add)
            P.tt(out[:], s2[:], out[:], ALU.mult)

        def run_seq(src, L, up, up_h, zd, wd, out0):
            nb = L // 128
            piece = min(L, PIECE)
            npieces = L // piece
            W = L + 256
            P.dma(up[:, 0:127], zero[:, 0:127])
            P.dma(up[:, 127 + L:W], zero[:, 0:129])
            for q in range(npieces):
                conv_piece(src, L, q, piece)
                P.copy(ub[:, 0:piece], cg[:, 1, 0:piece])
                P.dma(up[:, 127 + q * piece:127 + (q + 1) * piece], ub[:, 0:piece])
            ntile = (2 * nb * 128) // 512
            for ti in range(ntile):
                z_ = zt[ti % 2]
                P.dma(z_[:], zd[:, ti * 512:(ti + 1) * 512])
                P.mm(fb[0][0:64, :], w1[:, :], z_[:, :])
                sin3(hs[0], fb[0], 0)
                P.mm(fb[1][0:64, :], w23[:, 0:64], hs[0][:, :])
                sin3(hs[1], fb[1], 1)
                P.mm(fb[2][0:64, :], w23[:, 64:128], hs[1][:, :])
                sin3(hs[2], fb[2], 2)
                hbk = hb[ti % 2]
                for k in range(4):
                    cbi = ti * 4 + k
                    wcol = 0 if cbi >= nb else 64
                    P.mm(hbk[:, k * 64:(k + 1) * 64], hs[2][:, k * 128:(k + 1) * 128], w4[:, wcol:wcol + 64])
                wp_ = wp[ti % 2]
                P.dma(wp_[:], wd.re("p (b c) -> p b c", c=64)[:, ti * 4:(ti + 1) * 4, :])
                hv = View(hbk, hbk.t[:, 0:256].rearrange("p (k c) -> p k c", k=4))
                ov = View(Hm, Hm.t[:, :, ti * 4:(ti + 1) * 4].rearrange("p c k -> p k c"))
                P.tt(ov, hv, wp_[:], ALU.mult)
            ncol = (nb + 1) * 128
            for c in range(64):
                us = ush[c % 2]
                src_ap = bass.AP(up_h, c * W, [[1, 128], [1, ncol]])
                P.dma(us[:, 0:ncol], View(up, src_ap))
                ykb = yk[(c // 8) % 2]
                c8 = c % 8
                for m in range(nb + 1):
                    P.mm(ykb[:, c8 * nb:(c8 + 1) * nb], us[:, m * 128:(m + 1) * 128], Hm[:, c, nb - m:2 * nb - m],
                         start=(m == 0), stop=(m == nb))
                if c8 == 7:
                    c0 = c - 7
                    iv = View(ykb, ykb.t[:, 0:8 * nb].rearrange("p (c a) -> p c a", c=8))
                    ov = View(ytok, ytok.t[:, 0:nb, c0:c0 + 8].rearrange("p a c -> p c a"))
                    P.act(ov, iv, AF.Copy)
            for a in range(nb):
                k = a % 4
                P.transpose(tb[0:64, k * 128:(k + 1) * 128], ytok[:, a, :], ident[:])
                if k == 3 or a == nb - 1:
                    a0 = a - k
                    P.act(yconv[:, a0 * 128:(a + 1) * 128], tb[0:64, 0:(k + 1) * 128], AF.Copy)
            for q in range(npieces):
                conv_piece(src, L, q, piece)
                P.stt(cg[:, 1, 0:piece], cg[:, 1, 0:piece], vec[:, 4:5], yconv[:, q * piece:(q + 1) * piece], ALU.mult, ALU.add)
                P.tt(cg[:, 0, 0:piece], cg[:, 0, 0:piece], cg[:, 1, 0:piece], ALU.mult)
                P.dma(yb[:, out0 + q * piece:out0 + (q + 1) * piece], cg[:, 0, 0:piece])

        run_seq(pl, LL, upl, upl_h, zld, wld, 0)
        run_seq(pc, LC, upc, upc_h, zcd, wcd, LL)
        P.finish([yb])
    return nc


NS = 8448
NCH = 132


def dn_consts():
    U = np.triu(np.ones((64, 64), np.float32))
    Ls = np.tril(np.ones((64, 64), np.float32), -1)
    cm = np.zeros((128, 512), np.float32)
    cm[:64, 0:64] = U
    cm[:64, 64:128] = Ls
    cm[:, 128:256] = np.eye(128, dtype=np.float32)
    cm[:, 256:384] = 1.0
    return cm


def build_k4():
    nc, st, P = new_prog()
    with st:
        pqkv = P.dram("pqkv", [3, 128, NS], F32, "ExternalInput")
        tapsd = P.dram("taps", [128, 9], F32, "ExternalInput")
        grawd = P.dram("graw", [64, 2 * NCH], F32, "ExternalInput")
        scd = P.dram("scal", [128, 2], F32, "ExternalInput")
        cmd = P.dram("cm", [128, 512], F32, "ExternalInput")
        oT = P.dram("oT", [128, NS], F32, "ExternalOutput")

        cm = P.sbuf([128, 512], F32, "cm_s")
        taps = P.sbuf([128, 9], F32, "taps_s")
        graw = P.sbuf([64, 2 * NCH], F32, "graw_s")
        scl = P.sbuf([128, 2], F32, "scl_s")
        for d, s in [(cm, cmd), (taps, tapsd), (graw, grawd), (scl, scd)]:
            P.dma(d[:], s[:])
        U = cm[0:64, 0:64]
        Ls = cm[0:64, 64:128]
        I64 = cm[0:64, 128:192]
        I128 = cm[:, 128:256]
        ones = cm[:, 256:384]
        ones64 = cm[0:64, 256:384]
        eps = P.sbuf([128, 1], F32, "eps_s")
        P.memset(eps[:], 1e-6)

        qkv = [P.sbuf([128, NS], F32, "qkv%d" % g) for g in range(3)]
        PIECE = 2048
        pg = P.sbuf([128, PIECE + 2], F32, "pg")
        cgt = P.sbuf([128, PIECE], F32, "cgt")
        sg = P.sbuf([128, PIECE], F32, "sg")
        bank = [P.psum([128, 512], F32, "bk%d" % i) for i in range(8)]
        bctr = [0]

        def nb():
            b = bank[bctr[0] % 8]
            bctr[0] += 1
            return b

        segs = [(0, 256)] + [(256 + i * PIECE, PIECE) for i in range(4)]
        for g in range(3):
            for (s0, n) in segs:
                first = (s0 == 0 or s0 == 256)
                last = (s0 + n == 256 or s0 + n == NS)
                lo = s0 - (0 if first else 1)
                hi = s0 + n + (0 if last else 1)
                if first or last:
                    P.memset(pg[:], 0.0)
                P.dma(pg[:, 1 - (s0 - lo):1 + n + (hi - s0 - n)], pqkv[g, :, lo:hi])
                P.ts(cgt[:, 0:n], pg[:, 1:1 + n], taps[:, g * 3 + 1:g * 3 + 2], ALU.mult)
                P.stt(cgt[:, 0:n], pg[:, 0:n], taps[:, g * 3:g * 3 + 1], cgt[:, 0:n], ALU.mult, ALU.add)
                P.stt(cgt[:, 0:n], pg[:, 2:2 + n], taps[:, g * 3 + 2:g * 3 + 3], cgt[:, 0:n], ALU.mult, ALU.add)
                P.act(qkv[g][:, s0:s0 + n], cgt[:, 0:n], AF.Silu)
                if g < 2:
                    P.act(sg[:, 0:n], qkv[g][:, s0:s0 + n], AF.Square)
                    for c0 in range(0, n, 512):
                        w = min(512, n - c0)
                        b = nb()
                        P.mm(b[:, 0:w], ones, sg[:, c0:c0 + w])
                        P.act(sg[:, c0:c0 + w], b[:, 0:w], AF.Sqrt, bias=eps[:])
                    P.recip(sg[:, 0:n], sg[:, 0:n])
                    if g == 0:
                        P.stt(qkv[g][:, s0:s0 + n], qkv[g][:, s0:s0 + n], 128.0 ** -0.5, sg[:, 0:n], ALU.mult, ALU.mult)
                    else:
                        P.tt(qkv[g][:, s0:s0 + n], qkv[g][:, s0:s0 + n], sg[:, 0:n], ALU.mult)
        qT, kT, vT = qkv

        beta = P.sbuf([64, NCH], F32, "beta")
        nbeta = P.sbuf([64, NCH], F32, "nbeta")
        gg = P.sbuf([64, NCH], F32, "gg")
        ea = P.sbuf([128, 1], F32, "ea")
        P.act(beta[:], graw[:, 0:NCH], AF.Sigmoid)
        P.ts(nbeta[:], beta[:], -1.0, ALU.mult)
        P.act(gg[:], graw[:, NCH:2 * NCH], AF.Exp, bias=scl[0:64, 1:2])
        P.act(gg[:], gg[:], AF.Ln, bias=cm[0:64, 256:257])
        P.act(ea[:], scl[:, 0:1], AF.Exp)
        P.ts(gg[:], gg[:], ea[0:64, 0:1], ALU.mult, -1.0, ALU.mult)
        gc = P.sbuf([64, NCH], F32, "gc")
        egc = P.sbuf([64, NCH], F32, "egc")
        bke = P.sbuf([64, NCH], F32, "bke")
        kde = P.sbuf([64, NCH], F32, "kde")
        lastB = P.sbuf([128, NCH], F32, "lastB")
        b = nb()
        P.mm(b[0:64, 0:NCH], U, gg[:, :])
        P.act(gc[:], b[0:64, 0:NCH], AF.Copy)
        P.act(egc[:], gc[:], AF.Exp)
        P.tt(bke[:], beta[:], egc[:], ALU.mult)
        b = nb()
        P.mm(b[:, 0:NCH], ones64, gg[:, :])
        P.act(lastB[:], b[:, 0:NCH], AF.Exp)
        P.tt(kde[:], b[0:64, 0:NCH], gc[:], ALU.subtract)
        P.act(kde[:], kde[:], AF.Exp)

        S = [P.sbuf([128, 128], F32, "S%d" % i) for i in range(2)]
        P.memset(S[0][:], 0.0)
        oacc = P.sbuf([128, NS], F32, "oacc")

        def T(shape, name, n=2):
            return [P.sbuf(shape, F32, "%s%d" % (name, i)) for i in range(n)]
        gB = T([64, 128], "gB"); tt_ = T([128, 64], "tt"); a1 = T([64, 64], "a1"); gs = T([64, 64], "gs")
        gT_ = T([64, 64], "gT"); egr = T([128, 64], "egr"); X = T([64, 64], "X", 4); Z = T([64, 64], "Z", 4)
        W = T([64, 64], "W", 4); vb = T([64, 128], "vb"); kbd = T([64, 128], "kbd"); kd = T([64, 128], "kd")
        u = T([64, 128], "u"); wT = T([128, 64], "wT"); qd = T([128, 64], "qd"); qk = T([64, 64], "qk")
        vn = T([64, 128], "vn")
        for c in range(NCH):
            cols = slice(64 * c, 64 * c + 64)
            i2 = c % 2
            P.act(gB[i2][:], ones64, AF.Identity, scale=gg[:, c:c + 1])
            b1 = nb()
            P.mm(b1[:, 0:64], gB[i2][:], U)
            P.act(tt_[i2][:], b1[:, 0:64], AF.Copy)
            P.act(egr[i2][:], tt_[i2][:], AF.Exp)
            P.ts(a1[i2][:], tt_[i2][0:64, :], gc[:, c:c + 1], ALU.subtract, 0.0, ALU.min)
            P.act(gT_[i2][:], a1[i2][:], AF.Exp)
            P.tt(gT_[i2][:], gT_[i2][:], U, ALU.mult)
            P.ts(a1[i2][:], tt_[i2][0:64, :], gc[:, c:c + 1], ALU.subtract, -1.0, ALU.mult)
            P.ts(a1[i2][:], a1[i2][:], 0.0, ALU.min)
            P.act(gs[i2][:], a1[i2][:], AF.Exp)
            P.tt(gs[i2][:], gs[i2][:], Ls, ALU.mult)
            b2 = nb()
            P.mm(b2[0:64, 0:64], kT[:, cols], kT[:, cols])
            x0 = X[0]
            P.stt(x0[:], b2[0:64, 0:64], nbeta[:, c:c + 1], gs[i2][:], ALU.mult, ALU.mult)
            b3 = nb()
            P.transpose(b3[0:64, 0:64], x0[:], I64)
            z0 = Z[0]
            P.act(z0[:], b3[0:64, 0:64], AF.Copy)
            w0 = W[0]
            P.tt(w0[:], z0[:], I64, ALU.add)
            xk, zk, wk = x0, z0, w0
            for lev in range(5):
                xn, zn, wn = X[(lev + 1) % 4], Z[(lev + 1) % 4], W[(lev + 1) % 4]
                bx = nb()
                P.mm(bx[0:64, 0:64], zk[:], xk[:])
                P.act(xn[:], bx[0:64, 0:64], AF.Copy)
                if lev < 4:
                    bz = nb()
                    P.mm(bz[0:64, 0:64], xk[:], zk[:])
                    P.copy(zn[:], bz[0:64, 0:64])
                bw = nb()
                P.mm(bw[0:64, 0:64], xn[:], wk[:])
                P.tt(wn[:], bw[0:64, 0:64], wk[:], ALU.add)
                xk, zk, wk = xn, zn, wn
            Wf = wk
            bv = nb()
            P.transpose(bv[0:64, 0:128], vT[:, cols], I128)
            P.ts(vb[i2][:], bv[0:64, 0:128], beta[:, c:c + 1], ALU.mult)
            bk_ = nb()
            P.transpose(bk_[0:64, 0:128], kT[:, cols], I128)
            P.ts(kbd[i2][:], bk_[0:64, 0:128], bke[:, c:c + 1], ALU.mult)
            P.ts(kd[i2][:], bk_[0:64, 0:128], kde[:, c:c + 1], ALU.mult)
            bu = nb()
            P.mm(bu[0:64, 0:128], Wf[:], vb[i2][:])
            P.act(u[i2][:], bu[0:64, 0:128], AF.Copy)
            bwt = nb()
            P.mm(bwt[:, 0:64], kbd[i2][:], Wf[:])
            P.act(wT[i2][:], bwt[:, 0:64], AF.Copy)
            P.tt(qd[i2][:], qT[:, cols], egr[i2][:], ALU.mult)
            bq = nb()
            P.mm(bq[0:64, 0:64], kT[:, cols], qT[:, cols])
            P.tt(qk[i2][:], bq[0:64, 0:64], gT_[i2][:], ALU.mult)
            Sc, Sn = S[c % 2], S[(c + 1) % 2]
            bs = nb()
            P.mm(bs[0:64, 0:128], wT[i2][:], Sc[:])
            P.tt(vn[i2][:], u[i2][:], bs[0:64, 0:128], ALU.subtract)
            bo = nb()
            P.mm(bo[:, 0:64], Sc[:], qd[i2][:], start=True, stop=False)
            P.mm(bo[:, 0:64], vn[i2][:], qk[i2][:], start=False, stop=True)
            P.act(oacc[:, cols], bo[:, 0:64], AF.Copy)
            bn = nb()
            P.mm(bn[:, 0:128], kd[i2][:], vn[i2][:])
            P.stt(Sn[:], Sc[:], lastB[:, c:c + 1], bn[:, 0:128], ALU.mult, ALU.add)
        for i in range(4):
            P.dma(oT[:, i * 2112:(i + 1) * 2112], oacc[:, i * 2112:(i + 1) * 2112])
        P.finish([oT])
    return nc


D = 2048
NT = 1056
NL = 1024
DFF = 5632


def sumsq_rstd(P, src, nchunks, n, ones, bank, sqtmp, rstd, dim):
    for kc in range(nchunks):
        sq = sqtmp[kc % 2]
        P.act(sq[:, 0:n], src[:, kc, 0:n], AF.Square)
        P.mm(bank[:, 0:n], ones, sq[:, 0:n], start=(kc == 0), stop=(kc == nchunks - 1))
    P.act(rstd[:, 0:n], bank[:, 0:n], AF.Sqrt, scale=1.0 / dim, bias=EPSB(P))
    P.recip(rstd[:, 0:n], rstd[:, 0:n])


def build_k5a():
    nc, st, P = new_prog()
    with st:
        yT = P.dram("yT", [D, NT], F32, "ExternalInput")
        xT = P.dram("xT", [D, NT], F32, "ExternalInput")
        w_out = P.dram("w_out", [D, D], F32, "ExternalInput")
        modT = P.dram("modT", [128, 192], F32, "ExternalInput")
        gains = P.dram("gains", [128, 32], F32, "ExternalInput")
        onesd = P.dram("ones", [128, 128], F32, "ExternalInput")
        ofT = P.dram("ofT", [512, NT], F32, "ExternalInput")
        obT = P.dram("obT", [512, NT], F32, "ExternalInput")
        gtT = P.dram("gtT", [512, NT], F32, "ExternalInput")
        dngd = P.dram("dng", [128, 1], F32, "ExternalInput")
        xmT = P.dram("xmT", [D, NT], F32, "ExternalOutput")
        hT = P.dram("hT", [D, NT], F32, "ExternalOutput")
        dng = P.sbuf([128, 1], F32, "dng_s")
        P.dma(dng[:], dngd[:])
        dn_o = P.sbuf([128, 352], F32, "dn_o")
        dn_b = P.sbuf([128, 352], F32, "dn_b")
        dn_g = P.sbuf([128, 352], F32, "dn_g")

        wob = P.sbuf([128, 16, D], BF16, "wob")
        wst = [P.sbuf([128, 16, 256], F32, "wst%d" % i) for i in range(2)]
        mods = P.sbuf([128, 96, 2], F32, "mods")
        gs = P.sbuf([128, 32], F32, "gs")
        ones = P.sbuf([128, 128], F32, "ones_s")
        G1 = P.sbuf([128, 16, 2], F32, "G1")
        A2 = P.sbuf([128, 16, 2], F32, "A2")
        yb = P.sbuf([128, 16, 352], BF16, "yb")
        ystage = [P.sbuf([128, 352], F32, "ystage%d" % i) for i in range(2)]
        z = P.sbuf([128, 16, 352], F32, "z")
        xg = P.sbuf([128, 16, 352], F32, "xg")
        sqt = [P.sbuf([128, 352], F32, "sqt%d" % i) for i in range(2)]
        tmp = [P.sbuf([128, 352], F32, "tmp%d" % i) for i in range(2)]
        hout = [P.sbuf([128, 352], F32, "hout%d" % i) for i in range(2)]
        rstd = P.sbuf([128, 352], F32, "rstd")
        banks = [P.psum([128, 512], F32, "bank%d" % i) for i in range(6)]
        nbank = P.psum([128, 512], F32, "nbank")

        P.dma(mods[:], modT.re("k (c r) -> k c r", r=2)[:, :, :])
        P.dma(gs[:], gains[:])
        P.dma(ones[:], onesd[:])
        w_r = w_out.re("(kc k) c -> k kc c", k=128)
        for s in range(8):
            ws_ = wst[s % 2]
            for kh in range(2):
                P.dma(ws_[:, kh * 8:(kh + 1) * 8, :], w_r[:, kh * 8:(kh + 1) * 8, s * 256:(s + 1) * 256], nowaw=True)
            P.copy(wob[:, :, s * 256:(s + 1) * 256], ws_[:], eng=('dve' if s % 2 == 0 else 'pool'))
        for r in range(2):
            P.tt(G1[:, :, r], mods[:, 32:48, r], gs[:, 0:16], ALU.mult)
            P.ts(A2[:, :, r], mods[:, 64:80, r], 1.0, ALU.add)
            P.tt(A2[:, :, r], A2[:, :, r], gs[:, 16:32], ALU.mult)
        yT_r = yT.re("(kc k) n -> k kc n", k=128)
        xT_r = xT.re("(kc k) n -> k kc n", k=128)
        xm_r = xmT.re("(kc k) n -> k kc n", k=128)
        hT_r = hT.re("(kc k) n -> k kc n", k=128)
        bi = 0
        for tg in range(3):
            c0 = tg * 352
            nl = 352 if tg < 2 else 320
            rngs = [(0, nl, 0)] + ([(nl, 352, 1)] if nl < 352 else [])
            for kc in range(16):
                ys_ = ystage[kc % 2]
                if 8 <= kc < 12:
                    hh = kc - 8
                    P.dma(dn_o[:], ofT[hh * 128:(hh + 1) * 128, c0:c0 + 352])
                    P.dma(dn_b[:], obT[hh * 128:(hh + 1) * 128, c0:c0 + 352])
                    P.dma(dn_g[:], gtT[hh * 128:(hh + 1) * 128, c0:c0 + 352])
                    P.tt(dn_o[:], dn_o[:], dn_b[:], ALU.add)
                    P.act(dn_b[:], dn_o[:], AF.Square)
                    P.mm(nbank[:, 0:352], ones[:], dn_b[:])
                    P.act(dn_b[:], nbank[:, 0:352], AF.Sqrt, scale=1.0 / 128, bias=EPSB(P))
                    P.recip(dn_b[:], dn_b[:])
                    P.stt(dn_o[:], dn_o[:], dng[:, 0:1], dn_b[:], ALU.mult, ALU.mult)
                    P.act(dn_g[:], dn_g[:], AF.Silu)
                    P.tt(ys_[:], dn_o[:], dn_g[:], ALU.mult)
                else:
                    P.dma(ys_[:], yT_r[:, kc, c0:c0 + 352])
                P.copy(yb[:, kc, :], ys_[:], eng=('dve' if kc % 2 == 0 else 'pool'))
            P.dma(xg[:, 0:8, :], xT_r[:, 0:8, c0:c0 + 352])
            P.dma(xg[:, 8:16, :], xT_r[:, 8:16, c0:c0 + 352])
            for m in range(16):
                bk = banks[bi % 6]
                bi += 1
                for kc in range(16):
                    P.mm(bk[:, 0:352], wob[:, kc, m * 128:(m + 1) * 128], yb[:, kc, :], start=(kc == 0), stop=(kc == 15))
                P.act(z[:, m, :], bk[:, 0:352], AF.Copy)
            sumsq_rstd(P, z, 16, 352, ones[:], nbank, sqt, rstd, D)
            for kc in range(16):
                t = tmp[kc % 2]
                P.tt(t[:], z[:, kc, :], rstd[:], ALU.mult)
                for (a, b, r) in rngs:
                    P.stt(xg[:, kc, a:b], t[:, a:b], G1[:, kc, r:r + 1], xg[:, kc, a:b], ALU.mult, ALU.add)
            sumsq_rstd(P, xg, 16, 352, ones[:], nbank, sqt, rstd, D)
            for kc in range(16):
                t = tmp[kc % 2]
                ho = hout[kc % 2]
                P.tt(t[:], xg[:, kc, :], rstd[:], ALU.mult)
                for (a, b, r) in rngs:
                    P.ts(ho[:, a:b], t[:, a:b], A2[:, kc, r:r + 1], ALU.mult, mods[:, 48 + kc, r:r + 1], ALU.add)
                P.dma(hT_r[:, kc, c0:c0 + 352], ho[:])
            P.dma(xm_r[:, 0:8, c0:c0 + 352], xg[:, 0:8, :])
            P.dma(xm_r[:, 8:16, c0:c0 + 352], xg[:, 8:16, :])
        P.finish([xmT, hT])
    return nc


NP5 = 1060


def build_k5b():
    nc, st, P = new_prog()
    with st:
        hp = P.dram("hp", [D, NP5], F32, "ExternalInput")
        xmT = P.dram("xmT", [D, NT], F32, "ExternalInput")
        w_up = P.dram("w_up", [D, 2 * DFF], F32, "ExternalInput")
        wcv = P.dram("wcv", [128, 88 * 3], F32, "ExternalInput")
        w_dn = P.dram("w_dn", [DFF, D], F32, "ExternalInput")
        modT = P.dram("modT", [128, 192], F32, "ExternalInput")
        gains = P.dram("gains", [128, 16], F32, "ExternalInput")
        onesd = P.dram("ones", [128, 128], F32, "ExternalInput")
        xoT = P.dram("xoT", [D, NT], F32, "ExternalOutput")

        stf = [P.sbuf([128, 5632], F32, "stf%d" % i) for i in range(2)]
        stb = [P.sbuf([128, 5632], BF16, "stb%d" % i) for i in range(2)]
        mods = P.sbuf([128, 96, 2], F32, "mods")
        gs = P.sbuf([128, 16], F32, "gs")
        wc = P.sbuf([128, 88, 3], F32, "wc")
        ones = P.sbuf([128, 128], F32, "ones_s")
        G2 = P.sbuf([128, 16, 2], F32, "G2")
        hg = P.sbuf([128, 16, 376], BF16, "hg")
        hst = [P.sbuf([128, 376], F32, "hst%d" % i) for i in range(2)]
        gT = P.sbuf([128, 44, 372], BF16, "gT")
        dn = P.sbuf([128, 16, 372], F32, "dn")
        xg = P.sbuf([128, 16, 372], F32, "xg")
        ca = [P.sbuf([128, 372], F32, "ca%d" % i) for i in range(2)]
        cb = [P.sbuf([128, 372], F32, "cb%d" % i) for i in range(2)]
        sqt = [P.sbuf([128, 372], F32, "sqt%d" % i) for i in range(2)]
        tmp = [P.sbuf([128, 372], F32, "tmp%d" % i) for i in range(2)]
        rstd = P.sbuf([128, 372], F32, "rstd")
        banks = [P.psum([128, 512], F32, "bank%d" % i) for i in range(6)]
        nbank = P.psum([128, 512], F32, "nbank")

        P.dma(mods[:], modT.re("k (c r) -> k c r", r=2)[:, :, :])
        P.dma(gs[:], gains[:])
        P.dma(ones[:], onesd[:])
        P.dma(wc[:], wcv.re("k (c t) -> k c t", t=3)[:, :, :])
        for r in range(2):
            P.tt(G2[:, :, r], mods[:, 80:96, r], gs[:], ALU.mult)
        hp_r = hp.re("(kc k) n -> k kc n", k=128)
        xm_r = xmT.re("(kc k) n -> k kc n", k=128)
        xo_r = xoT.re("(kc k) n -> k kc n", k=128)
        wu_r = w_up.re("(kc k) c -> k kc c", k=128)
        wd_r = w_dn.re("(f k) c -> k f c", k=128)
        groups = [(0, 344, [(0, 342)], 0, 342, 342),
                  (342, 344, [(0, 342)], 342, 342, 342),
                  (684, 376, [(0, 340), (342, 32)], 684, 372, 340)]
        si = 0
        bi = 0
        for (u0, un, segs, xc0, gn, nl) in groups:
            for kc in range(16):
                hs_ = hst[kc % 2]
                P.dma(hs_[:, 0:un], hp_r[:, kc, u0:u0 + un])
                P.copy(hg[:, kc, 0:un], hs_[:, 0:un], eng=('dve' if kc % 2 == 0 else 'pool'))
            P.dma(xg[:, 0:8, 0:gn], xm_r[:, 0:8, xc0:xc0 + gn])
            P.dma(xg[:, 8:16, 0:gn], xm_r[:, 8:16, xc0:xc0 + gn])
            for f in range(44):
                sf, sb_ = stf[si % 2], stb[si % 2]
                si += 1
                sfv = sf.v(sf.t[:, 0:4096].rearrange("p (a b c) -> p a b c", a=16, b=2))
                sbv = sb_.v(sb_.t[:, 0:4096].rearrange("p (a b c) -> p a b c", a=16, b=2))
                P.dma(View(sf, sfv.ap[:, :, 0, :]), wu_r[:, :, f * 128:(f + 1) * 128], nowaw=True)
                P.dma(View(sf, sfv.ap[:, :, 1, :]), wu_r[:, :, DFF + f * 128:DFF + (f + 1) * 128], nowaw=True)
                P.copy(sb_[:, 0:4096], sf[:, 0:4096], eng='pool')
                bka = banks[bi % 6]
                bkb = banks[(bi + 1) % 6]
                bi += 2
                for kc in range(16):
                    P.mm(bka[:, 0:un], View(sb_, sbv.ap[:, kc, 0, :]), hg[:, kc, 0:un], start=(kc == 0), stop=(kc == 15))
                for kc in range(16):
                    P.mm(bkb[:, 0:un], View(sb_, sbv.ap[:, kc, 1, :]), hg[:, kc, 0:un], start=(kc == 0), stop=(kc == 15))
                ca_, cb_ = ca[f % 2], cb[f % 2]
                goff = 0
                for (lo, n) in segs:
                    for (cc_, bk, ch) in [(ca_, bka, f), (cb_, bkb, 44 + f)]:
                        P.ts(cc_[:, goff:goff + n], bk[:, lo + 1:lo + 1 + n], wc[:, ch, 1:2], ALU.mult)
                        P.stt(cc_[:, goff:goff + n], bk[:, lo:lo + n], wc[:, ch, 0:1], cc_[:, goff:goff + n], ALU.mult, ALU.add)
                        P.stt(cc_[:, goff:goff + n], bk[:, lo + 2:lo + 2 + n], wc[:, ch, 2:3], cc_[:, goff:goff + n], ALU.mult, ALU.add)
                    goff += n
                P.act(ca_[:, 0:gn], ca_[:, 0:gn], AF.Silu)
                P.tt(gT[:, f, 0:gn], ca_[:, 0:gn], cb_[:, 0:gn], ALU.mult)
            for m in range(16):
                sf, sb_ = stf[si % 2], stb[si % 2]
                si += 1
                sfv = sf.v(sf.t[:, :].rearrange("p (f c) -> p f c", f=44))
                sbv = sb_.v(sb_.t[:, :].rearrange("p (f c) -> p f c", f=44))
                P.dma(View(sf, sfv.ap[:, 0:22, :]), wd_r[:, 0:22, m * 128:(m + 1) * 128], nowaw=True)
                P.dma(View(sf, sfv.ap[:, 22:44, :]), wd_r[:, 22:44, m * 128:(m + 1) * 128], nowaw=True)
                P.copy(sb_[:], sf[:], eng='pool')
                bk = banks[bi % 6]
                bi += 1
                for f in range(44):
                    P.mm(bk[:, 0:gn], View(sb_, sbv.ap[:, f, :]), gT[:, f, 0:gn], start=(f == 0), stop=(f == 43))
                P.act(dn[:, m, 0:gn], bk[:, 0:gn], AF.Copy)
            sumsq_rstd(P, dn, 16, gn, ones[:], nbank, sqt, rstd, D)
            rngs = [(0, nl, 0)] + ([(nl, gn, 1)] if nl < gn else [])
            for kc in range(16):
                t = tmp[kc % 2]
                P.tt(t[:, 0:gn], dn[:, kc, 0:gn], rstd[:, 0:gn], ALU.mult)
                for (a, b, r) in rngs:
                    P.stt(xg[:, kc, a:b], t[:, a:b], G2[:, kc, r:r + 1], xg[:, kc, a:b], ALU.mult, ALU.add)
            P.dma(xo_r[:, 0:8, xc0:xc0 + gn], xg[:, 0:8, 0:gn])
            P.dma(xo_r[:, 8:16, xc0:xc0 + gn], xg[:, 8:16, 0:gn])
        P.finish([xoT])
    return nc


_PROGS = {}


def _prog(name, fn):
    if name not in _PROGS:
        _PROGS[name] = fn()
    return _PROGS[name]


def _run(name, fn, maps):
    nc = fn()
    res = run_bass_kernel_spmd(nc, maps, core_ids=list(range(8)))
    return res.results


def _c(a):
    return np.ascontiguousarray(a, dtype=np.float32)


def kernel(x, c, ctx, c_ctx, w_ada, b_ada, norm_mix_pre, norm_mix_post, norm_ffn_pre,
           norm_ffn_post, w_in, w_out, attn_q_norm, attn_k_norm, hy_short, hy_w1, hy_b1,
           hy_w2, hy_b2, hy_w3, hy_b3, hy_w4, hy_freq, hy_skip, dn_short, dn_a_log,
           dn_dt_bias, dn_norm, df_lambda, df_norm, ffn_up, ffn_conv, ffn_down):
    f = lambda a: np.asarray(a, dtype=np.float32)
    x = f(x)[0]; ctxv = f(ctx)[0]; c = f(c); c_ctx = f(c_ctx)
    w_ada = f(w_ada); b_ada = f(b_ada); w_in = f(w_in); w_out = f(w_out)
    ffn_up = f(ffn_up); ffn_conv = f(ffn_conv); ffn_down = f(ffn_down)
    ones = np.ones((128, 128), np.float32)
    ident = np.eye(128, dtype=np.float32)
    cc = np.stack([c[0], c_ctx], axis=-1).reshape(16, 128, 2).transpose(1, 0, 2).reshape(128, 32)
    maps = []
    for j in range(8):
        l, q = j // 4, j % 4
        maps.append({"cc": _c(cc), "w": _c(w_ada[l][:, q * 3072:(q + 1) * 3072]),
                     "b2": _c(np.broadcast_to(b_ada[l][None, q * 3072:(q + 1) * 3072], (2, 3072)))})
    r = _run('k0', build_k0, maps)
    mod = np.zeros((2, 2, 12288), np.float32)
    for j in range(8):
        l, q = j // 4, j % 4
        mod[l][:, q * 3072:(q + 1) * 3072] = r[j]["mod"]
    cos, sin = rope_tables()
    cm1 = const_mats()
    cm4 = dn_consts()
    hyc = [(hy_consts(LL, j), hy_consts(LC, j)) for j in range(8)]

    def shard_T(lat, cx, j):
        return _c(np.concatenate([lat[j * 1024:(j + 1) * 1024], cx[j * 32:(j + 1) * 32]], axis=0).T)

    for L in range(2):
        modT = _c(mod[L].reshape(2, 96, 128).transpose(2, 1, 0).reshape(128, 192))
        gain = _c(f(norm_mix_pre)[L].reshape(16, 128).T)
        qkg = _c(np.stack([np.tile(f(attn_q_norm)[L], 2), np.tile(f(attn_k_norm)[L], 2)], axis=1))
        maps = []
        for j in range(8):
            cj = np.tile(cos[j * 1024:(j + 1) * 1024].T, (4, 1))
            sj = np.tile(sin[j * 1024:(j + 1) * 1024].T, (4, 1))
            maps.append({"xT": shard_T(x, ctxv, j), "modT": modT, "gain": gain, "w_in": _c(w_in[L]), "qkg": qkg,
                         "cosT": _c(cj), "sinT": _c(sj), "cmat": cm1})
        r = _run('k1', build_k1, maps)
        pT = [r[j]["pT"] for j in range(8)]
        lat = np.concatenate([p[:, :1024] for p in pT], axis=1)
        cxp = np.concatenate([p[:, 1024:] for p in pT], axis=1)
        full = np.concatenate([cxp, lat], axis=1)
        kT = np.concatenate([full[512:640].reshape(2, 64, 8448), full[4880:5392].reshape(8, 64, 8448)], axis=0)

        def vt(rows, nh, dv):
            v = full[rows].T.reshape(66, 128, nh, dv)
            return _c(v.transpose(2, 1, 0, 3).reshape(nh, 128, 66 * dv))
        vv = vt(slice(640, 768), 2, 64)
        vd = vt(slice(5392, 5904), 4, 128)
        lam_init = 0.8 - 0.6 * math.exp(-0.3 * L)
        misc = np.zeros((128, 4), np.float32)
        misc[:, 0] = f(df_norm)[L]; misc[:, 1] = lam_init; misc[:, 2] = 1.0 - lam_init
        lamv = _c(np.broadcast_to(f(df_lambda)[L].reshape(1, 256), (128, 256)))
        maps = []
        for j in range(8):
            q = np.concatenate([pT[j][0:512].reshape(8, 64, 1056), pT[j][4368:4880].reshape(8, 64, 1056)], axis=0)
            maps.append({"qT": _c(q), "kT": _c(kT), "vv": vv, "vd": vd, "lamv": lamv, "misc": misc, "ones": ones})
        r = _run('k2', build_k2, maps)
        yaT = [r[j]["yaT"] for j in range(8)]
        ydT = [r[j]["ydT"] for j in range(8)]
        maps = []
        hs_ = f(hy_short)[L]
        for j in range(8):
            rows = [768 + g * 512 + 64 * j for g in range(3)]
            pl = np.stack([lat[r0:r0 + 64] for r0 in rows])
            pc = np.stack([cxp[r0:r0 + 64] for r0 in rows])
            sw = np.stack([hs_[:, g * 512 + 64 * j:g * 512 + 64 * j + 64].T for g in range(3)], axis=1).reshape(64, 9)
            vec = np.zeros((64, 8), np.float32)
            vec[:, 0] = f(hy_b1)[L]; vec[:, 1] = f(hy_b2)[L]; vec[:, 2] = f(hy_b3)[L]; vec[:, 3] = f(hy_freq)[L]
            vec[:, 4] = f(hy_skip)[L][64 * j:64 * j + 64]
            w4 = f(hy_w4)[L]
            w4s = np.concatenate([w4[:, 64 * j:64 * j + 64], w4[:, 512 + 64 * j:512 + 64 * j + 64]], axis=1)
            (zl, wl), (zc, wc) = hyc[j]
            maps.append({"pl": _c(pl), "pc": _c(pc), "sw": _c(sw), "w1": _c(f(hy_w1)[L]),
                         "w23": _c(np.concatenate([f(hy_w2)[L], f(hy_w3)[L]], axis=1)), "w4s": _c(w4s), "vec": vec,
                         "zl": zl, "zc": zc, "wl": wl, "wc": wc, "ident": ident})
        r = _run('k3', build_k3, maps)
        ybf = np.concatenate([r[j]["yb"] for j in range(8)], axis=0)
        yb_lat, yb_ctx = ybf[:, :8192], ybf[:, 8192:]
        maps = []
        ds_ = f(dn_short)[L]
        for j in range(8):
            h, d = j % 4, j // 4
            rows = [2304 + g * 512 + h * 128 for g in range(3)]
            seq = full
            if d == 1:
                seq = np.concatenate([cxp[:, ::-1], lat[:, ::-1]], axis=1)
            pq = np.stack([seq[r0:r0 + 128] for r0 in rows])
            braw = seq[4352 + d * 4 + h].reshape(132, 64).T
            araw = seq[4352 + 8 + d * 4 + h].reshape(132, 64).T
            taps = np.stack([ds_[:, g * 512 + h * 128:g * 512 + (h + 1) * 128].T for g in range(3)], axis=1)
            if d == 1:
                taps = taps[:, :, ::-1]
            scal = np.zeros((128, 2), np.float32)
            scal[:, 0] = f(dn_a_log)[L, d, h]; scal[:, 1] = f(dn_dt_bias)[L, d, h]
            maps.append({"pqkv": _c(pq), "taps": _c(taps.reshape(128, 9)), "graw": _c(np.concatenate([braw, araw], axis=1)),
                         "scal": scal, "cm": cm4})
        r = _run('k4', build_k4, maps)
        of_full = np.concatenate([r[j]["oT"] for j in range(4)], axis=0)
        ob_s = [r[4 + j]["oT"] for j in range(4)]
        ob_full = np.concatenate([np.concatenate([o[:, :256][:, ::-1], o[:, 256:][:, ::-1]], axis=1) for o in ob_s], axis=0)
        gate_lat, gate_ctx = lat[3840:4352], cxp[3840:4352]
        gains = _c(np.concatenate([f(norm_mix_post)[L].reshape(16, 128).T, f(norm_ffn_pre)[L].reshape(16, 128).T], axis=1))
        dng = _c(f(dn_norm)[L].reshape(128, 1))
        maps = []
        for j in range(8):
            sl, sc_ = slice(j * 1024, (j + 1) * 1024), slice(j * 32, (j + 1) * 32)
            ybj = np.concatenate([yb_lat[:, sl], yb_ctx[:, sc_]], axis=1)
            yT = np.concatenate([yaT[j], ybj, np.zeros((512, 1056), np.float32), ydT[j]], axis=0)
            ofj = np.concatenate([of_full[:, 256:][:, sl], of_full[:, :256][:, sc_]], axis=1)
            obj = np.concatenate([ob_full[:, 256:][:, sl], ob_full[:, :256][:, sc_]], axis=1)
            gtj = np.concatenate([gate_lat[:, sl], gate_ctx[:, sc_]], axis=1)
            maps.append({"yT": _c(yT), "xT": shard_T(x, ctxv, j), "w_out": _c(w_out[L]), "modT": modT, "gains": gains,
                         "ones": ones, "ofT": _c(ofj), "obT": _c(obj), "gtT": _c(gtj), "dng": dng})
        r = _run('k5a', build_k5a, maps)
        xm = [r[j]["xmT"] for j in range(8)]
        hh = [r[j]["hT"] for j in range(8)]
        h_lat = np.concatenate([a[:, :1024] for a in hh], axis=1).T
        h_ctx = np.concatenate([a[:, 1024:] for a in hh], axis=1).T
        z1 = np.zeros((1, 2048), np.float32)

        def hp(j):
            la = np.concatenate([h_lat[j * 1024 - 1:j * 1024] if j > 0 else z1, h_lat[j * 1024:(j + 1) * 1024],
                                 h_lat[(j + 1) * 1024:(j + 1) * 1024 + 1] if j < 7 else z1], axis=0)
            cx_ = np.concatenate([h_ctx[j * 32 - 1:j * 32] if j > 0 else z1, h_ctx[j * 32:(j + 1) * 32],
                                  h_ctx[(j + 1) * 32:(j + 1) * 32 + 1] if j < 7 else z1], axis=0)
            return _c(np.concatenate([la, cx_], axis=0).T)
        wcv = _c(ffn_conv[L].reshape(3, 88, 128).transpose(2, 1, 0).reshape(128, 264))
        g5 = _c(f(norm_ffn_post)[L].reshape(16, 128).T)
        maps = [{"hp": hp(j), "xmT": xm[j], "w_up": _c(ffn_up[L]), "wcv": wcv, "w_dn": _c(ffn_down[L]), "modT": modT,
                 "gains": g5, "ones": ones} for j in range(8)]
        r = _run('k5b', build_k5b, maps)
        xo = [r[j]["xoT"] for j in range(8)]
        x = np.ascontiguousarray(np.concatenate([a[:, :1024] for a in xo], axis=1).T)
        ctxv = np.ascontiguousarray(np.concatenate([a[:, 1024:] for a in xo], axis=1).T)
    return x[None].astype(np.float32)
```

```python
import math
import numpy as np
import contextlib
import concourse.bass as bass
import concourse.mybir as mybir
from concourse.bass_utils import run_bass_kernel_spmd

F32 = mybir.dt.float32
BF16 = mybir.dt.bfloat16
AF = mybir.ActivationFunctionType
ALU = mybir.AluOpType
AX = mybir.AxisListType


class View:
    def __init__(self, buf, ap):
        self.buf = buf
        self.ap = ap


class Buf:
    def __init__(self, t, name):
        self.t = t
        self.name = name
        self.wr = {}
        self.rd = {}

    def __getitem__(self, idx):
        return View(self, self.t[idx])

    def v(self, ap):
        return View(self, ap)

    def re(self, pat, **kw):
        return ReView(self, self.t.rearrange(pat, **kw))


class ReView:
    def __init__(self, buf, ap):
        self.buf = buf
        self.ap = ap

    def __getitem__(self, idx):
        return View(self.buf, self.ap[idx])


def _aps(x):
    return x.ap if isinstance(x, View) else x


class Prog:
    NDMASEM = 8

    def __init__(self, nc, stack):
        self.nc = nc
        self.stack = stack
        self.streams = ['pe', 'dve', 'act', 'pool', 'sp']
        self.sems = {}
        self.cnt = {}
        for k in ['pe', 'dve', 'act', 'pool']:
            self.sems[k] = stack.enter_context(nc.semaphore('s_' + k))
            self.cnt[k] = 0
        self.dq = {}
        for q in ['sp', 'act', 'pool']:
            keys = []
            for i in range(self.NDMASEM):
                k = 'd_%s_%d' % (q, i)
                self.sems[k] = stack.enter_context(nc.semaphore(k))
                self.cnt[k] = 0
                keys.append(k)
            self.dq[q] = [keys, 0]
        self.seen = {e: {} for e in self.streams}
        self.rec = {e: [] for e in self.streams}
        self.nbuf = 0
        self.dmarr = 0

    def sbuf(self, shape, dt, name=None):
        self.nbuf += 1
        name = name or ('sb%d' % self.nbuf)
        t = self.stack.enter_context(self.nc.sbuf_tensor(name, list(shape), dt))
        return Buf(t, name)

    def psum(self, shape, dt, name=None):
        self.nbuf += 1
        name = name or ('ps%d' % self.nbuf)
        t = self.stack.enter_context(self.nc.psum_tensor(name, list(shape), dt))
        return Buf(t, name)

    def dram(self, name, shape, dt, kind):
        t = self.nc.dram_tensor(name, list(shape), dt, kind=kind)
        return Buf(t.ap(), name)

    def _waits(self, e, reads, writes, extra=(), nowaw=False):
        need = {}

        def add(dep):
            if dep is None:
                return
            k, c = dep
            if need.get(k, 0) < c:
                need[k] = c
        raw_self = 0
        for b in reads:
            for k, c in b.wr.items():
                add((k, c))
                if k == e:
                    raw_self = max(raw_self, c)
        for b in writes:
            if not nowaw:
                for k, c in b.wr.items():
                    add((k, c))
            for k, c in b.rd.items():
                add((k, c))
        for d in extra:
            add(d)
        ws = []
        if e in need:
            del need[e]
        if raw_self > 0 and e != 'pe':
            need[e] = raw_self
        for k, c in need.items():
            if self.seen[e].get(k, 0) >= c:
                continue
            ws.append((self.sems[k], c))
            self.seen[e][k] = c
        return ws

    def _mark(self, k, c, reads, writes, nowaw):
        for b in reads:
            b.rd[k] = c
        for b in writes:
            if nowaw:
                b.wr[k] = c
            else:
                b.wr = {k: c}
                b.rd = {}

    def op(self, e, fn, reads=(), writes=(), nowaw=False):
        reads = [r.buf if isinstance(r, View) else r for r in reads if r is not None and not isinstance(r, (int, float))]
        writes = [w.buf if isinstance(w, View) else w for w in writes]
        ws = self._waits(e, reads, writes, nowaw=nowaw)
        self.cnt[e] += 1
        self.rec[e].append((ws, fn, self.sems[e], 1))
        self._mark(e, self.cnt[e], reads, writes, nowaw)

    def dma(self, out, in_, q=None, nowaw=False, **kw):
        if q is None:
            q = 'sp'
        reads = [in_.buf]
        writes = [out.buf]
        keys, idx = self.dq[q]
        k = keys[idx % len(keys)]
        self.dq[q][1] += 1
        prev = (k, self.cnt[k]) if self.cnt[k] > 0 else None
        ws = self._waits(q, reads, writes, extra=(prev,) if prev else (), nowaw=nowaw)
        self.cnt[k] += 16
        oa, ia = out.ap, in_.ap
        self.rec[q].append((ws, (lambda e: e.dma_start(out=oa, in_=ia, **kw)), self.sems[k], 16))
        self._mark(k, self.cnt[k], reads, writes, nowaw)

    def mm(self, out, lhsT, rhs, start=True, stop=True):
        o, l, r = out.ap, lhsT.ap, rhs.ap
        self.op('pe', lambda e: e.matmul(o, lhsT=l, rhs=r, start=start, stop=stop), reads=[lhsT, rhs], writes=[out])

    def transpose(self, out, in_, ident):
        o, i, d = out.ap, in_.ap, ident.ap
        self.op('pe', lambda e: e.transpose(o, i, d), reads=[in_, ident], writes=[out])

    def act(self, out, in_, func, scale=1.0, bias=None, eng='act', accum_out=None, nowaw=False):
        o, i = out.ap, in_.ap
        s = _aps(scale)
        b = _aps(bias)
        kw = {}
        if bias is not None:
            kw['bias'] = b
        if accum_out is not None:
            kw['accum_out'] = accum_out.ap
        wr = [out] + ([accum_out] if accum_out is not None else [])
        self.op('act', lambda e: e.activation(out=o, in_=i, func=func, scale=s, **kw),
                reads=[in_, scale if isinstance(scale, View) else None, bias if isinstance(bias, View) else None], writes=wr, nowaw=nowaw)

    def tt(self, out, in0, in1, op, eng='dve'):
        o, a, b = out.ap, in0.ap, in1.ap
        self.op(eng, lambda e: e.tensor_tensor(out=o, in0=a, in1=b, op=op), reads=[in0, in1], writes=[out])

    def ts(self, out, in0, s1, op0, s2=None, op1=None, eng='dve', accum_out=None, nowaw=False):
        o, a = out.ap, in0.ap
        x1, x2 = _aps(s1), _aps(s2)
        kw = {}
        if op1 is not None:
            kw['op1'] = op1
        if accum_out is not None:
            kw['accum_out'] = accum_out.ap
        wr = [out] + ([accum_out] if accum_out is not None else [])
        self.op(eng, lambda e: e.tensor_scalar(out=o, in0=a, scalar1=x1, scalar2=x2, op0=op0, **kw),
                reads=[in0, s1 if isinstance(s1, View) else None, s2 if isinstance(s2, View) else None], writes=wr, nowaw=nowaw)

    def stt(self, out, in0, scalar, in1, op0, op1):
        o, a, b = out.ap, in0.ap, in1.ap
        s = _aps(scalar)
        self.op('dve', lambda e: e.scalar_tensor_tensor(out=o, in0=a, scalar=s, in1=b, op0=op0, op1=op1),
                reads=[in0, in1, scalar if isinstance(scalar, View) else None], writes=[out])

    def copy(self, out, in_, eng='dve', nowaw=False):
        o, i = out.ap, in_.ap
        self.op(eng, lambda e: e.tensor_copy(out=o, in_=i), reads=[in_], writes=[out], nowaw=nowaw)

    def recip(self, out, in_):
        o, i = out.ap, in_.ap
        self.op('dve', lambda e: e.reciprocal(out=o, in_=i), reads=[in_], writes=[out])

    def memset(self, out, val, eng='dve'):
        o = out.ap
        self.op(eng, lambda e: e.memset(o, val), reads=[], writes=[out])

    def finish(self, bufs, e='sp'):
        ws = self._waits(e, bufs, [])
        self.rec[e].append((ws, None, None, 0))
        rec = self.rec

        def replay(lst):
            def f(eng):
                for ws, fn, sem, inc in lst:
                    for (s_, c_) in ws:
                        eng.wait_ge(s_, c_)
                    if fn is not None:
                        fn(eng).then_inc(sem, inc)
            return f
        with self.nc.Block() as block:
            if rec['sp']:
                block.sync(replay(rec['sp']))
            if rec['pe']:
                block.tensor(replay(rec['pe']))
            if rec['dve']:
                block.vector(replay(rec['dve']))
            if rec['act']:
                block.scalar(replay(rec['act']))
            if rec['pool']:
                block.gpsimd(replay(rec['pool']))


def new_prog():
    nc = bass.Bass("TRN2", target_bir_lowering=False)
    st = contextlib.ExitStack()
    return nc, st, Prog(nc, st)


D = 2048
NT = 1056
NL = 1024
IN_COLS = 5904
EPS = 1e-6


def build_k0():
    nc, st, P = new_prog()
    with st:
        cc = P.dram("cc", [128, 32], F32, "ExternalInput")
        w = P.dram("w", [2048, 3072], F32, "ExternalInput")
        b2 = P.dram("b2", [2, 3072], F32, "ExternalInput")
        mod = P.dram("mod", [2, 3072], F32, "ExternalOutput")
        cs = P.sbuf([128, 32], F32)
        ca = P.sbuf([128, 32], F32)
        bs = P.sbuf([2, 3072], F32)
        ms = P.sbuf([2, 3072], F32)
        wb = [P.sbuf([128, 3072], F32) for _ in range(4)]
        ps = [P.psum([128, 512], F32) for _ in range(6)]
        P.dma(cs[:], cc[:])
        P.dma(bs[:], b2[:])
        P.act(ca[:], cs[:], AF.Silu)
        for kc in range(16):
            wt = wb[kc % 4]
            P.dma(wt[:], w[kc * 128:(kc + 1) * 128, :])
            for g in range(6):
                P.mm(ps[g][0:2, :], ca[:, 2 * kc:2 * kc + 2], wt[:, g * 512:(g + 1) * 512], start=(kc == 0), stop=(kc == 15))
        for g in range(6):
            P.tt(ms[0:2, g * 512:(g + 1) * 512], ps[g][0:2, :], bs[0:2, g * 512:(g + 1) * 512], ALU.add)
        P.dma(mod[:], ms[:])
        P.finish([mod])
    return nc


def k1_groups():
    g = []
    for m in range(34):
        c0 = m * 128
        kind = None
        if m < 4:
            kind = 'gq_q'
        elif m == 4:
            kind = 'gq_k'
        g.append((c0, 128, kind))
    g.append((4352, 16, None))
    for m in range(12):
        c0 = 4368 + m * 128
        g.append((c0, 128, 'rope' if m < 8 else None))
    return g


def build_k1():
    nc, st, P = new_prog()
    with st:
        xT = P.dram("xT", [D, NT], F32, "ExternalInput")
        modT = P.dram("modT", [128, 96 * 2], F32, "ExternalInput")
        gain = P.dram("gain", [128, 16], F32, "ExternalInput")
        w_in = P.dram("w_in", [D, IN_COLS], F32, "ExternalInput")
        qkg = P.dram("qkg", [128, 2], F32, "ExternalInput")
        cosT = P.dram("cosT", [128, NL], F32, "ExternalInput")
        sinT = P.dram("sinT", [128, NL], F32, "ExternalInput")
        cmat = P.dram("cmat", [128, 3 * 128], F32, "ExternalInput")
        pT = P.dram("pT", [IN_COLS, NT], F32, "ExternalOutput")

        xs = P.sbuf([128, 16, NT], F32, "xs")
        hT = P.sbuf([128, 16, NT], BF16, "hT")
        mods = P.sbuf([128, 96, 2], F32, "mods")
        gs = P.sbuf([128, 16], F32, "gs")
        qk = P.sbuf([128, 2], F32, "qk")
        cs_ = P.sbuf([128, NL], F32, "cos")
        sn_ = P.sbuf([128, NL], F32, "sin")
        cm = P.sbuf([128, 384], F32, "cm")
        A = P.sbuf([128, 16, 2], F32, "A")
        rstd = P.sbuf([128, NT], F32, "rstd")
        tmp = [P.sbuf([128, NT], F32, "tmp%d" % i) for i in range(2)]
        banks = [P.psum([128, 512], F32, "bank%d" % i) for i in range(8)]

        xTr = xT.re("(kc k) n -> k kc n", k=128)
        for kc in range(16):
            P.dma(xs[:, kc, :], xTr[:, kc, :], nowaw=True)
        P.dma(mods[:], modT.re("k (c r) -> k c r", r=2)[:, :, :])
        P.dma(gs[:], gain[:])
        P.dma(qk[:], qkg[:])
        P.dma(cs_[:], cosT[:])
        P.dma(sn_[:], sinT[:])
        P.dma(cm[:], cmat[:])
        ones = cm[:, 0:128]
        bo = cm[:, 128:256]
        RT = cm[:, 256:384]

        for r in range(2):
            P.ts(A[:, :, r], mods[:, 16:32, r], 1.0, ALU.add)
            P.tt(A[:, :, r], A[:, :, r], gs[:], ALU.mult)
        for kc in range(16):
            sq = tmp[kc % 2]
            P.act(sq[:], xs[:, kc, :], AF.Square)
            for tg in range(3):
                P.mm(banks[tg][:, 0:352], ones, sq[:, tg * 352:(tg + 1) * 352], start=(kc == 0), stop=(kc == 15))
        for tg in range(3):
            P.act(rstd[:, tg * 352:(tg + 1) * 352], banks[tg][:, 0:352], AF.Sqrt, scale=1.0 / D, bias=EPSB(P))
        P.recip(rstd[:], rstd[:])
        for kc in range(16):
            t = tmp[kc % 2]
            P.tt(t[:], xs[:, kc, :], rstd[:], ALU.mult)
            P.act(hT[:, kc, 0:NL], t[:, 0:NL], AF.Identity, scale=A[:, kc, 0:1], bias=mods[:, kc, 0:1])
            P.ts(hT[:, kc, NL:NT], t[:, NL:NT], A[:, kc, 1:2], ALU.mult, mods[:, kc, 1:2], ALU.add)

        wst = [P.sbuf([128, 16, 256], F32, "wst%d" % i) for i in range(2)]
        wbf = [P.sbuf([128, 16, 256], BF16, "wbf%d" % i) for i in range(2)]
        pout = [P.sbuf([128, NT], F32, "pout%d" % i) for i in range(3)]
        w_r = w_in.re("(kc k) c -> k kc c", k=128)
        groups = k1_groups()
        slabs = []
        i = 0
        while i < len(groups):
            c0, n, _ = groups[i]
            if n == 128 and i + 1 < len(groups) and groups[i + 1][1] == 128 and groups[i + 1][0] == c0 + 128:
                slabs.append((c0, 256, [groups[i], groups[i + 1]]))
                i += 2
            else:
                slabs.append((c0, n, [groups[i]]))
                i += 1
        bi = 0
        gi = 0
        for si, (c0, wn, grs) in enumerate(slabs):
            ws_, wb_ = wst[si % 2], wbf[si % 2]
            for kh in range(2):
                P.dma(ws_[:, kh * 8:(kh + 1) * 8, 0:wn], w_r[:, kh * 8:(kh + 1) * 8, c0:c0 + wn], nowaw=True)
            P.copy(wb_[:, :, 0:wn], ws_[:, :, 0:wn], eng=('dve' if si % 2 == 0 else 'pool'))
            for (gc0, gn, kind) in grs:
                off = gc0 - c0
                po = pout[gi % 3]
                gi += 1
                for tg in range(3):
                    bk = banks[3 + (bi % 5)]
                    bi += 1
                    for kc in range(16):
                        P.mm(bk[0:gn, 0:352], wb_[:, kc, off:off + gn], hT[:, kc, tg * 352:(tg + 1) * 352], start=(kc == 0), stop=(kc == 15))
                    P.act(po[0:gn, tg * 352:(tg + 1) * 352], bk[0:gn, 0:352], AF.Copy)
                if kind in ('gq_q', 'gq_k'):
                    gcol = qk[:, 0:1] if kind == 'gq_q' else qk[:, 1:2]
                    sq = tmp[0]
                    P.act(sq[:], po[:], AF.Square)
                    r2 = tmp[1]
                    for tg in range(3):
                        P.mm(banks[tg][:, 0:352], bo, sq[:, tg * 352:(tg + 1) * 352])
                        P.act(r2[:, tg * 352:(tg + 1) * 352], banks[tg][:, 0:352], AF.Sqrt, scale=1.0 / 64, bias=EPSB(P))
                    P.recip(r2[:], r2[:])
                    P.stt(po[:], po[:], gcol, r2[:], ALU.mult, ALU.mult)
                if kind is not None:
                    t1 = tmp[0]
                    for hh in range(2):
                        P.mm(banks[hh][:, 0:512], RT, po[:, hh * 512:(hh + 1) * 512])
                    P.tt(t1[:, 0:NL], po[:, 0:NL], cs_[:], ALU.mult)
                    for hh in range(2):
                        P.tt(po[:, hh * 512:(hh + 1) * 512], banks[hh][:, 0:512], sn_[:, hh * 512:(hh + 1) * 512], ALU.mult)
                    P.tt(po[:, 0:NL], po[:, 0:NL], t1[:, 0:NL], ALU.add)
                P.dma(pT[gc0:gc0 + gn, :], po[0:gn, :])
        P.finish([pT])
    return nc


def EPSB(P):
    if not hasattr(P, '_epsb'):
        P._epsb = P.sbuf([128, 1], F32, "epsb")
        P.memset(P._epsb[:], EPS)
    return P._epsb[:]


def rope_tables():
    rows = 8192 // 64
    row = np.repeat(np.arange(rows, dtype=np.float32), 64)
    col = np.tile(np.arange(64, dtype=np.float32), rows)
    n_freq = 16
    inv = (np.float32(10000.0) ** (-np.arange(n_freq, dtype=np.float32) / n_freq)).astype(np.float32)
    ang = np.concatenate([row[:, None] * inv, col[:, None] * inv], axis=-1).astype(np.float32)
    return np.cos(ang).astype(np.float32), np.sin(ang).astype(np.float32)


def const_mats():
    ones = np.ones((128, 128), np.float32)
    bo = np.zeros((128, 128), np.float32)
    bo[:64, :64] = 1
    bo[64:, 64:] = 1
    RT = np.zeros((128, 128), np.float32)
    for m in range(128):
        if (m % 64) < 32:
            RT[m + 32, m] = -1.0
        else:
            RT[m - 32, m] = 1.0
    return np.concatenate([ones, bo, RT], axis=1)


NT = 1056
NL = 1024
NK = 8448
KT = 66


def build_k2():
    nc, st, P = new_prog()
    with st:
        qT = P.dram("qT", [16, 64, NT], F32, "ExternalInput")
        kT = P.dram("kT", [10, 64, NK], F32, "ExternalInput")
        vv = P.dram("vv", [2, 128, KT * 64], F32, "ExternalInput")
        vd = P.dram("vd", [4, 128, KT * 128], F32, "ExternalInput")
        lamv = P.dram("lamv", [128, 256], F32, "ExternalInput")
        misc = P.dram("misc", [128, 4], F32, "ExternalInput")
        onesd = P.dram("ones", [128, 128], F32, "ExternalInput")
        yaT = P.dram("yaT", [512, NT], F32, "ExternalOutput")
        ydT = P.dram("ydT", [512, NT], F32, "ExternalOutput")

        stage = [P.sbuf([128, 2112], F32, "stage%d" % i) for i in range(2)]
        kbf = [P.sbuf([64, NK], BF16, "kbf%d" % i) for i in range(2)]
        qbf = [P.sbuf([64, NT], BF16, "qbf%d" % i) for i in range(2)]
        vbf = [P.sbuf([128, KT * 128], BF16, "vbf%d" % i) for i in range(2)]
        pb = [P.sbuf([128, 1024], BF16, "pb%d" % i) for i in range(3)]
        ones_f = P.sbuf([128, 128], F32, "ones_f")
        ones_b = P.sbuf([128, 128], BF16, "ones_b")
        lv = P.sbuf([128, 256], F32, "lv")
        ms = P.sbuf([128, 4], F32, "ms")
        lam = P.sbuf([128, 4], F32, "lam")
        rd = P.sbuf([128, 512], F32, "rd")
        accA = P.sbuf([128, 512], F32, "accA")
        accB = P.sbuf([128, 512], F32, "accB")
        osb = [P.sbuf([128, NT], F32, "osb%d" % i) for i in range(3)]
        sq = P.sbuf([128, NT], F32, "sq")
        sbank = [P.psum([128, 1024], F32, "sbank%d" % i) for i in range(2)]
        obank = [P.psum([128, 512], F32, "obank%d" % i) for i in range(2)]
        dbank = [P.psum([128, 512], F32, "dbank%d" % i) for i in range(1)]
        nbank = P.psum([128, 512], F32, "nbank")

        P.dma(ones_f[:], onesd[:])
        P.copy(ones_b[:], ones_f[:])
        P.dma(lv[:], lamv[:])
        P.dma(ms[:], misc[:])
        pr = P.sbuf([128, 128], F32, "pr")
        P.tt(pr[:, 0:64], lv[:, 0:64], lv[:, 64:128], ALU.mult)
        P.tt(pr[:, 64:128], lv[:, 128:192], lv[:, 192:256], ALU.mult)
        o0, i0 = lam[:, 0:1].ap, pr[:, 0:64].ap
        P.op('dve', lambda e: e.tensor_reduce(out=o0, in_=i0, axis=AX.X, op=ALU.add), reads=[pr], writes=[lam])
        o1, i1 = lam[:, 1:2].ap, pr[:, 64:128].ap
        P.op('dve', lambda e: e.tensor_reduce(out=o1, in_=i1, axis=AX.X, op=ALU.add), reads=[pr], writes=[lam])
        P.act(lam[:, 0:2], lam[:, 0:2], AF.Exp)
        P.tt(lam[:, 2:3], lam[:, 0:1], lam[:, 1:2], ALU.subtract)
        P.tt(lam[:, 2:3], lam[:, 2:3], ms[:, 1:2], ALU.add)
        P.ts(lam[:, 3:4], lam[:, 2:3], -1.0, ALU.mult)
        gsc = P.sbuf([128, 2], F32, "gsc")
        P.tt(gsc[:, 0:1], ms[:, 0:1], ms[:, 2:3], ALU.mult)

        cnt = {'s': 0, 'p': 0, 'a': 0, 'st': 0, 'o': 0}

        def load_cast(dst_view_fn, src_view_fn, np_, ncols_total, piece=2112):
            c = 0
            while c < ncols_total:
                n = min(piece, ncols_total - c)
                sg = stage[cnt['st'] % 2]
                cnt['st'] += 1
                P.dma(sg[0:np_, 0:n], src_view_fn(c, n))
                P.copy(dst_view_fn(c, n), sg[0:np_, 0:n], eng='pool')
                c += n

        def attend(kb, qb, vb, dv, ob):
            for (q0, qn, nkt) in [(0, 512, KT), (512, 512, KT), (1024, 32, 2)]:
                a = cnt['a'] % 2
                cnt['a'] += 1
                ob_, db_ = obank[a], dbank[0]
                def issue_s(pp):
                    sb__ = sbank[cnt['s'] % 2]
                    cnt['s'] += 1
                    for hh_ in range(2):
                        kt_ = 2 * pp + hh_
                        P.mm(sb__[:, hh_ * 512:hh_ * 512 + qn], kb[:, kt_ * 128:(kt_ + 1) * 128], qb[:, q0:q0 + qn])
                    return sb__
                npair = nkt // 2
                nxt = issue_s(0)
                for pp in range(npair):
                    sb_ = nxt
                    if pp + 1 < npair:
                        nxt = issue_s(pp + 1)
                    pt = pb[cnt['p'] % 3]
                    cnt['p'] += 1
                    if qn == 512:
                        P.act(pt[:, 0:1024], sb_[:, 0:1024], AF.Exp, scale=0.125)
                    else:
                        for hh_ in range(2):
                            P.act(pt[:, hh_ * 512:hh_ * 512 + qn], sb_[:, hh_ * 512:hh_ * 512 + qn], AF.Exp, scale=0.125)
                    for hh_ in range(2):
                        kt = 2 * pp + hh_
                        pv = pt[:, hh_ * 512:hh_ * 512 + qn]
                        P.mm(ob_[0:dv, 0:qn], vb[:, kt * dv:(kt + 1) * dv], pv, start=(kt == 0), stop=(kt == nkt - 1))
                        if kt == 0:
                            P.copy(accA[:, 0:qn], pv)
                        elif kt == 1:
                            P.copy(accB[:, 0:qn], pv, eng='pool')
                        elif kt % 4 != 1:
                            P.tt(accA[:, 0:qn], accA[:, 0:qn], pv, ALU.add)
                        else:
                            P.tt(accB[:, 0:qn], accB[:, 0:qn], pv, ALU.add, eng='pool')
                P.tt(accA[:, 0:qn], accA[:, 0:qn], accB[:, 0:qn], ALU.add)
                P.mm(db_[0:dv, 0:qn], ones_f[:, 0:dv], accA[:, 0:qn])
                P.recip(rd[0:dv, 0:qn], db_[0:dv, 0:qn])
                P.tt(ob[0:dv, q0:q0 + qn], ob_[0:dv, 0:qn], rd[0:dv, 0:qn], ALU.mult)

        units = []

        def mk_gqa(g, hh):
            h = g * 4 + hh
            kb, vb, qb = kbf[g % 2], vbf[g % 2], qbf[h % 2]

            def load():
                if hh == 0:
                    load_cast(lambda c, n: kb[:, c:c + n], lambda c, n: kT[g, :, c:c + n], 64, NK)
                    load_cast(lambda c, n: vb[:, c:c + n], lambda c, n: vv[g, :, c:c + n], 128, KT * 64)
                load_cast(lambda c, n: qb[:, c:c + n], lambda c, n: qT[h, :, c:c + n], 64, NT)

            def comp():
                ob = osb[cnt['o'] % 3]
                cnt['o'] += 1
                attend(kb, qb, vb, 64, ob)
                P.dma(yaT[h * 64:(h + 1) * 64, :], ob[0:64, :])
            return load, comp

        dstate = {}

        def mk_diff(h, m):
            u = h * 2 + m
            vb, kb, qb = vbf[h % 2], kbf[u % 2], qbf[u % 2]

            def load():
                if m == 0:
                    load_cast(lambda c, n: vb[:, c:c + n], lambda c, n: vd[h, :, c:c + n], 128, KT * 128)
                load_cast(lambda c, n: kb[:, c:c + n], lambda c, n: kT[2 + u, :, c:c + n], 64, NK)
                load_cast(lambda c, n: qb[:, c:c + n], lambda c, n: qT[8 + u, :, c:c + n], 64, NT)

            def comp():
                ob = osb[cnt['o'] % 3]
                cnt['o'] += 1
                attend(kb, qb, vb, 128, ob)
                if m == 0:
                    dstate['o0'] = ob
                    return
                o0_, o1_ = dstate['o0'], ob
                P.stt(o0_[:], o1_[:], lam[:, 3:4], o0_[:], ALU.mult, ALU.add)
                P.act(sq[:], o0_[:], AF.Square)
                for tg in range(3):
                    P.mm(nbank[:, 0:352], ones_f[:], sq[:, tg * 352:(tg + 1) * 352])
                    P.act(o1_[:, tg * 352:(tg + 1) * 352], nbank[:, 0:352], AF.Sqrt, scale=1.0 / 128, bias=EPSB(P))
                P.recip(o1_[:], o1_[:])
                P.stt(o0_[:], o0_[:], gsc[:, 0:1], o1_[:], ALU.mult, ALU.mult)
                P.dma(ydT[h * 128:(h + 1) * 128, :], o0_[:])
            return load, comp

        for g in range(2):
            for hh in range(4):
                units.append(mk_gqa(g, hh))
        for h in range(4):
            for m in range(2):
                units.append(mk_diff(h, m))
        units[0][0]()
        for ui in range(len(units)):
            if ui + 1 < len(units):
                units[ui + 1][0]()
            units[ui][1]()
        P.finish([yaT, ydT])
    return nc


LL = 8192
LC = 256


def hy_consts(L, core):
    nb = L // 128
    cb = np.arange(2 * nb)[:, None]
    jp = np.arange(128)[None, :]
    b = cb - nb
    n_f = 128 * b + 127 - jp
    n_b = 128 * (-b) - 127 + jp
    n = np.where(b >= 0, n_f, n_b)
    valid = (n >= 0) & (n < L)
    n = np.where(valid, n, 0).astype(np.float32)
    t = (n / np.float32(L - 1)).astype(np.float32)
    w = (np.float32(2.0 * math.pi) * n / np.float32(L)).astype(np.float32)
    f = np.linspace(1e-4, 15, 16, dtype=np.float32)
    z = np.concatenate([t[..., None], np.cos(f * w[..., None]), -np.sin(f * w[..., None])], axis=-1).astype(np.float32)
    zT = np.ascontiguousarray(z.reshape(2 * nb * 128, 33).T)
    min_decay = math.log(1e-2) / 1.5
    max_decay = math.log(1e-2) / 0.3
    deltas = np.linspace(min_decay, max_decay, 512, dtype=np.float32)[core * 64:(core + 1) * 64]
    win = np.exp(-t[..., None] * np.abs(deltas)).astype(np.float32) * valid[..., None]
    win = np.ascontiguousarray(win.transpose(1, 0, 2).reshape(128, 2 * nb * 64)).astype(np.float32)
    return zT, win


def build_k3():
    nc, st, P = new_prog()
    with st:
        pl = P.dram("pl", [3, 64, LL], F32, "ExternalInput")
        pc = P.dram("pc", [3, 64, LC], F32, "ExternalInput")
        swd = P.dram("sw", [64, 9], F32, "ExternalInput")
        w1d = P.dram("w1", [33, 64], F32, "ExternalInput")
        w23d = P.dram("w23", [64, 128], F32, "ExternalInput")
        w4d = P.dram("w4s", [64, 128], F32, "ExternalInput")
        vecd = P.dram("vec", [64, 8], F32, "ExternalInput")
        zld = P.dram("zl", [33, 2 * 64 * 128], F32, "ExternalInput")
        zcd = P.dram("zc", [33, 2 * 2 * 128], F32, "ExternalInput")
        wld = P.dram("wl", [128, 128 * 64], F32, "ExternalInput")
        wcd = P.dram("wc", [128, 4 * 64], F32, "ExternalInput")
        identd = P.dram("ident", [128, 128], F32, "ExternalInput")
        yb = P.dram("yb", [64, LL + LC], F32, "ExternalOutput")
        upl_h = nc.dram_tensor("upl", [64, LL + 256], BF16, kind="Internal")
        upc_h = nc.dram_tensor("upc", [64, LC + 256], BF16, kind="Internal")
        upl = Buf(upl_h.ap(), "upl")
        upc = Buf(upc_h.ap(), "upc")

        sw = P.sbuf([64, 9], F32, "sw_s")
        w1 = P.sbuf([33, 64], F32, "w1_s")
        w23 = P.sbuf([64, 128], F32, "w23_s")
        w4 = P.sbuf([64, 128], F32, "w4_s")
        vec = P.sbuf([64, 8], F32, "vec_s")
        sc = P.sbuf([64, 4], F32, "sc_s")
        ident = P.sbuf([128, 128], F32, "ident_s")
        zero = P.sbuf([64, 136], BF16, "zero_s")
        for d, s in [(sw, swd), (w1, w1d), (w23, w23d), (w4, w4d), (vec, vecd), (ident, identd)]:
            P.dma(d[:], s[:])
        P.memset(zero[:], 0.0)
        P.ts(sc[:, 0:1], vec[:, 3:4], 1.0 / 3.0, ALU.mult)
        for k in range(3):
            P.tt(sc[:, 1 + k:2 + k], vec[:, k:k + 1], sc[:, 0:1], ALU.mult)

        PIECE = 2048
        pg = P.sbuf([64, 3, PIECE + 2], F32, "pg")
        cg = P.sbuf([64, 3, PIECE], F32, "cg")
        ub = P.sbuf([64, PIECE], BF16, "ub")
        Hm = P.sbuf([128, 64, 128], BF16, "Hm")
        ush = [P.sbuf([128, 65 * 128], BF16, "ush%d" % i) for i in range(2)]
        ytok = P.sbuf([128, 64, 64], F32, "ytok")
        yconv = P.sbuf([64, LL], F32, "yconv")
        zt = [P.sbuf([33, 512], F32, "zt%d" % i) for i in range(2)]
        hs = [P.sbuf([64, 512], F32, "hs%d" % i) for i in range(3)]
        s2 = P.sbuf([64, 512], F32, "s2")
        wp = [P.sbuf([128, 4, 64], F32, "wp%d" % i) for i in range(2)]
        fb = [P.psum([128, 512], F32, "fb%d" % i) for i in range(3)]
        hb = [P.psum([128, 512], F32, "hb%d" % i) for i in range(2)]
        yk = [P.psum([128, 512], F32, "yk%d" % i) for i in range(2)]
        tb = P.psum([128, 512], F32, "tb")

        def conv_piece(src, L, q, piece):
            lo = q * piece - 1
            hi = (q + 1) * piece + 1
            clo, chi = max(lo, 0), min(hi, L)
            if clo != lo or chi != hi:
                P.memset(pg[:], 0.0)
            for g in range(3):
                P.dma(pg[:, g, clo - lo:chi - lo], src[g, :, clo:chi])
            for g in range(3):
                P.ts(cg[:, g, 0:piece], pg[:, g, 1:1 + piece], sw[:, g * 3 + 1:g * 3 + 2], ALU.mult)
                P.stt(cg[:, g, 0:piece], pg[:, g, 0:piece], sw[:, g * 3:g * 3 + 1], cg[:, g, 0:piece], ALU.mult, ALU.add)
                P.stt(cg[:, g, 0:piece], pg[:, g, 2:2 + piece], sw[:, g * 3 + 2:g * 3 + 3], cg[:, g, 0:piece], ALU.mult, ALU.add)
            P.tt(cg[:, 1, 0:piece], cg[:, 1, 0:piece], cg[:, 2, 0:piece], ALU.mult)

        def sin3(out, ps, k):
            P.act(out[:], ps[0:64, :], AF.Sin, scale=sc[:, 0:1], bias=sc[:, 1 + k:2 + k])
            P.tt(s2[:], out[:], out[:], ALU.mult)
            P.ts(s2[:], s2[:], -4.0, ALU.mult, 3.0, ALU.add)
            P.tt(out[:], s2[:], out[:], ALU.mult)

        def run_seq(src, L, up, up_h, zd, wd, out0):
            nb = L // 128
            piece = min(L, PIECE)
            npieces = L // piece
            W = L + 256
            P.dma(up[:, 0:127], zero[:, 0:127])
            P.dma(up[:, 127 + L:W], zero[:, 0:129])
            for q in range(npieces):
                conv_piece(src, L, q, piece)
                P.copy(ub[:, 0:piece], cg[:, 1, 0:piece])
                P.dma(up[:, 127 + q * piece:127 + (q + 1) * piece], ub[:, 0:piece])
            ntile = (2 * nb * 128) // 512
            for ti in range(ntile):
                z_ = zt[ti % 2]
                P.dma(z_[:], zd[:, ti * 512:(ti + 1) * 512])
                P.mm(fb[0][0:64, :], w1[:, :], z_[:, :])
                sin3(hs[0], fb[0], 0)
                P.mm(fb[1][0:64, :], w23[:, 0:64], hs[0][:, :])
                sin3(hs[1], fb[1], 1)
                P.mm(fb[2][0:64, :], w23[:, 64:128], hs[1][:, :])
                sin3(hs[2], fb[2], 2)
                hbk = hb[ti % 2]
                for k in range(4):
                    cbi = ti * 4 + k
                    wcol = 0 if cbi >= nb else 64
                    P.mm(hbk[:, k * 64:(k + 1) * 64], hs[2][:, k * 128:(k + 1) * 128], w4[:, wcol:wcol + 64])
                wp_ = wp[ti % 2]
                P.dma(wp_[:], wd.re("p (b c) -> p b c", c=64)[:, ti * 4:(ti + 1) * 4, :])
                hv = View(hbk, hbk.t[:, 0:256].rearrange("p (k c) -> p k c", k=4))
                ov = View(Hm, Hm.t[:, :, ti * 4:(ti + 1) * 4].rearrange("p c k -> p k c"))
                P.tt(ov, hv, wp_[:], ALU.mult)
            ncol = (nb + 1) * 128
            for c in range(64):
                us = ush[c % 2]
                src_ap = bass.AP(up_h, c * W, [[1, 128], [1, ncol]])
                P.dma(us[:, 0:ncol], View(up, src_ap))
                ykb = yk[(c // 8) % 2]
                c8 = c % 8
                for m in range(nb + 1):
                    P.mm(ykb[:, c8 * nb:(c8 + 1) * nb], us[:, m * 128:(m + 1) * 128], Hm[:, c, nb - m:2 * nb - m],
                         start=(m == 0), stop=(m == nb))
                if c8 == 7:
                    c0 = c - 7
                    iv = View(ykb, ykb.t[:, 0:8 * nb].rearrange("p (c a) -> p c a", c=8))
                    ov = View(ytok, ytok.t[:, 0:nb, c0:c0 + 8].rearrange("p a c -> p c a"))
                    P.act(ov, iv, AF.Copy)
            for a in range(nb):
                k = a % 4
                P.transpose(tb[0:64, k * 128:(k + 1) * 128], ytok[:, a, :], ident[:])
                if k == 3 or a == nb - 1:
                    a0 = a - k
                    P.act(yconv[:, a0 * 128:(a + 1) * 128], tb[0:64, 0:(k + 1) * 128], AF.Copy)
            for q in range(npieces):
                conv_piece(src, L, q, piece)
                P.stt(cg[:, 1, 0:piece], cg[:, 1, 0:piece], vec[:, 4:5], yconv[:, q * piece:(q + 1) * piece], ALU.mult, ALU.add)
                P.tt(cg[:, 0, 0:piece], cg[:, 0, 0:piece], cg[:, 1, 0:piece], ALU.mult)
                P.dma(yb[:, out0 + q * piece:out0 + (q + 1) * piece], cg[:, 0, 0:piece])

        run_seq(pl, LL, upl, upl_h, zld, wld, 0)
        run_seq(pc, LC, upc, upc_h, zcd, wcd, LL)
        P.finish([yb])
    return nc


NS = 8448
NCH = 132


def dn_consts():
    U = np.triu(np.ones((64, 64), np.float32))
    Ls = np.tril(np.ones((64, 64), np.float32), -1)
    cm = np.zeros((128, 512), np.float32)
    cm[:64, 0:64] = U
    cm[:64, 64:128] = Ls
    cm[:, 128:256] = np.eye(128, dtype=np.float32)
    cm[:, 256:384] = 1.0
    return cm


def build_k4():
    nc, st, P = new_prog()
    with st:
        pqkv = P.dram("pqkv", [3, 128, NS], F32, "ExternalInput")
        tapsd = P.dram("taps", [128, 9], F32, "ExternalInput")
        grawd = P.dram("graw", [64, 2 * NCH], F32, "ExternalInput")
        scd = P.dram("scal", [128, 2], F32, "ExternalInput")
        cmd = P.dram("cm", [128, 512], F32, "ExternalInput")
        oT = P.dram("oT", [128, NS], F32, "ExternalOutput")

        cm = P.sbuf([128, 512], F32, "cm_s")
        taps = P.sbuf([128, 9], F32, "taps_s")
        graw = P.sbuf([64, 2 * NCH], F32, "graw_s")
        scl = P.sbuf([128, 2], F32, "scl_s")
        for d, s in [(cm, cmd), (taps, tapsd), (graw, grawd), (scl, scd)]:
            P.dma(d[:], s[:])
        U = cm[0:64, 0:64]
        Ls = cm[0:64, 64:128]
        I64 = cm[0:64, 128:192]
        I128 = cm[:, 128:256]
        ones = cm[:, 256:384]
        ones64 = cm[0:64, 256:384]
        eps = P.sbuf([128, 1], F32, "eps_s")
        P.memset(eps[:], 1e-6)

        qkv = [P.sbuf([128, NS], F32, "qkv%d" % g) for g in range(3)]
        PIECE = 2048
        pg = P.sbuf([128, PIECE + 2], F32, "pg")
        cgt = P.sbuf([128, PIECE], F32, "cgt")
        sg = P.sbuf([128, PIECE], F32, "sg")
        bank = [P.psum([128, 512], F32, "bk%d" % i) for i in range(8)]
        bctr = [0]

        def nb():
            b = bank[bctr[0] % 8]
            bctr[0] += 1
            return b

        segs = [(0, 256)] + [(256 + i * PIECE, PIECE) for i in range(4)]
        for g in range(3):
            for (s0, n) in segs:
                first = (s0 == 0 or s0 == 256)
                last = (s0 + n == 256 or s0 + n == NS)
                lo = s0 - (0 if first else 1)
                hi = s0 + n + (0 if last else 1)
                if first or last:
                    P.memset(pg[:], 0.0)
                P.dma(pg[:, 1 - (s0 - lo):1 + n + (hi - s0 - n)], pqkv[g, :, lo:hi])
                P.ts(cgt[:, 0:n], pg[:, 1:1 + n], taps[:, g * 3 + 1:g * 3 + 2], ALU.mult)
                P.stt(cgt[:, 0:n], pg[:, 0:n], taps[:, g * 3:g * 3 + 1], cgt[:, 0:n], ALU.mult, ALU.add)
                P.stt(cgt[:, 0:n], pg[:, 2:2 + n], taps[:, g * 3 + 2:g * 3 + 3], cgt[:, 0:n], ALU.mult, ALU.add)
                P.act(qkv[g][:, s0:s0 + n], cgt[:, 0:n], AF.Silu)
                if g < 2:
                    P.act(sg[:, 0:n], qkv[g][:, s0:s0 + n], AF.Square)
                    for c0 in range(0, n, 512):
                        w = min(512, n - c0)
                        b = nb()
                        P.mm(b[:, 0:w], ones, sg[:, c0:c0 + w])
                        P.act(sg[:, c0:c0 + w], b[:, 0:w], AF.Sqrt, bias=eps[:])
                    P.recip(sg[:, 0:n], sg[:, 0:n])
                    if g == 0:
                        P.stt(qkv[g][:, s0:s0 + n], qkv[g][:, s0:s0 + n], 128.0 ** -0.5, sg[:, 0:n], ALU.mult, ALU.mult)
                    else:
                        P.tt(qkv[g][:, s0:s0 + n], qkv[g][:, s0:s0 + n], sg[:, 0:n], ALU.mult)
        qT, kT, vT = qkv

        beta = P.sbuf([64, NCH], F32, "beta")
        nbeta = P.sbuf([64, NCH], F32, "nbeta")
        gg = P.sbuf([64, NCH], F32, "gg")
        ea = P.sbuf([128, 1], F32, "ea")
        P.act(beta[:], graw[:, 0:NCH], AF.Sigmoid)
        P.ts(nbeta[:], beta[:], -1.0, ALU.mult)
        P.act(gg[:], graw[:, NCH:2 * NCH], AF.Exp, bias=scl[0:64, 1:2])
        P.act(gg[:], gg[:], AF.Ln, bias=cm[0:64, 256:257])
        P.act(ea[:], scl[:, 0:1], AF.Exp)
        P.ts(gg[:], gg[:], ea[0:64, 0:1], ALU.mult, -1.0, ALU.mult)
        gc = P.sbuf([64, NCH], F32, "gc")
        egc = P.sbuf([64, NCH], F32, "egc")
        bke = P.sbuf([64, NCH], F32, "bke")
        kde = P.sbuf([64, NCH], F32, "kde")
        lastB = P.sbuf([128, NCH], F32, "lastB")
        b = nb()
        P.mm(b[0:64, 0:NCH], U, gg[:, :])
        P.act(gc[:], b[0:64, 0:NCH], AF.Copy)
        P.act(egc[:], gc[:], AF.Exp)
        P.tt(bke[:], beta[:], egc[:], ALU.mult)
        b = nb()
        P.mm(b[:, 0:NCH], ones64, gg[:, :])
        P.act(lastB[:], b[:, 0:NCH], AF.Exp)
        P.tt(kde[:], b[0:64, 0:NCH], gc[:], ALU.subtract)
        P.act(kde[:], kde[:], AF.Exp)

        S = [P.sbuf([128, 128], F32, "S%d" % i) for i in range(2)]
        P.memset(S[0][:], 0.0)
        oacc = P.sbuf([128, NS], F32, "oacc")

        def T(shape, name, n=2):
            return [P.sbuf(shape, F32, "%s%d" % (name, i)) for i in range(n)]
        gB = T([64, 128], "gB"); tt_ = T([128, 64], "tt"); a1 = T([64, 64], "a1"); gs = T([64, 64], "gs")
        gT_ = T([64, 64], "gT"); egr = T([128, 64], "egr")
        XX = [T([64, 64], "Xa%d" % i, 4) for i in range(2)]
        ZZ = [T([64, 64], "Za%d" % i, 4) for i in range(2)]
        WW = [T([64, 64], "Wa%d" % i, 4) for i in range(2)]
        vb = T([64, 128], "vb", 4); kbd = T([64, 128], "kbd", 4); kd = T([64, 128], "kd", 4)
        u = T([64, 128], "u", 4); wT = T([128, 64], "wT", 4); qd = T([128, 64], "qd", 4); qk = T([64, 64], "qk", 4)
        vn = T([64, 128], "vn", 4)
        def pre(c):
            cols = slice(64 * c, 64 * c + 64)
            i2 = c % 2
            i4 = c % 4
            X, Z, W = XX[i2], ZZ[i2], WW[i2]
            P.act(gB[i2][:], ones64, AF.Identity, scale=gg[:, c:c + 1])
            yield
            b1 = nb()
            P.mm(b1[:, 0:64], gB[i2][:], U)
            P.act(tt_[i2][:], b1[:, 0:64], AF.Copy)
            P.act(egr[i2][:], tt_[i2][:], AF.Exp)
            P.ts(a1[i2][:], tt_[i2][0:64, :], gc[:, c:c + 1], ALU.subtract, 0.0, ALU.min)
            P.act(gT_[i2][:], a1[i2][:], AF.Exp)
            yield
            P.tt(gT_[i2][:], gT_[i2][:], U, ALU.mult)
            P.ts(a1[i2][:], tt_[i2][0:64, :], gc[:, c:c + 1], ALU.subtract, -1.0, ALU.mult)
            P.ts(a1[i2][:], a1[i2][:], 0.0, ALU.min)
            yield
            P.act(gs[i2][:], a1[i2][:], AF.Exp)
            yield
            P.tt(gs[i2][:], gs[i2][:], Ls, ALU.mult)
            b2 = nb()
            P.mm(b2[0:64, 0:64], kT[:, cols], kT[:, cols])
            x0 = X[0]
            P.stt(x0[:], b2[0:64, 0:64], nbeta[:, c:c + 1], gs[i2][:], ALU.mult, ALU.mult)
            b3 = nb()
            P.transpose(b3[0:64, 0:64], x0[:], I64)
            yield
            z0 = Z[0]
            P.act(z0[:], b3[0:64, 0:64], AF.Copy)
            yield
            w0 = W[0]
            P.tt(w0[:], z0[:], I64, ALU.add)
            yield
            xk, zk, wk = x0, z0, w0
            for lev in range(5):
                xn, zn, wn = X[(lev + 1) % 4], Z[(lev + 1) % 4], W[(lev + 1) % 4]
                bx = nb()
                P.mm(bx[0:64, 0:64], zk[:], xk[:])
                yield
                P.act(xn[:], bx[0:64, 0:64], AF.Copy)
                yield
                if lev < 4:
                    bz = nb()
                    P.mm(bz[0:64, 0:64], xk[:], zk[:])
                    yield
                    P.copy(zn[:], bz[0:64, 0:64])
                    yield
                bw = nb()
                P.mm(bw[0:64, 0:64], xn[:], wk[:])
                yield
                P.tt(wn[:], bw[0:64, 0:64], wk[:], ALU.add)
                yield
                xk, zk, wk = xn, zn, wn
            Wf = wk
            bv = nb()
            P.transpose(bv[0:64, 0:128], vT[:, cols], I128)
            yield
            P.ts(vb[i4][:], bv[0:64, 0:128], beta[:, c:c + 1], ALU.mult)
            yield
            bk_ = nb()
            P.transpose(bk_[0:64, 0:128], kT[:, cols], I128)
            yield
            P.ts(kbd[i4][:], bk_[0:64, 0:128], bke[:, c:c + 1], ALU.mult)
            yield
            P.ts(kd[i4][:], bk_[0:64, 0:128], kde[:, c:c + 1], ALU.mult)
            yield
            bu = nb()
            P.mm(bu[0:64, 0:128], Wf[:], vb[i4][:])
            yield
            P.act(u[i4][:], bu[0:64, 0:128], AF.Copy)
            yield
            bwt = nb()
            P.mm(bwt[:, 0:64], kbd[i4][:], Wf[:])
            yield
            P.act(wT[i4][:], bwt[:, 0:64], AF.Copy)
            yield
            P.tt(qd[i4][:], qT[:, cols], egr[i2][:], ALU.mult)
            yield
            bq = nb()
            P.mm(bq[0:64, 0:64], kT[:, cols], qT[:, cols])
            yield
            P.tt(qk[i4][:], bq[0:64, 0:64], gT_[i2][:], ALU.mult)
            yield

        def scan(c):
            cols = slice(64 * c, 64 * c + 64)
            i4 = c % 4
            Sc, Sn = S[c % 2], S[(c + 1) % 2]
            bs = nb()
            P.mm(bs[0:64, 0:128], wT[i4][:], Sc[:])
            P.tt(vn[i4][:], u[i4][:], bs[0:64, 0:128], ALU.subtract)
            bo = nb()
            P.mm(bo[:, 0:64], Sc[:], qd[i4][:], start=True, stop=False)
            P.mm(bo[:, 0:64], vn[i4][:], qk[i4][:], start=False, stop=True)
            P.act(oacc[:, cols], bo[:, 0:64], AF.Copy)
            bn = nb()
            P.mm(bn[:, 0:128], kd[i4][:], vn[i4][:])
            P.stt(Sn[:], Sc[:], lastB[:, c:c + 1], bn[:, 0:128], ALU.mult, ALU.add)

        def run_pair(gens):
            live = list(gens)
            while live:
                for g_ in list(live):
                    try:
                        next(g_)
                    except StopIteration:
                        live.remove(g_)
        for cp in range(0, NCH, 2):
            run_pair([pre(cp), pre(cp + 1)])
            scan(cp)
            scan(cp + 1)
        for i in range(4):
            P.dma(oT[:, i * 2112:(i + 1) * 2112], oacc[:, i * 2112:(i + 1) * 2112])
        P.finish([oT])
    return nc


D = 2048
NT = 1056
NL = 1024
DFF = 5632


def sumsq_rstd(P, src, nchunks, n, ones, bank, sqtmp, rstd, dim):
    for kc in range(nchunks):
        sq = sqtmp[kc % 2]
        P.act(sq[:, 0:n], src[:, kc, 0:n], AF.Square)
        P.mm(bank[:, 0:n], ones, sq[:, 0:n], start=(kc == 0), stop=(kc == nchunks - 1))
    P.act(rstd[:, 0:n], bank[:, 0:n], AF.Sqrt, scale=1.0 / dim, bias=EPSB(P))
    P.recip(rstd[:, 0:n], rstd[:, 0:n])


def build_k5a():
    nc, st, P = new_prog()
    with st:
        yT = P.dram("yT", [D, NT], F32, "ExternalInput")
        xT = P.dram("xT", [D, NT], F32, "ExternalInput")
        w_out = P.dram("w_out", [D, D], F32, "ExternalInput")
        modT = P.dram("modT", [128, 192], F32, "ExternalInput")
        gains = P.dram("gains", [128, 32], F32, "ExternalInput")
        onesd = P.dram("ones", [128, 128], F32, "ExternalInput")
        ofT = P.dram("ofT", [512, NT], F32, "ExternalInput")
        obT = P.dram("obT", [512, NT], F32, "ExternalInput")
        gtT = P.dram("gtT", [512, NT], F32, "ExternalInput")
        dngd = P.dram("dng", [128, 1], F32, "ExternalInput")
        xmT = P.dram("xmT", [D, NT], F32, "ExternalOutput")
        hT = P.dram("hT", [D, NT], F32, "ExternalOutput")
        dng = P.sbuf([128, 1], F32, "dng_s")
        P.dma(dng[:], dngd[:])
        dn_o = P.sbuf([128, 352], F32, "dn_o")
        dn_b = P.sbuf([128, 352], F32, "dn_b")
        dn_g = P.sbuf([128, 352], F32, "dn_g")

        wob = P.sbuf([128, 16, D], BF16, "wob")
        wst = [P.sbuf([128, 16, 256], F32, "wst%d" % i) for i in range(2)]
        mods = P.sbuf([128, 96, 2], F32, "mods")
        gs = P.sbuf([128, 32], F32, "gs")
        ones = P.sbuf([128, 128], F32, "ones_s")
        G1 = P.sbuf([128, 16, 2], F32, "G1")
        A2 = P.sbuf([128, 16, 2], F32, "A2")
        yb = P.sbuf([128, 16, 352], BF16, "yb")
        ystage = [P.sbuf([128, 352], F32, "ystage%d" % i) for i in range(2)]
        z = P.sbuf([128, 16, 352], F32, "z")
        xg = P.sbuf([128, 16, 352], F32, "xg")
        sqt = [P.sbuf([128, 352], F32, "sqt%d" % i) for i in range(2)]
        tmp = [P.sbuf([128, 352], F32, "tmp%d" % i) for i in range(2)]
        hout = [P.sbuf([128, 352], F32, "hout%d" % i) for i in range(2)]
        rstd = P.sbuf([128, 352], F32, "rstd")
        banks = [P.psum([128, 512], F32, "bank%d" % i) for i in range(6)]
        nbank = P.psum([128, 512], F32, "nbank")

        P.dma(mods[:], modT.re("k (c r) -> k c r", r=2)[:, :, :])
        P.dma(gs[:], gains[:])
        P.dma(ones[:], onesd[:])
        w_r = w_out.re("(kc k) c -> k kc c", k=128)
        for s in range(8):
            ws_ = wst[s % 2]
            for kh in range(2):
                P.dma(ws_[:, kh * 8:(kh + 1) * 8, :], w_r[:, kh * 8:(kh + 1) * 8, s * 256:(s + 1) * 256], nowaw=True)
            P.copy(wob[:, :, s * 256:(s + 1) * 256], ws_[:], eng=('dve' if s % 2 == 0 else 'pool'))
        for r in range(2):
            P.tt(G1[:, :, r], mods[:, 32:48, r], gs[:, 0:16], ALU.mult)
            P.ts(A2[:, :, r], mods[:, 64:80, r], 1.0, ALU.add)
            P.tt(A2[:, :, r], A2[:, :, r], gs[:, 16:32], ALU.mult)
        yT_r = yT.re("(kc k) n -> k kc n", k=128)
        xT_r = xT.re("(kc k) n -> k kc n", k=128)
        xm_r = xmT.re("(kc k) n -> k kc n", k=128)
        hT_r = hT.re("(kc k) n -> k kc n", k=128)
        bi = 0
        for tg in range(3):
            c0 = tg * 352
            nl = 352 if tg < 2 else 320
            rngs = [(0, nl, 0)] + ([(nl, 352, 1)] if nl < 352 else [])
            for kc in range(16):
                ys_ = ystage[kc % 2]
                if 8 <= kc < 12:
                    hh = kc - 8
                    P.dma(dn_o[:], ofT[hh * 128:(hh + 1) * 128, c0:c0 + 352])
                    P.dma(dn_b[:], obT[hh * 128:(hh + 1) * 128, c0:c0 + 352])
                    P.dma(dn_g[:], gtT[hh * 128:(hh + 1) * 128, c0:c0 + 352])
                    P.tt(dn_o[:], dn_o[:], dn_b[:], ALU.add)
                    P.act(dn_b[:], dn_o[:], AF.Square)
                    P.mm(nbank[:, 0:352], ones[:], dn_b[:])
                    P.act(dn_b[:], nbank[:, 0:352], AF.Sqrt, scale=1.0 / 128, bias=EPSB(P))
                    P.recip(dn_b[:], dn_b[:])
                    P.stt(dn_o[:], dn_o[:], dng[:, 0:1], dn_b[:], ALU.mult, ALU.mult)
                    P.act(dn_g[:], dn_g[:], AF.Silu)
                    P.tt(ys_[:], dn_o[:], dn_g[:], ALU.mult)
                else:
                    P.dma(ys_[:], yT_r[:, kc, c0:c0 + 352])
                P.copy(yb[:, kc, :], ys_[:], eng=('dve' if kc % 2 == 0 else 'pool'))
            P.dma(xg[:, 0:8, :], xT_r[:, 0:8, c0:c0 + 352])
            P.dma(xg[:, 8:16, :], xT_r[:, 8:16, c0:c0 + 352])
            for m in range(16):
                bk = banks[bi % 6]
                bi += 1
                for kc in range(16):
                    P.mm(bk[:, 0:352], wob[:, kc, m * 128:(m + 1) * 128], yb[:, kc, :], start=(kc == 0), stop=(kc == 15))
                P.act(z[:, m, :], bk[:, 0:352], AF.Copy)
            sumsq_rstd(P, z, 16, 352, ones[:], nbank, sqt, rstd, D)
            for kc in range(16):
                t = tmp[kc % 2]
                P.tt(t[:], z[:, kc, :], rstd[:], ALU.mult)
                for (a, b, r) in rngs:
                    P.stt(xg[:, kc, a:b], t[:, a:b], G1[:, kc, r:r + 1], xg[:, kc, a:b], ALU.mult, ALU.add)
            sumsq_rstd(P, xg, 16, 352, ones[:], nbank, sqt, rstd, D)
            for kc in range(16):
                t = tmp[kc % 2]
                ho = hout[kc % 2]
                P.tt(t[:], xg[:, kc, :], rstd[:], ALU.mult)
                for (a, b, r) in rngs:
                    P.ts(ho[:, a:b], t[:, a:b], A2[:, kc, r:r + 1], ALU.mult, mods[:, 48 + kc, r:r + 1], ALU.add)
                P.dma(hT_r[:, kc, c0:c0 + 352], ho[:])
            P.dma(xm_r[:, 0:8, c0:c0 + 352], xg[:, 0:8, :])
            P.dma(xm_r[:, 8:16, c0:c0 + 352], xg[:, 8:16, :])
        P.finish([xmT, hT])
    return nc


NP5 = 1060


def build_k5b():
    nc, st, P = new_prog()
    with st:
        hp = P.dram("hp", [D, NP5], F32, "ExternalInput")
        xmT = P.dram("xmT", [D, NT], F32, "ExternalInput")
        w_up = P.dram("w_up", [D, 2 * DFF], F32, "ExternalInput")
        wcv = P.dram("wcv", [128, 88 * 3], F32, "ExternalInput")
        w_dn = P.dram("w_dn", [DFF, D], F32, "ExternalInput")
        modT = P.dram("modT", [128, 192], F32, "ExternalInput")
        gains = P.dram("gains", [128, 16], F32, "ExternalInput")
        onesd = P.dram("ones", [128, 128], F32, "ExternalInput")
        xoT = P.dram("xoT", [D, NT], F32, "ExternalOutput")

        stf = [P.sbuf([128, 5632], F32, "stf%d" % i) for i in range(2)]
        stb = [P.sbuf([128, 5632], BF16, "stb%d" % i) for i in range(2)]
        mods = P.sbuf([128, 96, 2], F32, "mods")
        gs = P.sbuf([128, 16], F32, "gs")
        wc = P.sbuf([128, 88, 3], F32, "wc")
        ones = P.sbuf([128, 128], F32, "ones_s")
        G2 = P.sbuf([128, 16, 2], F32, "G2")
        hg = P.sbuf([128, 16, 376], BF16, "hg")
        hst = [P.sbuf([128, 376], F32, "hst%d" % i) for i in range(2)]
        gT = P.sbuf([128, 44, 372], BF16, "gT")
        dn = P.sbuf([128, 16, 372], F32, "dn")
        xg = P.sbuf([128, 16, 372], F32, "xg")
        ca = [P.sbuf([128, 372], F32, "ca%d" % i) for i in range(2)]
        cb = [P.sbuf([128, 372], F32, "cb%d" % i) for i in range(2)]
        sqt = [P.sbuf([128, 372], F32, "sqt%d" % i) for i in range(2)]
        tmp = [P.sbuf([128, 372], F32, "tmp%d" % i) for i in range(2)]
        rstd = P.sbuf([128, 372], F32, "rstd")
        banks = [P.psum([128, 512], F32, "bank%d" % i) for i in range(6)]
        nbank = P.psum([128, 512], F32, "nbank")

        P.dma(mods[:], modT.re("k (c r) -> k c r", r=2)[:, :, :])
        P.dma(gs[:], gains[:])
        P.dma(ones[:], onesd[:])
        P.dma(wc[:], wcv.re("k (c t) -> k c t", t=3)[:, :, :])
        for r in range(2):
            P.tt(G2[:, :, r], mods[:, 80:96, r], gs[:], ALU.mult)
        hp_r = hp.re("(kc k) n -> k kc n", k=128)
        xm_r = xmT.re("(kc k) n -> k kc n", k=128)
        xo_r = xoT.re("(kc k) n -> k kc n", k=128)
        wu_r = w_up.re("(kc k) c -> k kc c", k=128)
        wd_r = w_dn.re("(f k) c -> k f c", k=128)
        groups = [(0, 344, [(0, 342)], 0, 342, 342),
                  (342, 344, [(0, 342)], 342, 342, 342),
                  (684, 376, [(0, 340), (342, 32)], 684, 372, 340)]
        si = 0
        bi = 0
        for (u0, un, segs, xc0, gn, nl) in groups:
            for kc in range(16):
                hs_ = hst[kc % 2]
                P.dma(hs_[:, 0:un], hp_r[:, kc, u0:u0 + un])
                P.copy(hg[:, kc, 0:un], hs_[:, 0:un], eng=('dve' if kc % 2 == 0 else 'pool'))
            P.dma(xg[:, 0:8, 0:gn], xm_r[:, 0:8, xc0:xc0 + gn])
            P.dma(xg[:, 8:16, 0:gn], xm_r[:, 8:16, xc0:xc0 + gn])
            for f in range(44):
                sf, sb_ = stf[si % 2], stb[si % 2]
                si += 1
                sfv = sf.v(sf.t[:, 0:4096].rearrange("p (a b c) -> p a b c", a=16, b=2))
                sbv = sb_.v(sb_.t[:, 0:4096].rearrange("p (a b c) -> p a b c", a=16, b=2))
                P.dma(View(sf, sfv.ap[:, :, 0, :]), wu_r[:, :, f * 128:(f + 1) * 128], nowaw=True)
                P.dma(View(sf, sfv.ap[:, :, 1, :]), wu_r[:, :, DFF + f * 128:DFF + (f + 1) * 128], nowaw=True)
                P.copy(sb_[:, 0:4096], sf[:, 0:4096], eng='pool')
                bka = banks[bi % 6]
                bkb = banks[(bi + 1) % 6]
                bi += 2
                for kc in range(16):
                    P.mm(bka[:, 0:un], View(sb_, sbv.ap[:, kc, 0, :]), hg[:, kc, 0:un], start=(kc == 0), stop=(kc == 15))
                for kc in range(16):
                    P.mm(bkb[:, 0:un], View(sb_, sbv.ap[:, kc, 1, :]), hg[:, kc, 0:un], start=(kc == 0), stop=(kc == 15))
                ca_, cb_ = ca[f % 2], cb[f % 2]
                goff = 0
                for (lo, n) in segs:
                    for (cc_, bk, ch) in [(ca_, bka, f), (cb_, bkb, 44 + f)]:
                        P.ts(cc_[:, goff:goff + n], bk[:, lo + 1:lo + 1 + n], wc[:, ch, 1:2], ALU.mult)
                        P.stt(cc_[:, goff:goff + n], bk[:, lo:lo + n], wc[:, ch, 0:1], cc_[:, goff:goff + n], ALU.mult, ALU.add)
                        P.stt(cc_[:, goff:goff + n], bk[:, lo + 2:lo + 2 + n], wc[:, ch, 2:3], cc_[:, goff:goff + n], ALU.mult, ALU.add)
                    goff += n
                P.act(ca_[:, 0:gn], ca_[:, 0:gn], AF.Silu)
                P.tt(gT[:, f, 0:gn], ca_[:, 0:gn], cb_[:, 0:gn], ALU.mult)
            for m in range(16):
                sf, sb_ = stf[si % 2], stb[si % 2]
                si += 1
                sfv = sf.v(sf.t[:, :].rearrange("p (f c) -> p f c", f=44))
                sbv = sb_.v(sb_.t[:, :].rearrange("p (f c) -> p f c", f=44))
                P.dma(View(sf, sfv.ap[:, 0:22, :]), wd_r[:, 0:22, m * 128:(m + 1) * 128], nowaw=True)
                P.dma(View(sf, sfv.ap[:, 22:44, :]), wd_r[:, 22:44, m * 128:(m + 1) * 128], nowaw=True)
                P.copy(sb_[:], sf[:], eng='pool')
                bk = banks[bi % 6]
                bi += 1
                for f in range(44):
                    P.mm(bk[:, 0:gn], View(sb_, sbv.ap[:, f, :]), gT[:, f, 0:gn], start=(f == 0), stop=(f == 43))
                P.act(dn[:, m, 0:gn], bk[:, 0:gn], AF.Copy)
            sumsq_rstd(P, dn, 16, gn, ones[:], nbank, sqt, rstd, D)
            rngs = [(0, nl, 0)] + ([(nl, gn, 1)] if nl < gn else [])
            for kc in range(16):
                t = tmp[kc % 2]
                P.tt(t[:, 0:gn], dn[:, kc, 0:gn], rstd[:, 0:gn], ALU.mult)
                for (a, b, r) in rngs:
                    P.stt(xg[:, kc, a:b], t[:, a:b], G2[:, kc, r:r + 1], xg[:, kc, a:b], ALU.mult, ALU.add)
            P.dma(xo_r[:, 0:8, xc0:xc0 + gn], xg[:, 0:8, 0:gn])
            P.dma(xo_r[:, 8:16, xc0:xc0 + gn], xg[:, 8:16, 0:gn])
        P.finish([xoT])
    return nc


_PROGS = {}


def _prog(name, fn):
    if name not in _PROGS:
        _PROGS[name] = fn()
    return _PROGS[name]


def _run(name, fn, maps):
    nc = fn()
    res = run_bass_kernel_spmd(nc, maps, core_ids=list(range(8)))
    return res.results


def _c(a):
    return np.ascontiguousarray(a, dtype=np.float32)


def kernel(x, c, ctx, c_ctx, w_ada, b_ada, norm_mix_pre, norm_mix_post, norm_ffn_pre,
           norm_ffn_post, w_in, w_out, attn_q_norm, attn_k_norm, hy_short, hy_w1, hy_b1,
           hy_w2, hy_b2, hy_w3, hy_b3, hy_w4, hy_freq, hy_skip, dn_short, dn_a_log,
           dn_dt_bias, dn_norm, df_lambda, df_norm, ffn_up, ffn_conv, ffn_down):
    f = lambda a: np.asarray(a, dtype=np.float32)
    x = f(x)[0]; ctxv = f(ctx)[0]; c = f(c); c_ctx = f(c_ctx)
    w_ada = f(w_ada); b_ada = f(b_ada); w_in = f(w_in); w_out = f(w_out)
    ffn_up = f(ffn_up); ffn_conv = f(ffn_conv); ffn_down = f(ffn_down)
    ones = np.ones((128, 128), np.float32)
    ident = np.eye(128, dtype=np.float32)
    cc = np.stack([c[0], c_ctx], axis=-1).reshape(16, 128, 2).transpose(1, 0, 2).reshape(128, 32)
    maps = []
    for j in range(8):
        l, q = j // 4, j % 4
        maps.append({"cc": _c(cc), "w": _c(w_ada[l][:, q * 3072:(q + 1) * 3072]),
                     "b2": _c(np.broadcast_to(b_ada[l][None, q * 3072:(q + 1) * 3072], (2, 3072)))})
    r = _run('k0', build_k0, maps)
    mod = np.zeros((2, 2, 12288), np.float32)
    for j in range(8):
        l, q = j // 4, j % 4
        mod[l][:, q * 3072:(q + 1) * 3072] = r[j]["mod"]
    cos, sin = rope_tables()
    cm1 = const_mats()
    cm4 = dn_consts()
    hyc = [(hy_consts(LL, j), hy_consts(LC, j)) for j in range(8)]

    def shard_T(lat, cx, j):
        return _c(np.concatenate([lat[j * 1024:(j + 1) * 1024], cx[j * 32:(j + 1) * 32]], axis=0).T)

    for L in range(2):
        modT = _c(mod[L].reshape(2, 96, 128).transpose(2, 1, 0).reshape(128, 192))
        gain = _c(f(norm_mix_pre)[L].reshape(16, 128).T)
        qkg = _c(np.stack([np.tile(f(attn_q_norm)[L], 2), np.tile(f(attn_k_norm)[L], 2)], axis=1))
        maps = []
        for j in range(8):
            cj = np.tile(cos[j * 1024:(j + 1) * 1024].T, (4, 1))
            sj = np.tile(sin[j * 1024:(j + 1) * 1024].T, (4, 1))
            maps.append({"xT": shard_T(x, ctxv, j), "modT": modT, "gain": gain, "w_in": _c(w_in[L]), "qkg": qkg,
                         "cosT": _c(cj), "sinT": _c(sj), "cmat": cm1})
        r = _run('k1', build_k1, maps)
        pT = [r[j]["pT"] for j in range(8)]
        lat = np.concatenate([p[:, :1024] for p in pT], axis=1)
        cxp = np.concatenate([p[:, 1024:] for p in pT], axis=1)
        full = np.concatenate([cxp, lat], axis=1)
        kT = np.concatenate([full[512:640].reshape(2, 64, 8448), full[4880:5392].reshape(8, 64, 8448)], axis=0)

        def vt(rows, nh, dv):
            v = full[rows].T.reshape(66, 128, nh, dv)
            return _c(v.transpose(2, 1, 0, 3).reshape(nh, 128, 66 * dv))
        vv = vt(slice(640, 768), 2, 64)
        vd = vt(slice(5392, 5904), 4, 128)
        lam_init = 0.8 - 0.6 * math.exp(-0.3 * L)
        misc = np.zeros((128, 4), np.float32)
        misc[:, 0] = f(df_norm)[L]; misc[:, 1] = lam_init; misc[:, 2] = 1.0 - lam_init
        lamv = _c(np.broadcast_to(f(df_lambda)[L].reshape(1, 256), (128, 256)))
        maps = []
        for j in range(8):
            q = np.concatenate([pT[j][0:512].reshape(8, 64, 1056), pT[j][4368:4880].reshape(8, 64, 1056)], axis=0)
            maps.append({"qT": _c(q), "kT": _c(kT), "vv": vv, "vd": vd, "lamv": lamv, "misc": misc, "ones": ones})
        r = _run('k2', build_k2, maps)
        yaT = [r[j]["yaT"] for j in range(8)]
        ydT = [r[j]["ydT"] for j in range(8)]
        maps = []
        hs_ = f(hy_short)[L]
        for j in range(8):
            rows = [768 + g * 512 + 64 * j for g in range(3)]
            pl = np.stack([lat[r0:r0 + 64] for r0 in rows])
            pc = np.stack([cxp[r0:r0 + 64] for r0 in rows])
            sw = np.stack([hs_[:, g * 512 + 64 * j:g * 512 + 64 * j + 64].T for g in range(3)], axis=1).reshape(64, 9)
            vec = np.zeros((64, 8), np.float32)
            vec[:, 0] = f(hy_b1)[L]; vec[:, 1] = f(hy_b2)[L]; vec[:, 2] = f(hy_b3)[L]; vec[:, 3] = f(hy_freq)[L]
            vec[:, 4] = f(hy_skip)[L][64 * j:64 * j + 64]
            w4 = f(hy_w4)[L]
            w4s = np.concatenate([w4[:, 64 * j:64 * j + 64], w4[:, 512 + 64 * j:512 + 64 * j + 64]], axis=1)
            (zl, wl), (zc, wc) = hyc[j]
            maps.append({"pl": _c(pl), "pc": _c(pc), "sw": _c(sw), "w1": _c(f(hy_w1)[L]),
                         "w23": _c(np.concatenate([f(hy_w2)[L], f(hy_w3)[L]], axis=1)), "w4s": _c(w4s), "vec": vec,
                         "zl": zl, "zc": zc, "wl": wl, "wc": wc, "ident": ident})
        r = _run('k3', build_k3, maps)
        ybf = np.concatenate([r[j]["yb"] for j in range(8)], axis=0)
        yb_lat, yb_ctx = ybf[:, :8192], ybf[:, 8192:]
        maps = []
        ds_ = f(dn_short)[L]
        for j in range(8):
            h, d = j % 4, j // 4
            rows = [2304 + g * 512 + h * 128 for g in range(3)]
            seq = full
            if d == 1:
                seq = np.concatenate([cxp[:, ::-1], lat[:, ::-1]], axis=1)
            pq = np.stack([seq[r0:r0 + 128] for r0 in rows])
            braw = seq[4352 + d * 4 + h].reshape(132, 64).T
            araw = seq[4352 + 8 + d * 4 + h].reshape(132, 64).T
            taps = np.stack([ds_[:, g * 512 + h * 128:g * 512 + (h + 1) * 128].T for g in range(3)], axis=1)
            if d == 1:
                taps = taps[:, :, ::-1]
            scal = np.zeros((128, 2), np.float32)
            scal[:, 0] = f(dn_a_log)[L, d, h]; scal[:, 1] = f(dn_dt_bias)[L, d, h]
            maps.append({"pqkv": _c(pq), "taps": _c(taps.reshape(128, 9)), "graw": _c(np.concatenate([braw, araw], axis=1)),
                         "scal": scal, "cm": cm4})
        r = _run('k4', build_k4, maps)
        of_full = np.concatenate([r[j]["oT"] for j in range(4)], axis=0)
        ob_s = [r[4 + j]["oT"] for j in range(4)]
        ob_full = np.concatenate([np.concatenate([o[:, :256][:, ::-1], o[:, 256:][:, ::-1]], axis=1) for o in ob_s], axis=0)
        gate_lat, gate_ctx = lat[3840:4352], cxp[3840:4352]
        gains = _c(np.concatenate([f(norm_mix_post)[L].reshape(16, 128).T, f(norm_ffn_pre)[L].reshape(16, 128).T], axis=1))
        dng = _c(f(dn_norm)[L].reshape(128, 1))
        maps = []
        for j in range(8):
            sl, sc_ = slice(j * 1024, (j + 1) * 1024), slice(j * 32, (j + 1) * 32)
            ybj = np.concatenate([yb_lat[:, sl], yb_ctx[:, sc_]], axis=1)
            yT = np.concatenate([yaT[j], ybj, np.zeros((512, 1056), np.float32), ydT[j]], axis=0)
            ofj = np.concatenate([of_full[:, 256:][:, sl], of_full[:, :256][:, sc_]], axis=1)
            obj = np.concatenate([ob_full[:, 256:][:, sl], ob_full[:, :256][:, sc_]], axis=1)
            gtj = np.concatenate([gate_lat[:, sl], gate_ctx[:, sc_]], axis=1)
            maps.append({"yT": _c(yT), "xT": shard_T(x, ctxv, j), "w_out": _c(w_out[L]), "modT": modT, "gains": gains,
                         "ones": ones, "ofT": _c(ofj), "obT": _c(obj), "gtT": _c(gtj), "dng": dng})
        r = _run('k5a', build_k5a, maps)
        xm = [r[j]["xmT"] for j in range(8)]
        hh = [r[j]["hT"] for j in range(8)]
        h_lat = np.concatenate([a[:, :1024] for a in hh], axis=1).T
        h_ctx = np.concatenate([a[:, 1024:] for a in hh], axis=1).T
        z1 = np.zeros((1, 2048), np.float32)

        def hp(j):
            la = np.concatenate([h_lat[j * 1024 - 1:j * 1024] if j > 0 else z1, h_lat[j * 1024:(j + 1) * 1024],
                                 h_lat[(j + 1) * 1024:(j + 1) * 1024 + 1] if j < 7 else z1], axis=0)
            cx_ = np.concatenate([h_ctx[j * 32 - 1:j * 32] if j > 0 else z1, h_ctx[j * 32:(j + 1) * 32],
                                  h_ctx[(j + 1) * 32:(j + 1) * 32 + 1] if j < 7 else z1], axis=0)
            return _c(np.concatenate([la, cx_], axis=0).T)
        wcv = _c(ffn_conv[L].reshape(3, 88, 128).transpose(2, 1, 0).reshape(128, 264))
        g5 = _c(f(norm_ffn_post)[L].reshape(16, 128).T)
        maps = [{"hp": hp(j), "xmT": xm[j], "w_up": _c(ffn_up[L]), "wcv": wcv, "w_dn": _c(ffn_down[L]), "modT": modT,
                 "gains": g5, "ones": ones} for j in range(8)]
        r = _run('k5b', build_k5b, maps)
        xo = [r[j]["xoT"] for j in range(8)]
        x = np.ascontiguousarray(np.concatenate([a[:, :1024] for a in xo], axis=1).T)
        ctxv = np.ascontiguousarray(np.concatenate([a[:, 1024:] for a in xo], axis=1).T)
    return x[None].astype(np.float32)
```

```python
import math
import numpy as np
import contextlib
import concourse.bass as bass
import concourse.mybir as mybir
from concourse.bass_utils import run_bass_kernel_spmd

F32 = mybir.dt.float32
BF16 = mybir.dt.bfloat16
AF = mybir.ActivationFunctionType
ALU = mybir.AluOpType
AX = mybir.AxisListType


class View:
    def __init__(self, buf, ap):
        self.buf = buf
        self.ap = ap


class Buf:
    def __init__(self, t, name):
        self.t = t
        self.name = name
        self.wr = {}
        self.rd = {}

    def __getitem__(self, idx):
        return View(self, self.t[idx])

    def v(self, ap):
        return View(self, ap)

    def re(self, pat, **kw):
        return ReView(self, self.t.rearrange(pat, **kw))


class ReView:
    def __init__(self, buf, ap):
        self.buf = buf
        self.ap = ap

    def __getitem__(self, idx):
        return View(self.buf, self.ap[idx])


def _aps(x):
    return x.ap if isinstance(x, View) else x


class Prog:
    NDMASEM = 8

    def __init__(self, nc, stack):
        self.nc = nc
        self.stack = stack
        self.streams = ['pe', 'dve', 'act', 'pool', 'sp']
        self.sems = {}
        self.cnt = {}
        for k in ['pe', 'dve', 'act', 'pool']:
            self.sems[k] = stack.enter_context(nc.semaphore('s_' + k))
            self.cnt[k] = 0
        self.dq = {}
        for q in ['sp', 'act', 'pool']:
            keys = []
            for i in range(self.NDMASEM):
                k = 'd_%s_%d' % (q, i)
                self.sems[k] = stack.enter_context(nc.semaphore(k))
                self.cnt[k] = 0
                keys.append(k)
            self.dq[q] = [keys, 0]
        self.seen = {e: {} for e in self.streams}
        self.rec = {e: [] for e in self.streams}
        self.nbuf = 0
        self.dmarr = 0

    def sbuf(self, shape, dt, name=None):
        self.nbuf += 1
        name = name or ('sb%d' % self.nbuf)
        t = self.stack.enter_context(self.nc.sbuf_tensor(name, list(shape), dt))
        return Buf(t, name)

    def psum(self, shape, dt, name=None):
        self.nbuf += 1
        name = name or ('ps%d' % self.nbuf)
        t = self.stack.enter_context(self.nc.psum_tensor(name, list(shape), dt))
        return Buf(t, name)

    def dram(self, name, shape, dt, kind):
        t = self.nc.dram_tensor(name, list(shape), dt, kind=kind)
        return Buf(t.ap(), name)

    def _waits(self, e, reads, writes, extra=(), nowaw=False):
        need = {}

        def add(dep):
            if dep is None:
                return
            k, c = dep
            if need.get(k, 0) < c:
                need[k] = c
        raw_self = 0
        for b in reads:
            for k, c in b.wr.items():
                add((k, c))
                if k == e:
                    raw_self = max(raw_self, c)
        for b in writes:
            if not nowaw:
                for k, c in b.wr.items():
                    add((k, c))
            for k, c in b.rd.items():
                add((k, c))
        for d in extra:
            add(d)
        ws = []
        if e in need:
            del need[e]
        if raw_self > 0 and e != 'pe':
            need[e] = raw_self
        for k, c in need.items():
            if self.seen[e].get(k, 0) >= c:
                continue
            ws.append((self.sems[k], c))
            self.seen[e][k] = c
        return ws

    def _mark(self, k, c, reads, writes, nowaw):
        for b in reads:
            b.rd[k] = c
        for b in writes:
            if nowaw:
                b.wr[k] = c
            else:
                b.wr = {k: c}
                b.rd = {}

    def op(self, e, fn, reads=(), writes=(), nowaw=False):
        reads = [r.buf if isinstance(r, View) else r for r in reads if r is not None and not isinstance(r, (int, float))]
        writes = [w.buf if isinstance(w, View) else w for w in writes]
        ws = self._waits(e, reads, writes, nowaw=nowaw)
        self.cnt[e] += 1
        self.rec[e].append((ws, fn, self.sems[e], 1))
        self._mark(e, self.cnt[e], reads, writes, nowaw)

    def dma(self, out, in_, q=None, nowaw=False, **kw):
        if q is None:
            q = 'sp'
        reads = [in_.buf]
        writes = [out.buf]
        keys, idx = self.dq[q]
        k = keys[idx % len(keys)]
        self.dq[q][1] += 1
        prev = (k, self.cnt[k]) if self.cnt[k] > 0 else None
        ws = self._waits(q, reads, writes, extra=(prev,) if prev else (), nowaw=nowaw)
        self.cnt[k] += 16
        oa, ia = out.ap, in_.ap
        self.rec[q].append((ws, (lambda e: e.dma_start(out=oa, in_=ia, **kw)), self.sems[k], 16))
        self._mark(k, self.cnt[k], reads, writes, nowaw)

    def mm(self, out, lhsT, rhs, start=True, stop=True):
        o, l, r = out.ap, lhsT.ap, rhs.ap
        self.op('pe', lambda e: e.matmul(o, lhsT=l, rhs=r, start=start, stop=stop), reads=[lhsT, rhs], writes=[out])

    def transpose(self, out, in_, ident):
        o, i, d = out.ap, in_.ap, ident.ap
        self.op('pe', lambda e: e.transpose(o, i, d), reads=[in_, ident], writes=[out])

    def act(self, out, in_, func, scale=1.0, bias=None, eng='act', accum_out=None, nowaw=False):
        o, i = out.ap, in_.ap
        s = _aps(scale)
        b = _aps(bias)
        kw = {}
        if bias is not None:
            kw['bias'] = b
        if accum_out is not None:
            kw['accum_out'] = accum_out.ap
        wr = [out] + ([accum_out] if accum_out is not None else [])
        self.op('act', lambda e: e.activation(out=o, in_=i, func=func, scale=s, **kw),
                reads=[in_, scale if isinstance(scale, View) else None, bias if isinstance(bias, View) else None], writes=wr, nowaw=nowaw)

    def tt(self, out, in0, in1, op, eng='dve'):
        o, a, b = out.ap, in0.ap, in1.ap
        self.op(eng, lambda e: e.tensor_tensor(out=o, in0=a, in1=b, op=op), reads=[in0, in1], writes=[out])

    def ts(self, out, in0, s1, op0, s2=None, op1=None, eng='dve', accum_out=None, nowaw=False):
        o, a = out.ap, in0.ap
        x1, x2 = _aps(s1), _aps(s2)
        kw = {}
        if op1 is not None:
            kw['op1'] = op1
        if accum_out is not None:
            kw['accum_out'] = accum_out.ap
        wr = [out] + ([accum_out] if accum_out is not None else [])
        self.op(eng, lambda e: e.tensor_scalar(out=o, in0=a, scalar1=x1, scalar2=x2, op0=op0, **kw),
                reads=[in0, s1 if isinstance(s1, View) else None, s2 if isinstance(s2, View) else None], writes=wr, nowaw=nowaw)

    def stt(self, out, in0, scalar, in1, op0, op1):
        o, a, b = out.ap, in0.ap, in1.ap
        s = _aps(scalar)
        self.op('dve', lambda e: e.scalar_tensor_tensor(out=o, in0=a, scalar=s, in1=b, op0=op0, op1=op1),
                reads=[in0, in1, scalar if isinstance(scalar, View) else None], writes=[out])

    def copy(self, out, in_, eng='dve', nowaw=False):
        o, i = out.ap, in_.ap
        self.op(eng, lambda e: e.tensor_copy(out=o, in_=i), reads=[in_], writes=[out], nowaw=nowaw)

    def recip(self, out, in_):
        o, i = out.ap, in_.ap
        self.op('dve', lambda e: e.reciprocal(out=o, in_=i), reads=[in_], writes=[out])

    def memset(self, out, val, eng='dve'):
        o = out.ap
        self.op(eng, lambda e: e.memset(o, val), reads=[], writes=[out])

    def finish(self, bufs, e='sp'):
        ws = self._waits(e, bufs, [])
        self.rec[e].append((ws, None, None, 0))
        rec = self.rec

        def replay(lst):
            def f(eng):
                for ws, fn, sem, inc in lst:
                    for (s_, c_) in ws:
                        eng.wait_ge(s_, c_)
                    if fn is not None:
                        fn(eng).then_inc(sem, inc)
            return f
        with self.nc.Block() as block:
            if rec['sp']:
                block.sync(replay(rec['sp']))
            if rec['pe']:
                block.tensor(replay(rec['pe']))
            if rec['dve']:
                block.vector(replay(rec['dve']))
            if rec['act']:
                block.scalar(replay(rec['act']))
            if rec['pool']:
                block.gpsimd(replay(rec['pool']))


def new_prog():
    nc = bass.Bass("TRN2", target_bir_lowering=False)
    st = contextlib.ExitStack()
    return nc, st, Prog(nc, st)


D = 2048
NT = 1056
NL = 1024
IN_COLS = 5904
EPS = 1e-6


def build_k0():
    nc, st, P = new_prog()
    with st:
        cc = P.dram("cc", [128, 32], F32, "ExternalInput")
        w = P.dram("w", [2048, 3072], F32, "ExternalInput")
        b2 = P.dram("b2", [2, 3072], F32, "ExternalInput")
        mod = P.dram("mod", [2, 3072], F32, "ExternalOutput")
        cs = P.sbuf([128, 32], F32)
        ca = P.sbuf([128, 32], F32)
        bs = P.sbuf([2, 3072], F32)
        ms = P.sbuf([2, 3072], F32)
        wb = [P.sbuf([128, 3072], F32) for _ in range(4)]
        ps = [P.psum([128, 512], F32) for _ in range(6)]
        P.dma(cs[:], cc[:])
        P.dma(bs[:], b2[:])
        P.act(ca[:], cs[:], AF.Silu)
        for kc in range(16):
            wt = wb[kc % 4]
            P.dma(wt[:], w[kc * 128:(kc + 1) * 128, :])
            for g in range(6):
                P.mm(ps[g][0:2, :], ca[:, 2 * kc:2 * kc + 2], wt[:, g * 512:(g + 1) * 512], start=(kc == 0), stop=(kc == 15))
        for g in range(6):
            P.tt(ms[0:2, g * 512:(g + 1) * 512], ps[g][0:2, :], bs[0:2, g * 512:(g + 1) * 512], ALU.add)
        P.dma(mod[:], ms[:])
        P.finish([mod])
    return nc


def k1_groups():
    g = []
    for m in range(34):
        c0 = m * 128
        kind = None
        if m < 4:
            kind = 'gq_q'
        elif m == 4:
            kind = 'gq_k'
        g.append((c0, 128, kind))
    g.append((4352, 16, None))
    for m in range(12):
        c0 = 4368 + m * 128
        g.append((c0, 128, 'rope' if m < 8 else None))
    return g


def build_k1():
    nc, st, P = new_prog()
    with st:
        xT = P.dram("xT", [D, NT], F32, "ExternalInput")
        modT = P.dram("modT", [128, 96 * 2], F32, "ExternalInput")
        gain = P.dram("gain", [128, 16], F32, "ExternalInput")
        w_in = P.dram("w_in", [D, IN_COLS], F32, "ExternalInput")
        qkg = P.dram("qkg", [128, 2], F32, "ExternalInput")
        cosT = P.dram("cosT", [128, NL], F32, "ExternalInput")
        sinT = P.dram("sinT", [128, NL], F32, "ExternalInput")
        cmat = P.dram("cmat", [128, 3 * 128], F32, "ExternalInput")
        pT = P.dram("pT", [IN_COLS, NT], F32, "ExternalOutput")

        xs = P.sbuf([128, 16, NT], F32, "xs")
        hT = P.sbuf([128, 16, NT], BF16, "hT")
        mods = P.sbuf([128, 96, 2], F32, "mods")
        gs = P.sbuf([128, 16], F32, "gs")
        qk = P.sbuf([128, 2], F32, "qk")
        cs_ = P.sbuf([128, NL], F32, "cos")
        sn_ = P.sbuf([128, NL], F32, "sin")
        cm = P.sbuf([128, 384], F32, "cm")
        A = P.sbuf([128, 16, 2], F32, "A")
        rstd = P.sbuf([128, NT], F32, "rstd")
        tmp = [P.sbuf([128, NT], F32, "tmp%d" % i) for i in range(2)]
        banks = [P.psum([128, 512], F32, "bank%d" % i) for i in range(8)]

        xTr = xT.re("(kc k) n -> k kc n", k=128)
        for kc in range(16):
            P.dma(xs[:, kc, :], xTr[:, kc, :], nowaw=True)
        P.dma(mods[:], modT.re("k (c r) -> k c r", r=2)[:, :, :])
        P.dma(gs[:], gain[:])
        P.dma(qk[:], qkg[:])
        P.dma(cs_[:], cosT[:])
        P.dma(sn_[:], sinT[:])
        P.dma(cm[:], cmat[:])
        ones = cm[:, 0:128]
        bo = cm[:, 128:256]
        RT = cm[:, 256:384]

        for r in range(2):
            P.ts(A[:, :, r], mods[:, 16:32, r], 1.0, ALU.add)
            P.tt(A[:, :, r], A[:, :, r], gs[:], ALU.mult)
        for kc in range(16):
            sq = tmp[kc % 2]
            P.act(sq[:], xs[:, kc, :], AF.Square)
            for tg in range(3):
                P.mm(banks[tg][:, 0:352], ones, sq[:, tg * 352:(tg + 1) * 352], start=(kc == 0), stop=(kc == 15))
        for tg in range(3):
            P.act(rstd[:, tg * 352:(tg + 1) * 352], banks[tg][:, 0:352], AF.Sqrt, scale=1.0 / D, bias=EPSB(P))
        P.recip(rstd[:], rstd[:])
        for kc in range(16):
            t = tmp[kc % 2]
            P.tt(t[:], xs[:, kc, :], rstd[:], ALU.mult)
            P.act(hT[:, kc, 0:NL], t[:, 0:NL], AF.Identity, scale=A[:, kc, 0:1], bias=mods[:, kc, 0:1])
            P.ts(hT[:, kc, NL:NT], t[:, NL:NT], A[:, kc, 1:2], ALU.mult, mods[:, kc, 1:2], ALU.add)

        wst = [P.sbuf([128, 16, 256], F32, "wst%d" % i) for i in range(2)]
        wbf = [P.sbuf([128, 16, 256], BF16, "wbf%d" % i) for i in range(2)]
        pout = [P.sbuf([128, NT], F32, "pout%d" % i) for i in range(3)]
        w_r = w_in.re("(kc k) c -> k kc c", k=128)
        groups = k1_groups()
        slabs = []
        i = 0
        while i < len(groups):
            c0, n, _ = groups[i]
            if n == 128 and i + 1 < len(groups) and groups[i + 1][1] == 128 and groups[i + 1][0] == c0 + 128:
                slabs.append((c0, 256, [groups[i], groups[i + 1]]))
                i += 2
            else:
                slabs.append((c0, n, [groups[i]]))
                i += 1
        bi = 0
        gi = 0
        for si, (c0, wn, grs) in enumerate(slabs):
            ws_, wb_ = wst[si % 2], wbf[si % 2]
            for kh in range(2):
                P.dma(ws_[:, kh * 8:(kh + 1) * 8, 0:wn], w_r[:, kh * 8:(kh + 1) * 8, c0:c0 + wn], nowaw=True)
            P.copy(wb_[:, :, 0:wn], ws_[:, :, 0:wn], eng=('dve' if si % 2 == 0 else 'pool'))
            for (gc0, gn, kind) in grs:
                off = gc0 - c0
                po = pout[gi % 3]
                gi += 1
                for tg in range(3):
                    bk = banks[3 + (bi % 5)]
                    bi += 1
                    for kc in range(16):
                        P.mm(bk[0:gn, 0:352], wb_[:, kc, off:off + gn], hT[:, kc, tg * 352:(tg + 1) * 352], start=(kc == 0), stop=(kc == 15))
                    P.act(po[0:gn, tg * 352:(tg + 1) * 352], bk[0:gn, 0:352], AF.Copy)
                if kind in ('gq_q', 'gq_k'):
                    gcol = qk[:, 0:1] if kind == 'gq_q' else qk[:, 1:2]
                    sq = tmp[0]
                    P.act(sq[:], po[:], AF.Square)
                    r2 = tmp[1]
                    for tg in range(3):
                        P.mm(banks[tg][:, 0:352], bo, sq[:, tg * 352:(tg + 1) * 352])
                        P.act(r2[:, tg * 352:(tg + 1) * 352], banks[tg][:, 0:352], AF.Sqrt, scale=1.0 / 64, bias=EPSB(P))
                    P.recip(r2[:], r2[:])
                    P.stt(po[:], po[:], gcol, r2[:], ALU.mult, ALU.mult)
                if kind is not None:
                    t1 = tmp[0]
                    for hh in range(2):
                        P.mm(banks[hh][:, 0:512], RT, po[:, hh * 512:(hh + 1) * 512])
                    P.tt(t1[:, 0:NL], po[:, 0:NL], cs_[:], ALU.mult)
                    for hh in range(2):
                        P.tt(po[:, hh * 512:(hh + 1) * 512], banks[hh][:, 0:512], sn_[:, hh * 512:(hh + 1) * 512], ALU.mult)
                    P.tt(po[:, 0:NL], po[:, 0:NL], t1[:, 0:NL], ALU.add)
                P.dma(pT[gc0:gc0 + gn, :], po[0:gn, :])
        P.finish([pT])
    return nc


def EPSB(P):
    if not hasattr(P, '_epsb'):
        P._epsb = P.sbuf([128, 1], F32, "epsb")
        P.memset(P._epsb[:], EPS)
    return P._epsb[:]


def rope_tables():
    rows = 8192 // 64
    row = np.repeat(np.arange(rows, dtype=np.float32), 64)
    col = np.tile(np.arange(64, dtype=np.float32), rows)
    n_freq = 16
    inv = (np.float32(10000.0) ** (-np.arange(n_freq, dtype=np.float32) / n_freq)).astype(np.float32)
    ang = np.concatenate([row[:, None] * inv, col[:, None] * inv], axis=-1).astype(np.float32)
    return np.cos(ang).astype(np.float32), np.sin(ang).astype(np.float32)


def const_mats():
    ones = np.ones((128, 128), np.float32)
    bo = np.zeros((128, 128), np.float32)
    bo[:64, :64] = 1
    bo[64:, 64:] = 1
    RT = np.zeros((128, 128), np.float32)
    for m in range(128):
        if (m % 64) < 32:
            RT[m + 32, m] = -1.0
        else:
            RT[m - 32, m] = 1.0
    return np.concatenate([ones, bo, RT], axis=1)


NT = 1056
NL = 1024
NK = 8448
KT = 66


def build_k2():
    nc, st, P = new_prog()
    with st:
        qT = P.dram("qT", [16, 64, NT], F32, "ExternalInput")
        kT = P.dram("kT", [10, 64, NK], F32, "ExternalInput")
        vv = P.dram("vv", [2, 128, KT * 64], F32, "ExternalInput")
        vd = P.dram("vd", [4, 128, KT * 128], F32, "ExternalInput")
        lamv = P.dram("lamv", [128, 256], F32, "ExternalInput")
        misc = P.dram("misc", [128, 4], F32, "ExternalInput")
        onesd = P.dram("ones", [128, 128], F32, "ExternalInput")
        yaT = P.dram("yaT", [512, NT], F32, "ExternalOutput")
        ydT = P.dram("ydT", [512, NT], F32, "ExternalOutput")

        stage = [P.sbuf([128, 2112], F32, "stage%d" % i) for i in range(2)]
        kbf = [P.sbuf([64, NK], BF16, "kbf%d" % i) for i in range(2)]
        qbf = [P.sbuf([64, NT], BF16, "qbf%d" % i) for i in range(2)]
        vbf = [P.sbuf([128, KT * 128], BF16, "vbf%d" % i) for i in range(2)]
        pb = [P.sbuf([128, 1024], BF16, "pb%d" % i) for i in range(3)]
        ones_f = P.sbuf([128, 128], F32, "ones_f")
        ones_b = P.sbuf([128, 128], BF16, "ones_b")
        lv = P.sbuf([128, 256], F32, "lv")
        ms = P.sbuf([128, 4], F32, "ms")
        lam = P.sbuf([128, 4], F32, "lam")
        rd = P.sbuf([128, 512], F32, "rd")
        accA = P.sbuf([128, 512], F32, "accA")
        accB = P.sbuf([128, 512], F32, "accB")
        osb = [P.sbuf([128, NT], F32, "osb%d" % i) for i in range(3)]
        sq = P.sbuf([128, NT], F32, "sq")
        sbank = [P.psum([128, 1024], F32, "sbank%d" % i) for i in range(2)]
        obank = [P.psum([128, 512], F32, "obank%d" % i) for i in range(2)]
        dbank = [P.psum([128, 512], F32, "dbank%d" % i) for i in range(1)]
        nbank = P.psum([128, 512], F32, "nbank")

        P.dma(ones_f[:], onesd[:])
        P.copy(ones_b[:], ones_f[:])
        P.dma(lv[:], lamv[:])
        P.dma(ms[:], misc[:])
        pr = P.sbuf([128, 128], F32, "pr")
        P.tt(pr[:, 0:64], lv[:, 0:64], lv[:, 64:128], ALU.mult)
        P.tt(pr[:, 64:128], lv[:, 128:192], lv[:, 192:256], ALU.mult)
        o0, i0 = lam[:, 0:1].ap, pr[:, 0:64].ap
        P.op('dve', lambda e: e.tensor_reduce(out=o0, in_=i0, axis=AX.X, op=ALU.add), reads=[pr], writes=[lam])
        o1, i1 = lam[:, 1:2].ap, pr[:, 64:128].ap
        P.op('dve', lambda e: e.tensor_reduce(out=o1, in_=i1, axis=AX.X, op=ALU.add), reads=[pr], writes=[lam])
        P.act(lam[:, 0:2], lam[:, 0:2], AF.Exp)
        P.tt(lam[:, 2:3], lam[:, 0:1], lam[:, 1:2], ALU.subtract)
        P.tt(lam[:, 2:3], lam[:, 2:3], ms[:, 1:2], ALU.add)
        P.ts(lam[:, 3:4], lam[:, 2:3], -1.0, ALU.mult)
        gsc = P.sbuf([128, 2], F32, "gsc")
        P.tt(gsc[:, 0:1], ms[:, 0:1], ms[:, 2:3], ALU.mult)

        cnt = {'s': 0, 'p': 0, 'a': 0, 'st': 0, 'o': 0}

        def load_cast(dst_view_fn, src_view_fn, np_, ncols_total, piece=2112):
            c = 0
            while c < ncols_total:
                n = min(piece, ncols_total - c)
                sg = stage[cnt['st'] % 2]
                cnt['st'] += 1
                P.dma(sg[0:np_, 0:n], src_view_fn(c, n))
                P.copy(dst_view_fn(c, n), sg[0:np_, 0:n], eng='pool')
                c += n

        def attend(kb, qb, vb, dv, ob):
            for (q0, qn, nkt) in [(0, 512, KT), (512, 512, KT), (1024, 32, 2)]:
                a = cnt['a'] % 2
                cnt['a'] += 1
                ob_, db_ = obank[a], dbank[0]
                def issue_s(pp):
                    sb__ = sbank[cnt['s'] % 2]
                    cnt['s'] += 1
                    for hh_ in range(2):
                        kt_ = 2 * pp + hh_
                        P.mm(sb__[:, hh_ * 512:hh_ * 512 + qn], kb[:, kt_ * 128:(kt_ + 1) * 128], qb[:, q0:q0 + qn])
                    return sb__
                npair = nkt // 2
                nxt = issue_s(0)
                for pp in range(npair):
                    sb_ = nxt
                    if pp + 1 < npair:
                        nxt = issue_s(pp + 1)
                    pt = pb[cnt['p'] % 3]
                    cnt['p'] += 1
                    if qn == 512:
                        P.act(pt[:, 0:1024], sb_[:, 0:1024], AF.Exp, scale=0.125)
                    else:
                        for hh_ in range(2):
                            P.act(pt[:, hh_ * 512:hh_ * 512 + qn], sb_[:, hh_ * 512:hh_ * 512 + qn], AF.Exp, scale=0.125)
                    for hh_ in range(2):
                        kt = 2 * pp + hh_
                        pv = pt[:, hh_ * 512:hh_ * 512 + qn]
                        P.mm(ob_[0:dv, 0:qn], vb[:, kt * dv:(kt + 1) * dv], pv, start=(kt == 0), stop=(kt == nkt - 1))
                        if kt == 0:
                            P.copy(accA[:, 0:qn], pv)
                        elif kt == 1:
                            P.copy(accB[:, 0:qn], pv, eng='pool')
                        elif kt % 4 != 1:
                            P.tt(accA[:, 0:qn], accA[:, 0:qn], pv, ALU.add)
                        else:
                            P.tt(accB[:, 0:qn], accB[:, 0:qn], pv, ALU.add, eng='pool')
                P.tt(accA[:, 0:qn], accA[:, 0:qn], accB[:, 0:qn], ALU.add)
                P.mm(db_[0:dv, 0:qn], ones_f[:, 0:dv], accA[:, 0:qn])
                P.recip(rd[0:dv, 0:qn], db_[0:dv, 0:qn])
                P.tt(ob[0:dv, q0:q0 + qn], ob_[0:dv, 0:qn], rd[0:dv, 0:qn], ALU.mult)

        units = []

        def mk_gqa(g, hh):
            h = g * 4 + hh
            kb, vb, qb = kbf[g % 2], vbf[g % 2], qbf[h % 2]

            def load():
                if hh == 0:
                    load_cast(lambda c, n: kb[:, c:c + n], lambda c, n: kT[g, :, c:c + n], 64, NK)
                    load_cast(lambda c, n: vb[:, c:c + n], lambda c, n: vv[g, :, c:c + n], 128, KT * 64)
                load_cast(lambda c, n: qb[:, c:c + n], lambda c, n: qT[h, :, c:c + n], 64, NT)

            def comp():
                ob = osb[cnt['o'] % 3]
                cnt['o'] += 1
                attend(kb, qb, vb, 64, ob)
                P.dma(yaT[h * 64:(h + 1) * 64, :], ob[0:64, :])
            return load, comp

        dstate = {}

        def mk_diff(h, m):
            u = h * 2 + m
            vb, kb, qb = vbf[h % 2], kbf[u % 2], qbf[u % 2]

            def load():
                if m == 0:
                    load_cast(lambda c, n: vb[:, c:c + n], lambda c, n: vd[h, :, c:c + n], 128, KT * 128)
                load_cast(lambda c, n: kb[:, c:c + n], lambda c, n: kT[2 + u, :, c:c + n], 64, NK)
                load_cast(lambda c, n: qb[:, c:c + n], lambda c, n: qT[8 + u, :, c:c + n], 64, NT)

            def comp():
                ob = osb[cnt['o'] % 3]
                cnt['o'] += 1
                attend(kb, qb, vb, 128, ob)
                if m == 0:
                    dstate['o0'] = ob
                    return
                o0_, o1_ = dstate['o0'], ob
                P.stt(o0_[:], o1_[:], lam[:, 3:4], o0_[:], ALU.mult, ALU.add)
                P.act(sq[:], o0_[:], AF.Square)
                for tg in range(3):
                    P.mm(nbank[:, 0:352], ones_f[:], sq[:, tg * 352:(tg + 1) * 352])
                    P.act(o1_[:, tg * 352:(tg + 1) * 352], nbank[:, 0:352], AF.Sqrt, scale=1.0 / 128, bias=EPSB(P))
                P.recip(o1_[:], o1_[:])
                P.stt(o0_[:], o0_[:], gsc[:, 0:1], o1_[:], ALU.mult, ALU.mult)
                P.dma(ydT[h * 128:(h + 1) * 128, :], o0_[:])
            return load, comp

        for g in range(2):
            for hh in range(4):
                units.append(mk_gqa(g, hh))
        for h in range(4):
            for m in range(2):
                units.append(mk_diff(h, m))
        units[0][0]()
        for ui in range(len(units)):
            if ui + 1 < len(units):
                units[ui + 1][0]()
            units[ui][1]()
        P.finish([yaT, ydT])
    return nc


LL = 8192
LC = 256


def hy_consts(L, core):
    nb = L // 128
    cb = np.arange(2 * nb)[:, None]
    jp = np.arange(128)[None, :]
    b = cb - nb
    n_f = 128 * b + 127 - jp
    n_b = 128 * (-b) - 127 + jp
    n = np.where(b >= 0, n_f, n_b)
    valid = (n >= 0) & (n < L)
    n = np.where(valid, n, 0).astype(np.float32)
    t = (n / np.float32(L - 1)).astype(np.float32)
    w = (np.float32(2.0 * math.pi) * n / np.float32(L)).astype(np.float32)
    f = np.linspace(1e-4, 15, 16, dtype=np.float32)
    z = np.concatenate([t[..., None], np.cos(f * w[..., None]), -np.sin(f * w[..., None])], axis=-1).astype(np.float32)
    zT = np.ascontiguousarray(z.reshape(2 * nb * 128, 33).T)
    min_decay = math.log(1e-2) / 1.5
    max_decay = math.log(1e-2) / 0.3
    deltas = np.linspace(min_decay, max_decay, 512, dtype=np.float32)[core * 64:(core + 1) * 64]
    win = np.exp(-t[..., None] * np.abs(deltas)).astype(np.float32) * valid[..., None]
    win = np.ascontiguousarray(win.transpose(1, 0, 2).reshape(128, 2 * nb * 64)).astype(np.float32)
    return zT, win


def build_k3():
    nc, st, P = new_prog()
    with st:
        pl = P.dram("pl", [3, 64, LL], F32, "ExternalInput")
        pc = P.dram("pc", [3, 64, LC], F32, "ExternalInput")
        swd = P.dram("sw", [64, 9], F32, "ExternalInput")
        w1d = P.dram("w1", [33, 64], F32, "ExternalInput")
        w23d = P.dram("w23", [64, 128], F32, "ExternalInput")
        w4d = P.dram("w4s", [64, 128], F32, "ExternalInput")
        vecd = P.dram("vec", [64, 8], F32, "ExternalInput")
        zld = P.dram("zl", [33, 2 * 64 * 128], F32, "ExternalInput")
        zcd = P.dram("zc", [33, 2 * 2 * 128], F32, "ExternalInput")
        wld = P.dram("wl", [128, 128 * 64], F32, "ExternalInput")
        wcd = P.dram("wc", [128, 4 * 64], F32, "ExternalInput")
        identd = P.dram("ident", [128, 128], F32, "ExternalInput")
        yb = P.dram("yb", [64, LL + LC], F32, "ExternalOutput")
        upl_h = nc.dram_tensor("upl", [64, LL + 256], BF16, kind="Internal")
        upc_h = nc.dram_tensor("upc", [64, LC + 256], BF16, kind="Internal")
        upl = Buf(upl_h.ap(), "upl")
        upc = Buf(upc_h.ap(), "upc")

        sw = P.sbuf([64, 9], F32, "sw_s")
        w1 = P.sbuf([33, 64], F32, "w1_s")
        w23 = P.sbuf([64, 128], F32, "w23_s")
        w4 = P.sbuf([64, 128], F32, "w4_s")
        vec = P.sbuf([64, 8], F32, "vec_s")
        sc = P.sbuf([64, 4], F32, "sc_s")
        ident = P.sbuf([128, 128], F32, "ident_s")
        zero = P.sbuf([64, 136], BF16, "zero_s")
        for d, s in [(sw, swd), (w1, w1d), (w23, w23d), (w4, w4d), (vec, vecd), (ident, identd)]:
            P.dma(d[:], s[:])
        P.memset(zero[:], 0.0)
        P.ts(sc[:, 0:1], vec[:, 3:4], 1.0 / 3.0, ALU.mult)
        for k in range(3):
            P.tt(sc[:, 1 + k:2 + k], vec[:, k:k + 1], sc[:, 0:1], ALU.mult)

        PIECE = 2048
        pg = P.sbuf([64, 3, PIECE + 2], F32, "pg")
        cg = P.sbuf([64, 3, PIECE], F32, "cg")
        ub = P.sbuf([64, PIECE], BF16, "ub")
        Hm = P.sbuf([128, 64, 128], BF16, "Hm")
        ush = [P.sbuf([128, 65 * 128], BF16, "ush%d" % i) for i in range(2)]
        ytok = P.sbuf([128, 64, 64], F32, "ytok")
        yconv = P.sbuf([64, LL], F32, "yconv")
        zt = [P.sbuf([33, 512], F32, "zt%d" % i) for i in range(2)]
        hs = [P.sbuf([64, 512], F32, "hs%d" % i) for i in range(3)]
        s2 = P.sbuf([64, 512], F32, "s2")
        wp = [P.sbuf([128, 4, 64], F32, "wp%d" % i) for i in range(2)]
        fb = [P.psum([128, 512], F32, "fb%d" % i) for i in range(3)]
        hb = [P.psum([128, 512], F32, "hb%d" % i) for i in range(2)]
        yk = [P.psum([128, 512], F32, "yk%d" % i) for i in range(2)]
        tb = P.psum([128, 512], F32, "tb")

        def conv_piece(src, L, q, piece):
            lo = q * piece - 1
            hi = (q + 1) * piece + 1
            clo, chi = max(lo, 0), min(hi, L)
            if clo != lo or chi != hi:
                P.memset(pg[:], 0.0)
            for g in range(3):
                P.dma(pg[:, g, clo - lo:chi - lo], src[g, :, clo:chi])
            for g in range(3):
                P.ts(cg[:, g, 0:piece], pg[:, g, 1:1 + piece], sw[:, g * 3 + 1:g * 3 + 2], ALU.mult)
                P.stt(cg[:, g, 0:piece], pg[:, g, 0:piece], sw[:, g * 3:g * 3 + 1], cg[:, g, 0:piece], ALU.mult, ALU.add)
                P.stt(cg[:, g, 0:piece], pg[:, g, 2:2 + piece], sw[:, g * 3 + 2:g * 3 + 3], cg[:, g, 0:piece], ALU.mult, ALU.add)
            P.tt(cg[:, 1, 0:piece], cg[:, 1, 0:piece], cg[:, 2, 0:piece], ALU.mult)

        def sin3(out, ps, k):
            P.act(out[:], ps[0:64, :], AF.Sin, scale=sc[:, 0:1], bias=sc[:, 1 + k:2 + k])
            P.tt(s2[:], out[:], out[:], ALU.mult)
            P.ts(s2[:], s2[:], -4.0, ALU.mult, 3.0, ALU.add)
            P.tt(out[:], s2[:], out[:], ALU.mult)

        def run_seq(src, L, up, up_h, zd, wd, out0):
            nb = L // 128
            piece = min(L, PIECE)
            npieces = L // piece
            W = L + 256
            P.dma(up[:, 0:127], zero[:, 0:127])
            P.dma(up[:, 127 + L:W], zero[:, 0:129])
            for q in range(npieces):
                conv_piece(src, L, q, piece)
                P.copy(ub[:, 0:piece], cg[:, 1, 0:piece])
                P.dma(up[:, 127 + q * piece:127 + (q + 1) * piece], ub[:, 0:piece])
            ntile = (2 * nb * 128) // 512
            for ti in range(ntile):
                z_ = zt[ti % 2]
                P.dma(z_[:], zd[:, ti * 512:(ti + 1) * 512])
                P.mm(fb[0][0:64, :], w1[:, :], z_[:, :])
                sin3(hs[0], fb[0], 0)
                P.mm(fb[1][0:64, :], w23[:, 0:64], hs[0][:, :])
                sin3(hs[1], fb[1], 1)
                P.mm(fb[2][0:64, :], w23[:, 64:128], hs[1][:, :])
                sin3(hs[2], fb[2], 2)
                hbk = hb[ti % 2]
                for k in range(4):
                    cbi = ti * 4 + k
                    wcol = 0 if cbi >= nb else 64
                    P.mm(hbk[:, k * 64:(k + 1) * 64], hs[2][:, k * 128:(k + 1) * 128], w4[:, wcol:wcol + 64])
                wp_ = wp[ti % 2]
                P.dma(wp_[:], wd.re("p (b c) -> p b c", c=64)[:, ti * 4:(ti + 1) * 4, :])
                hv = View(hbk, hbk.t[:, 0:256].rearrange("p (k c) -> p k c", k=4))
                ov = View(Hm, Hm.t[:, :, ti * 4:(ti + 1) * 4].rearrange("p c k -> p k c"))
                P.tt(ov, hv, wp_[:], ALU.mult)
            ncol = (nb + 1) * 128
            for c in range(64):
                us = ush[c % 2]
                src_ap = bass.AP(up_h, c * W, [[1, 128], [1, ncol]])
                P.dma(us[:, 0:ncol], View(up, src_ap))
                ykb = yk[(c // 8) % 2]
                c8 = c % 8
                for m in range(nb + 1):
                    P.mm(ykb[:, c8 * nb:(c8 + 1) * nb], us[:, m * 128:(m + 1) * 128], Hm[:, c, nb - m:2 * nb - m],
                         start=(m == 0), stop=(m == nb))
                if c8 == 7:
                    c0 = c - 7
                    iv = View(ykb, ykb.t[:, 0:8 * nb].rearrange("p (c a) -> p c a", c=8))
                    ov = View(ytok, ytok.t[:, 0:nb, c0:c0 + 8].rearrange("p a c -> p c a"))
                    P.act(ov, iv, AF.Copy)
            for a in range(nb):
                k = a % 4
                P.transpose(tb[0:64, k * 128:(k + 1) * 128], ytok[:, a, :], ident[:])
                if k == 3 or a == nb - 1:
                    a0 = a - k
                    P.act(yconv[:, a0 * 128:(a + 1) * 128], tb[0:64, 0:(k + 1) * 128], AF.Copy)
            for q in range(npieces):
                conv_piece(src, L, q, piece)
                P.stt(cg[:, 1, 0:piece], cg[:, 1, 0:piece], vec[:, 4:5], yconv[:, q * piece:(q + 1) * piece], ALU.mult, ALU.add)
                P.tt(cg[:, 0, 0:piece], cg[:, 0, 0:piece], cg[:, 1, 0:piece], ALU.mult)
                P.dma(yb[:, out0 + q * piece:out0 + (q + 1) * piece], cg[:, 0, 0:piece])

        run_seq(pl, LL, upl, upl_h, zld, wld, 0)
        run_seq(pc, LC, upc, upc_h, zcd, wcd, LL)
        P.finish([yb])
    return nc


NS = 8448
NCH = 132


def dn_consts():
    U = np.triu(np.ones((64, 64), np.float32))
    Ls = np.tril(np.ones((64, 64), np.float32), -1)
    cm = np.zeros((128, 512), np.float32)
    cm[:64, 0:64] = U
    cm[:64, 64:128] = Ls
    cm[:, 128:256] = np.eye(128, dtype=np.float32)
    cm[:, 256:384] = 1.0
    return cm


def build_k4():
    nc, st, P = new_prog()
    with st:
        pqkv = P.dram("pqkv", [3, 128, NS], F32, "ExternalInput")
        tapsd = P.dram("taps", [128, 9], F32, "ExternalInput")
        grawd = P.dram("graw", [64, 2 * NCH], F32, "ExternalInput")
        scd = P.dram("scal", [128, 2], F32, "ExternalInput")
        cmd = P.dram("cm", [128, 512], F32, "ExternalInput")
        oT = P.dram("oT", [128, NS], F32, "ExternalOutput")

        cm = P.sbuf([128, 512], F32, "cm_s")
        taps = P.sbuf([128, 9], F32, "taps_s")
        graw = P.sbuf([64, 2 * NCH], F32, "graw_s")
        scl = P.sbuf([128, 2], F32, "scl_s")
        for d, s in [(cm, cmd), (taps, tapsd), (graw, grawd), (scl, scd)]:
            P.dma(d[:], s[:])
        U = cm[0:64, 0:64]
        Ls = cm[0:64, 64:128]
        I64 = cm[0:64, 128:192]
        I128 = cm[:, 128:256]
        ones = cm[:, 256:384]
        ones64 = cm[0:64, 256:384]
        eps = P.sbuf([128, 1], F32, "eps_s")
        P.memset(eps[:], 1e-6)

        qkv = [P.sbuf([128, NS], F32, "qkv%d" % g) for g in range(3)]
        PIECE = 2048
        pg = P.sbuf([128, PIECE + 2], F32, "pg")
        cgt = P.sbuf([128, PIECE], F32, "cgt")
        sg = P.sbuf([128, PIECE], F32, "sg")
        bank = [P.psum([128, 512], F32, "bk%d" % i) for i in range(8)]
        bctr = [0]

        def nb():
            b = bank[bctr[0] % 8]
            bctr[0] += 1
            return b

        segs = [(0, 256)] + [(256 + i * PIECE, PIECE) for i in range(4)]
        for g in range(3):
            for (s0, n) in segs:
                first = (s0 == 0 or s0 == 256)
                last = (s0 + n == 256 or s0 + n == NS)
                lo = s0 - (0 if first else 1)
                hi = s0 + n + (0 if last else 1)
                if first or last:
                    P.memset(pg[:], 0.0)
                P.dma(pg[:, 1 - (s0 - lo):1 + n + (hi - s0 - n)], pqkv[g, :, lo:hi])
                P.ts(cgt[:, 0:n], pg[:, 1:1 + n], taps[:, g * 3 + 1:g * 3 + 2], ALU.mult)
                P.stt(cgt[:, 0:n], pg[:, 0:n], taps[:, g * 3:g * 3 + 1], cgt[:, 0:n], ALU.mult, ALU.add)
                P.stt(cgt[:, 0:n], pg[:, 2:2 + n], taps[:, g * 3 + 2:g * 3 + 3], cgt[:, 0:n], ALU.mult, ALU.add)
                P.act(qkv[g][:, s0:s0 + n], cgt[:, 0:n], AF.Silu)
                if g < 2:
                    P.act(sg[:, 0:n], qkv[g][:, s0:s0 + n], AF.Square)
                    for c0 in range(0, n, 512):
                        w = min(512, n - c0)
                        b = nb()
                        P.mm(b[:, 0:w], ones, sg[:, c0:c0 + w])
                        P.act(sg[:, c0:c0 + w], b[:, 0:w], AF.Sqrt, bias=eps[:])
                    P.recip(sg[:, 0:n], sg[:, 0:n])
                    if g == 0:
                        P.stt(qkv[g][:, s0:s0 + n], qkv[g][:, s0:s0 + n], 128.0 ** -0.5, sg[:, 0:n], ALU.mult, ALU.mult)
                    else:
                        P.tt(qkv[g][:, s0:s0 + n], qkv[g][:, s0:s0 + n], sg[:, 0:n], ALU.mult)
        qT, kT, vT = qkv

        beta = P.sbuf([64, NCH], F32, "beta")
        nbeta = P.sbuf([64, NCH], F32, "nbeta")
        gg = P.sbuf([64, NCH], F32, "gg")
        ea = P.sbuf([128, 1], F32, "ea")
        P.act(beta[:], graw[:, 0:NCH], AF.Sigmoid)
        P.ts(nbeta[:], beta[:], -1.0, ALU.mult)
        P.act(gg[:], graw[:, NCH:2 * NCH], AF.Exp, bias=scl[0:64, 1:2])
        P.act(gg[:], gg[:], AF.Ln, bias=cm[0:64, 256:257])
        P.act(ea[:], scl[:, 0:1], AF.Exp)
        P.ts(gg[:], gg[:], ea[0:64, 0:1], ALU.mult, -1.0, ALU.mult)
        gc = P.sbuf([64, NCH], F32, "gc")
        egc = P.sbuf([64, NCH], F32, "egc")
        bke = P.sbuf([64, NCH], F32, "bke")
        kde = P.sbuf([64, NCH], F32, "kde")
        lastB = P.sbuf([128, NCH], F32, "lastB")
        b = nb()
        P.mm(b[0:64, 0:NCH], U, gg[:, :])
        P.act(gc[:], b[0:64, 0:NCH], AF.Copy)
        P.act(egc[:], gc[:], AF.Exp)
        P.tt(bke[:], beta[:], egc[:], ALU.mult)
        b = nb()
        P.mm(b[:, 0:NCH], ones64, gg[:, :])
        P.act(lastB[:], b[:, 0:NCH], AF.Exp)
        P.tt(kde[:], b[0:64, 0:NCH], gc[:], ALU.subtract)
        P.act(kde[:], kde[:], AF.Exp)

        S = [P.sbuf([128, 128], F32, "S%d" % i) for i in range(2)]
        P.memset(S[0][:], 0.0)
        oacc = P.sbuf([128, NS], F32, "oacc")

        def T(shape, name, n=2):
            return [P.sbuf(shape, F32, "%s%d" % (name, i)) for i in range(n)]
        gB = T([64, 128], "gB", 4); tt_ = T([128, 64], "tt", 4); a1 = T([64, 64], "a1", 4); gs = T([64, 64], "gs", 4)
        gT_ = T([64, 64], "gT", 4); egr = T([128, 64], "egr", 4)
        XX = [T([64, 64], "Xa%d" % i, 4) for i in range(4)]
        ZZ = [T([64, 64], "Za%d" % i, 4) for i in range(4)]
        WW = [T([64, 64], "Wa%d" % i, 4) for i in range(4)]
        vb = T([64, 128], "vb", 4); kbd = T([64, 128], "kbd", 4); kd = T([64, 128], "kd", 4)
        u = T([64, 128], "u", 4); wT = T([128, 64], "wT", 4); qd = T([128, 64], "qd", 4); qk = T([64, 64], "qk", 4)
        vn = T([64, 128], "vn", 4)
        def pre(c):
            cols = slice(64 * c, 64 * c + 64)
            i2 = c % 4
            i4 = c % 4
            X, Z, W = XX[i2], ZZ[i2], WW[i2]
            P.act(gB[i2][:], ones64, AF.Identity, scale=gg[:, c:c + 1])
            yield
            b1 = nb()
            P.mm(b1[:, 0:64], gB[i2][:], U)
            P.act(tt_[i2][:], b1[:, 0:64], AF.Copy)
            P.act(egr[i2][:], tt_[i2][:], AF.Exp)
            P.ts(a1[i2][:], tt_[i2][0:64, :], gc[:, c:c + 1], ALU.subtract, 0.0, ALU.min)
            P.act(gT_[i2][:], a1[i2][:], AF.Exp)
            yield
            P.tt(gT_[i2][:], gT_[i2][:], U, ALU.mult)
            P.ts(a1[i2][:], tt_[i2][0:64, :], gc[:, c:c + 1], ALU.subtract, -1.0, ALU.mult)
            P.ts(a1[i2][:], a1[i2][:], 0.0, ALU.min)
            yield
            P.act(gs[i2][:], a1[i2][:], AF.Exp)
            yield
            P.tt(gs[i2][:], gs[i2][:], Ls, ALU.mult)
            b2 = nb()
            P.mm(b2[0:64, 0:64], kT[:, cols], kT[:, cols])
            x0 = X[0]
            P.stt(x0[:], b2[0:64, 0:64], nbeta[:, c:c + 1], gs[i2][:], ALU.mult, ALU.mult)
            b3 = nb()
            P.transpose(b3[0:64, 0:64], x0[:], I64)
            yield
            z0 = Z[0]
            P.act(z0[:], b3[0:64, 0:64], AF.Copy)
            yield
            w0 = W[0]
            P.tt(w0[:], z0[:], I64, ALU.add)
            yield
            xk, zk, wk = x0, z0, w0
            for lev in range(5):
                xn, zn, wn = X[(lev + 1) % 4], Z[(lev + 1) % 4], W[(lev + 1) % 4]
                bx = nb()
                P.mm(bx[0:64, 0:64], zk[:], xk[:])
                yield
                P.act(xn[:], bx[0:64, 0:64], AF.Copy)
                yield
                if lev < 4:
                    bz = nb()
                    P.mm(bz[0:64, 0:64], xk[:], zk[:])
                    yield
                    P.copy(zn[:], bz[0:64, 0:64])
                    yield
                bw = nb()
                P.mm(bw[0:64, 0:64], xn[:], wk[:])
                yield
                P.tt(wn[:], bw[0:64, 0:64], wk[:], ALU.add)
                yield
                xk, zk, wk = xn, zn, wn
            Wf = wk
            bv = nb()
            P.transpose(bv[0:64, 0:128], vT[:, cols], I128)
            yield
            P.ts(vb[i4][:], bv[0:64, 0:128], beta[:, c:c + 1], ALU.mult)
            yield
            bk_ = nb()
            P.transpose(bk_[0:64, 0:128], kT[:, cols], I128)
            yield
            P.ts(kbd[i4][:], bk_[0:64, 0:128], bke[:, c:c + 1], ALU.mult)
            yield
            P.ts(kd[i4][:], bk_[0:64, 0:128], kde[:, c:c + 1], ALU.mult)
            yield
            bu = nb()
            P.mm(bu[0:64, 0:128], Wf[:], vb[i4][:])
            yield
            P.act(u[i4][:], bu[0:64, 0:128], AF.Copy)
            yield
            bwt = nb()
            P.mm(bwt[:, 0:64], kbd[i4][:], Wf[:])
            yield
            P.act(wT[i4][:], bwt[:, 0:64], AF.Copy)
            yield
            P.tt(qd[i4][:], qT[:, cols], egr[i2][:], ALU.mult)
            yield
            bq = nb()
            P.mm(bq[0:64, 0:64], kT[:, cols], qT[:, cols])
            yield
            P.tt(qk[i4][:], bq[0:64, 0:64], gT_[i2][:], ALU.mult)
            yield

        def scan(c):
            cols = slice(64 * c, 64 * c + 64)
            i4 = c % 4
            Sc, Sn = S[c % 2], S[(c + 1) % 2]
            bs = nb()
            P.mm(bs[0:64, 0:128], wT[i4][:], Sc[:])
            P.tt(vn[i4][:], u[i4][:], bs[0:64, 0:128], ALU.subtract)
            bo = nb()
            P.mm(bo[:, 0:64], Sc[:], qd[i4][:], start=True, stop=False)
            P.mm(bo[:, 0:64], vn[i4][:], qk[i4][:], start=False, stop=True)
            P.act(oacc[:, cols], bo[:, 0:64], AF.Copy)
            bn = nb()
            P.mm(bn[:, 0:128], kd[i4][:], vn[i4][:])
            P.stt(Sn[:], Sc[:], lastB[:, c:c + 1], bn[:, 0:128], ALU.mult, ALU.add)

        def run_pair(gens):
            live = list(gens)
            while live:
                for g_ in list(live):
                    try:
                        next(g_)
                    except StopIteration:
                        live.remove(g_)
        for cp in range(0, NCH, 4):
            run_pair([pre(cp + i_) for i_ in range(4)])
            for i_ in range(4):
                scan(cp + i_)
        for i in range(4):
            P.dma(oT[:, i * 2112:(i + 1) * 2112], oacc[:, i * 2112:(i + 1) * 2112])
        P.finish([oT])
    return nc


D = 2048
NT = 1056
NL = 1024
DFF = 5632


def sumsq_rstd(P, src, nchunks, n, ones, bank, sqtmp, rstd, dim):
    for kc in range(nchunks):
        sq = sqtmp[kc % 2]
        P.act(sq[:, 0:n], src[:, kc, 0:n], AF.Square)
        P.mm(bank[:, 0:n], ones, sq[:, 0:n], start=(kc == 0), stop=(kc == nchunks - 1))
    P.act(rstd[:, 0:n], bank[:, 0:n], AF.Sqrt, scale=1.0 / dim, bias=EPSB(P))
    P.recip(rstd[:, 0:n], rstd[:, 0:n])


def build_k5a():
    nc, st, P = new_prog()
    with st:
        yT = P.dram("yT", [D, NT], F32, "ExternalInput")
        xT = P.dram("xT", [D, NT], F32, "ExternalInput")
        w_out = P.dram("w_out", [D, D], F32, "ExternalInput")
        modT = P.dram("modT", [128, 192], F32, "ExternalInput")
        gains = P.dram("gains", [128, 32], F32, "ExternalInput")
        onesd = P.dram("ones", [128, 128], F32, "ExternalInput")
        ofT = P.dram("ofT", [512, NT], F32, "ExternalInput")
        obT = P.dram("obT", [512, NT], F32, "ExternalInput")
        gtT = P.dram("gtT", [512, NT], F32, "ExternalInput")
        dngd = P.dram("dng", [128, 1], F32, "ExternalInput")
        xmT = P.dram("xmT", [D, NT], F32, "ExternalOutput")
        hT = P.dram("hT", [D, NT], F32, "ExternalOutput")
        dng = P.sbuf([128, 1], F32, "dng_s")
        P.dma(dng[:], dngd[:])
        dn_o = P.sbuf([128, 352], F32, "dn_o")
        dn_b = P.sbuf([128, 352], F32, "dn_b")
        dn_g = P.sbuf([128, 352], F32, "dn_g")

        wob = P.sbuf([128, 16, D], BF16, "wob")
        wst = [P.sbuf([128, 16, 256], F32, "wst%d" % i) for i in range(2)]
        mods = P.sbuf([128, 96, 2], F32, "mods")
        gs = P.sbuf([128, 32], F32, "gs")
        ones = P.sbuf([128, 128], F32, "ones_s")
        G1 = P.sbuf([128, 16, 2], F32, "G1")
        A2 = P.sbuf([128, 16, 2], F32, "A2")
        yb = P.sbuf([128, 16, 352], BF16, "yb")
        ystage = [P.sbuf([128, 352], F32, "ystage%d" % i) for i in range(2)]
        z = P.sbuf([128, 16, 352], F32, "z")
        xg = P.sbuf([128, 16, 352], F32, "xg")
        sqt = [P.sbuf([128, 352], F32, "sqt%d" % i) for i in range(2)]
        tmp = [P.sbuf([128, 352], F32, "tmp%d" % i) for i in range(2)]
        hout = [P.sbuf([128, 352], F32, "hout%d" % i) for i in range(2)]
        rstd = P.sbuf([128, 352], F32, "rstd")
        banks = [P.psum([128, 512], F32, "bank%d" % i) for i in range(6)]
        nbank = P.psum([128, 512], F32, "nbank")

        P.dma(mods[:], modT.re("k (c r) -> k c r", r=2)[:, :, :])
        P.dma(gs[:], gains[:])
        P.dma(ones[:], onesd[:])
        w_r = w_out.re("(kc k) c -> k kc c", k=128)
        for s in range(8):
            ws_ = wst[s % 2]
            for kh in range(2):
                P.dma(ws_[:, kh * 8:(kh + 1) * 8, :], w_r[:, kh * 8:(kh + 1) * 8, s * 256:(s + 1) * 256], nowaw=True)
            P.copy(wob[:, :, s * 256:(s + 1) * 256], ws_[:], eng=('dve' if s % 2 == 0 else 'pool'))
        for r in range(2):
            P.tt(G1[:, :, r], mods[:, 32:48, r], gs[:, 0:16], ALU.mult)
            P.ts(A2[:, :, r], mods[:, 64:80, r], 1.0, ALU.add)
            P.tt(A2[:, :, r], A2[:, :, r], gs[:, 16:32], ALU.mult)
        yT_r = yT.re("(kc k) n -> k kc n", k=128)
        xT_r = xT.re("(kc k) n -> k kc n", k=128)
        xm_r = xmT.re("(kc k) n -> k kc n", k=128)
        hT_r = hT.re("(kc k) n -> k kc n", k=128)
        bi = 0
        for tg in range(3):
            c0 = tg * 352
            nl = 352 if tg < 2 else 320
            rngs = [(0, nl, 0)] + ([(nl, 352, 1)] if nl < 352 else [])
            for kc in range(16):
                ys_ = ystage[kc % 2]
                if 8 <= kc < 12:
                    hh = kc - 8
                    P.dma(dn_o[:], ofT[hh * 128:(hh + 1) * 128, c0:c0 + 352])
                    P.dma(dn_b[:], obT[hh * 128:(hh + 1) * 128, c0:c0 + 352])
                    P.dma(dn_g[:], gtT[hh * 128:(hh + 1) * 128, c0:c0 + 352])
                    P.tt(dn_o[:], dn_o[:], dn_b[:], ALU.add)
                    P.act(dn_b[:], dn_o[:], AF.Square)
                    P.mm(nbank[:, 0:352], ones[:], dn_b[:])
                    P.act(dn_b[:], nbank[:, 0:352], AF.Sqrt, scale=1.0 / 128, bias=EPSB(P))
                    P.recip(dn_b[:], dn_b[:])
                    P.stt(dn_o[:], dn_o[:], dng[:, 0:1], dn_b[:], ALU.mult, ALU.mult)
                    P.act(dn_g[:], dn_g[:], AF.Silu)
                    P.tt(ys_[:], dn_o[:], dn_g[:], ALU.mult)
                else:
                    P.dma(ys_[:], yT_r[:, kc, c0:c0 + 352])
                P.copy(yb[:, kc, :], ys_[:], eng=('dve' if kc % 2 == 0 else 'pool'))
            P.dma(xg[:, 0:8, :], xT_r[:, 0:8, c0:c0 + 352])
            P.dma(xg[:, 8:16, :], xT_r[:, 8:16, c0:c0 + 352])
            for m in range(16):
                bk = banks[bi % 6]
                bi += 1
                for kc in range(16):
                    P.mm(bk[:, 0:352], wob[:, kc, m * 128:(m + 1) * 128], yb[:, kc, :], start=(kc == 0), stop=(kc == 15))
                P.act(z[:, m, :], bk[:, 0:352], AF.Copy)
            sumsq_rstd(P, z, 16, 352, ones[:], nbank, sqt, rstd, D)
            for kc in range(16):
                t = tmp[kc % 2]
                P.tt(t[:], z[:, kc, :], rstd[:], ALU.mult)
                for (a, b, r) in rngs:
                    P.stt(xg[:, kc, a:b], t[:, a:b], G1[:, kc, r:r + 1], xg[:, kc, a:b], ALU.mult, ALU.add)
            sumsq_rstd(P, xg, 16, 352, ones[:], nbank, sqt, rstd, D)
            for kc in range(16):
                t = tmp[kc % 2]
                ho = hout[kc % 2]
                P.tt(t[:], xg[:, kc, :], rstd[:], ALU.mult)
                for (a, b, r) in rngs:
                    P.ts(ho[:, a:b], t[:, a:b], A2[:, kc, r:r + 1], ALU.mult, mods[:, 48 + kc, r:r + 1], ALU.add)
                P.dma(hT_r[:, kc, c0:c0 + 352], ho[:])
            P.dma(xm_r[:, 0:8, c0:c0 + 352], xg[:, 0:8, :])
            P.dma(xm_r[:, 8:16, c0:c0 + 352], xg[:, 8:16, :])
        P.finish([xmT, hT])
    return nc


NP5 = 1060


def build_k5b():
    nc, st, P = new_prog()
    with st:
        hp = P.dram("hp", [D, NP5], F32, "ExternalInput")
        xmT = P.dram("xmT", [D, NT], F32, "ExternalInput")
        w_up = P.dram("w_up", [D, 2 * DFF], F32, "ExternalInput")
        wcv = P.dram("wcv", [128, 88 * 3], F32, "ExternalInput")
        w_dn = P.dram("w_dn", [DFF, D], F32, "ExternalInput")
        modT = P.dram("modT", [128, 192], F32, "ExternalInput")
        gains = P.dram("gains", [128, 16], F32, "ExternalInput")
        onesd = P.dram("ones", [128, 128], F32, "ExternalInput")
        xoT = P.dram("xoT", [D, NT], F32, "ExternalOutput")

        stf = [P.sbuf([128, 5632], F32, "stf%d" % i) for i in range(2)]
        stb = [P.sbuf([128, 5632], BF16, "stb%d" % i) for i in range(2)]
        mods = P.sbuf([128, 96, 2], F32, "mods")
        gs = P.sbuf([128, 16], F32, "gs")
        wc = P.sbuf([128, 88, 3], F32, "wc")
        ones = P.sbuf([128, 128], F32, "ones_s")
        G2 = P.sbuf([128, 16, 2], F32, "G2")
        hg = P.sbuf([128, 16, 376], BF16, "hg")
        hst = [P.sbuf([128, 376], F32, "hst%d" % i) for i in range(2)]
        gT = P.sbuf([128, 44, 372], BF16, "gT")
        dn = P.sbuf([128, 16, 372], F32, "dn")
        xg = P.sbuf([128, 16, 372], F32, "xg")
        ca = [P.sbuf([128, 372], F32, "ca%d" % i) for i in range(2)]
        cb = [P.sbuf([128, 372], F32, "cb%d" % i) for i in range(2)]
        sqt = [P.sbuf([128, 372], F32, "sqt%d" % i) for i in range(2)]
        tmp = [P.sbuf([128, 372], F32, "tmp%d" % i) for i in range(2)]
        rstd = P.sbuf([128, 372], F32, "rstd")
        banks = [P.psum([128, 512], F32, "bank%d" % i) for i in range(6)]
        nbank = P.psum([128, 512], F32, "nbank")

        P.dma(mods[:], modT.re("k (c r) -> k c r", r=2)[:, :, :])
        P.dma(gs[:], gains[:])
        P.dma(ones[:], onesd[:])
        P.dma(wc[:], wcv.re("k (c t) -> k c t", t=3)[:, :, :])
        for r in range(2):
            P.tt(G2[:, :, r], mods[:, 80:96, r], gs[:], ALU.mult)
        hp_r = hp.re("(kc k) n -> k kc n", k=128)
        xm_r = xmT.re("(kc k) n -> k kc n", k=128)
        xo_r = xoT.re("(kc k) n -> k kc n", k=128)
        wu_r = w_up.re("(kc k) c -> k kc c", k=128)
        wd_r = w_dn.re("(f k) c -> k f c", k=128)
        groups = [(0, 344, [(0, 342)], 0, 342, 342),
                  (342, 344, [(0, 342)], 342, 342, 342),
                  (684, 376, [(0, 340), (342, 32)], 684, 372, 340)]
        si = 0
        bi = 0
        for (u0, un, segs, xc0, gn, nl) in groups:
            for kc in range(16):
                hs_ = hst[kc % 2]
                P.dma(hs_[:, 0:un], hp_r[:, kc, u0:u0 + un])
                P.copy(hg[:, kc, 0:un], hs_[:, 0:un], eng=('dve' if kc % 2 == 0 else 'pool'))
            P.dma(xg[:, 0:8, 0:gn], xm_r[:, 0:8, xc0:xc0 + gn])
            P.dma(xg[:, 8:16, 0:gn], xm_r[:, 8:16, xc0:xc0 + gn])
            for f in range(44):
                sf, sb_ = stf[si % 2], stb[si % 2]
                si += 1
                sfv = sf.v(sf.t[:, 0:4096].rearrange("p (a b c) -> p a b c", a=16, b=2))
                sbv = sb_.v(sb_.t[:, 0:4096].rearrange("p (a b c) -> p a b c", a=16, b=2))
                P.dma(View(sf, sfv.ap[:, :, 0, :]), wu_r[:, :, f * 128:(f + 1) * 128], nowaw=True)
                P.dma(View(sf, sfv.ap[:, :, 1, :]), wu_r[:, :, DFF + f * 128:DFF + (f + 1) * 128], nowaw=True)
                P.copy(sb_[:, 0:4096], sf[:, 0:4096], eng='pool')
                bka = banks[bi % 6]
                bkb = banks[(bi + 1) % 6]
                bi += 2
                for kc in range(16):
                    P.mm(bka[:, 0:un], View(sb_, sbv.ap[:, kc, 0, :]), hg[:, kc, 0:un], start=(kc == 0), stop=(kc == 15))
                for kc in range(16):
                    P.mm(bkb[:, 0:un], View(sb_, sbv.ap[:, kc, 1, :]), hg[:, kc, 0:un], start=(kc == 0), stop=(kc == 15))
                ca_, cb_ = ca[f % 2], cb[f % 2]
                goff = 0
                for (lo, n) in segs:
                    for (cc_, bk, ch) in [(ca_, bka, f), (cb_, bkb, 44 + f)]:
                        P.ts(cc_[:, goff:goff + n], bk[:, lo + 1:lo + 1 + n], wc[:, ch, 1:2], ALU.mult)
                        P.stt(cc_[:, goff:goff + n], bk[:, lo:lo + n], wc[:, ch, 0:1], cc_[:, goff:goff + n], ALU.mult, ALU.add)
                        P.stt(cc_[:, goff:goff + n], bk[:, lo + 2:lo + 2 + n], wc[:, ch, 2:3], cc_[:, goff:goff + n], ALU.mult, ALU.add)
                    goff += n
                P.act(ca_[:, 0:gn], ca_[:, 0:gn], AF.Silu)
                P.tt(gT[:, f, 0:gn], ca_[:, 0:gn], cb_[:, 0:gn], ALU.mult)
            for m in range(16):
                sf, sb_ = stf[si % 2], stb[si % 2]
                si += 1
                sfv = sf.v(sf.t[:, :].rearrange("p (f c) -> p f c", f=44))
                sbv = sb_.v(sb_.t[:, :].rearrange("p (f c) -> p f c", f=44))
                P.dma(View(sf, sfv.ap[:, 0:22, :]), wd_r[:, 0:22, m * 128:(m + 1) * 128], nowaw=True)
                P.dma(View(sf, sfv.ap[:, 22:44, :]), wd_r[:, 22:44, m * 128:(m + 1) * 128], nowaw=True)
                P.copy(sb_[:], sf[:], eng='pool')
                bk = banks[bi % 6]
                bi += 1
                for f in range(44):
                    P.mm(bk[:, 0:gn], View(sb_, sbv.ap[:, f, :]), gT[:, f, 0:gn], start=(f == 0), stop=(f == 43))
                P.act(dn[:, m, 0:gn], bk[:, 0:gn], AF.Copy)
            sumsq_rstd(P, dn, 16, gn, ones[:], nbank, sqt, rstd, D)
            rngs = [(0, nl, 0)] + ([(nl, gn, 1)] if nl < gn else [])
            for kc in range(16):
                t = tmp[kc % 2]
                P.tt(t[:, 0:gn], dn[:, kc, 0:gn], rstd[:, 0:gn], ALU.mult)
                for (a, b, r) in rngs:
                    P.stt(xg[:, kc, a:b], t[:, a:b], G2[:, kc, r:r + 1], xg[:, kc, a:b], ALU.mult, ALU.add)
            P.dma(xo_r[:, 0:8, xc0:xc0 + gn], xg[:, 0:8, 0:gn])
            P.dma(xo_r[:, 8:16, xc0:xc0 + gn], xg[:, 8:16, 0:gn])
        P.finish([xoT])
    return nc


_PROGS = {}


def _prog(name, fn):
    if name not in _PROGS:
        _PROGS[name] = fn()
    return _PROGS[name]


def _run(name, fn, maps):
    nc = fn()
    res = run_bass_kernel_spmd(nc, maps, core_ids=list(range(8)))
    return res.results


def _c(a):
    return np.ascontiguousarray(a, dtype=np.float32)


def kernel(x, c, ctx, c_ctx, w_ada, b_ada, norm_mix_pre, norm_mix_post, norm_ffn_pre,
           norm_ffn_post, w_in, w_out, attn_q_norm, attn_k_norm, hy_short, hy_w1, hy_b1,
           hy_w2, hy_b2, hy_w3, hy_b3, hy_w4, hy_freq, hy_skip, dn_short, dn_a_log,
           dn_dt_bias, dn_norm, df_lambda, df_norm, ffn_up, ffn_conv, ffn_down):
    f = lambda a: np.asarray(a, dtype=np.float32)
    x = f(x)[0]; ctxv = f(ctx)[0]; c = f(c); c_ctx = f(c_ctx)
    w_ada = f(w_ada); b_ada = f(b_ada); w_in = f(w_in); w_out = f(w_out)
    ffn_up = f(ffn_up); ffn_conv = f(ffn_conv); ffn_down = f(ffn_down)
    ones = np.ones((128, 128), np.float32)
    ident = np.eye(128, dtype=np.float32)
    cc = np.stack([c[0], c_ctx], axis=-1).reshape(16, 128, 2).transpose(1, 0, 2).reshape(128, 32)
    maps = []
    for j in range(8):
        l, q = j // 4, j % 4
        maps.append({"cc": _c(cc), "w": _c(w_ada[l][:, q * 3072:(q + 1) * 3072]),
                     "b2": _c(np.broadcast_to(b_ada[l][None, q * 3072:(q + 1) * 3072], (2, 3072)))})
    r = _run('k0', build_k0, maps)
    mod = np.zeros((2, 2, 12288), np.float32)
    for j in range(8):
        l, q = j // 4, j % 4
        mod[l][:, q * 3072:(q + 1) * 3072] = r[j]["mod"]
    cos, sin = rope_tables()
    cm1 = const_mats()
    cm4 = dn_consts()
    hyc = [(hy_consts(LL, j), hy_consts(LC, j)) for j in range(8)]

    def shard_T(lat, cx, j):
        return _c(np.concatenate([lat[j * 1024:(j + 1) * 1024], cx[j * 32:(j + 1) * 32]], axis=0).T)

    for L in range(2):
        modT = _c(mod[L].reshape(2, 96, 128).transpose(2, 1, 0).reshape(128, 192))
        gain = _c(f(norm_mix_pre)[L].reshape(16, 128).T)
        qkg = _c(np.stack([np.tile(f(attn_q_norm)[L], 2), np.tile(f(attn_k_norm)[L], 2)], axis=1))
        maps = []
        for j in range(8):
            cj = np.tile(cos[j * 1024:(j + 1) * 1024].T, (4, 1))
            sj = np.tile(sin[j * 1024:(j + 1) * 1024].T, (4, 1))
            maps.append({"xT": shard_T(x, ctxv, j), "modT": modT, "gain": gain, "w_in": _c(w_in[L]), "qkg": qkg,
                         "cosT": _c(cj), "sinT": _c(sj), "cmat": cm1})
        r = _run('k1', build_k1, maps)
        pT = [r[j]["pT"] for j in range(8)]
        lat = np.concatenate([p[:, :1024] for p in pT], axis=1)
        cxp = np.concatenate([p[:, 1024:] for p in pT], axis=1)
        full = np.concatenate([cxp, lat], axis=1)
        kT = np.concatenate([full[512:640].reshape(2, 64, 8448), full[4880:5392].reshape(8, 64, 8448)], axis=0)

        def vt(rows, nh, dv):
            v = full[rows].T.reshape(66, 128, nh, dv)
            return _c(v.transpose(2, 1, 0, 3).reshape(nh, 128, 66 * dv))
        vv = vt(slice(640, 768), 2, 64)
        vd = vt(slice(5392, 5904), 4, 128)
        lam_init = 0.8 - 0.6 * math.exp(-0.3 * L)
        misc = np.zeros((128, 4), np.float32)
        misc[:, 0] = f(df_norm)[L]; misc[:, 1] = lam_init; misc[:, 2] = 1.0 - lam_init
        lamv = _c(np.broadcast_to(f(df_lambda)[L].reshape(1, 256), (128, 256)))
        maps = []
        for j in range(8):
            q = np.concatenate([pT[j][0:512].reshape(8, 64, 1056), pT[j][4368:4880].reshape(8, 64, 1056)], axis=0)
            maps.append({"qT": _c(q), "kT": _c(kT), "vv": vv, "vd": vd, "lamv": lamv, "misc": misc, "ones": ones})
        r = _run('k2', build_k2, maps)
        yaT = [r[j]["yaT"] for j in range(8)]
        ydT = [r[j]["ydT"] for j in range(8)]
        maps = []
        hs_ = f(hy_short)[L]
        for j in range(8):
            rows = [768 + g * 512 + 64 * j for g in range(3)]
            pl = np.stack([lat[r0:r0 + 64] for r0 in rows])
            pc = np.stack([cxp[r0:r0 + 64] for r0 in rows])
            sw = np.stack([hs_[:, g * 512 + 64 * j:g * 512 + 64 * j + 64].T for g in range(3)], axis=1).reshape(64, 9)
            vec = np.zeros((64, 8), np.float32)
            vec[:, 0] = f(hy_b1)[L]; vec[:, 1] = f(hy_b2)[L]; vec[:, 2] = f(hy_b3)[L]; vec[:, 3] = f(hy_freq)[L]
            vec[:, 4] = f(hy_skip)[L][64 * j:64 * j + 64]
            w4 = f(hy_w4)[L]
            w4s = np.concatenate([w4[:, 64 * j:64 * j + 64], w4[:, 512 + 64 * j:512 + 64 * j + 64]], axis=1)
            (zl, wl), (zc, wc) = hyc[j]
            maps.append({"pl": _c(pl), "pc": _c(pc), "sw": _c(sw), "w1": _c(f(hy_w1)[L]),
                         "w23": _c(np.concatenate([f(hy_w2)[L], f(hy_w3)[L]], axis=1)), "w4s": _c(w4s), "vec": vec,
                         "zl": zl, "zc": zc, "wl": wl, "wc": wc, "ident": ident})
        r = _run('k3', build_k3, maps)
        ybf = np.concatenate([r[j]["yb"] for j in range(8)], axis=0)
        yb_lat, yb_ctx = ybf[:, :8192], ybf[:, 8192:]
        maps = []
        ds_ = f(dn_short)[L]
        for j in range(8):
            h, d = j % 4, j // 4
            rows = [2304 + g * 512 + h * 128 for g in range(3)]
            seq = full
            if d == 1:
                seq = np.concatenate([cxp[:, ::-1], lat[:, ::-1]], axis=1)
            pq = np.stack([seq[r0:r0 + 128] for r0 in rows])
            braw = seq[4352 + d * 4 + h].reshape(132, 64).T
            araw = seq[4352 + 8 + d * 4 + h].reshape(132, 64).T
            taps = np.stack([ds_[:, g * 512 + h * 128:g * 512 + (h + 1) * 128].T for g in range(3)], axis=1)
            if d == 1:
                taps = taps[:, :, ::-1]
            scal = np.zeros((128, 2), np.float32)
            scal[:, 0] = f(dn_a_log)[L, d, h]; scal[:, 1] = f(dn_dt_bias)[L, d, h]
            maps.append({"pqkv": _c(pq), "taps": _c(taps.reshape(128, 9)), "graw": _c(np.concatenate([braw, araw], axis=1)),
                         "scal": scal, "cm": cm4})
        r = _run('k4', build_k4, maps)
        of_full = np.concatenate([r[j]["oT"] for j in range(4)], axis=0)
        ob_s = [r[4 + j]["oT"] for j in range(4)]
        ob_full = np.concatenate([np.concatenate([o[:, :256][:, ::-1], o[:, 256:][:, ::-1]], axis=1) for o in ob_s], axis=0)
        gate_lat, gate_ctx = lat[3840:4352], cxp[3840:4352]
        gains = _c(np.concatenate([f(norm_mix_post)[L].reshape(16, 128).T, f(norm_ffn_pre)[L].reshape(16, 128).T], axis=1))
        dng = _c(f(dn_norm)[L].reshape(128, 1))
        maps = []
        for j in range(8):
            sl, sc_ = slice(j * 1024, (j + 1) * 1024), slice(j * 32, (j + 1) * 32)
            ybj = np.concatenate([yb_lat[:, sl], yb_ctx[:, sc_]], axis=1)
            yT = np.concatenate([yaT[j], ybj, np.zeros((512, 1056), np.float32), ydT[j]], axis=0)
            ofj = np.concatenate([of_full[:, 256:][:, sl], of_full[:, :256][:, sc_]], axis=1)
            obj = np.concatenate([ob_full[:, 256:][:, sl], ob_full[:, :256][:, sc_]], axis=1)
            gtj = np.concatenate([gate_lat[:, sl], gate_ctx[:, sc_]], axis=1)
            maps.append({"yT": _c(yT), "xT": shard_T(x, ctxv, j), "w_out": _c(w_out[L]), "modT": modT, "gains": gains,
                         "ones": ones, "ofT": _c(ofj), "obT": _c(obj), "gtT": _c(gtj), "dng": dng})
        r = _run('k5a', build_k5a, maps)
        xm = [r[j]["xmT"] for j in range(8)]
        hh = [r[j]["hT"] for j in range(8)]
        h_lat = np.concatenate([a[:, :1024] for a in hh], axis=1).T
        h_ctx = np.concatenate([a[:, 1024:] for a in hh], axis=1).T
        z1 = np.zeros((1, 2048), np.float32)

        def hp(j):
            la = np.concatenate([h_lat[j * 1024 - 1:j * 1024] if j > 0 else z1, h_lat[j * 1024:(j + 1) * 1024],
                                 h_lat[(j + 1) * 1024:(j + 1) * 1024 + 1] if j < 7 else z1], axis=0)
            cx_ = np.concatenate([h_ctx[j * 32 - 1:j * 32] if j > 0 else z1, h_ctx[j * 32:(j + 1) * 32],
                                  h_ctx[(j + 1) * 32:(j + 1) * 32 + 1] if j < 7 else z1], axis=0)
            return _c(np.concatenate([la, cx_], axis=0).T)
        wcv = _c(ffn_conv[L].reshape(3, 88, 128).transpose(2, 1, 0).reshape(128, 264))
        g5 = _c(f(norm_ffn_post)[L].reshape(16, 128).T)
        maps = [{"hp": hp(j), "xmT": xm[j], "w_up": _c(ffn_up[L]), "wcv": wcv, "w_dn": _c(ffn_down[L]), "modT": modT,
                 "gains": g5, "ones": ones} for j in range(8)]
        r = _run('k5b', build_k5b, maps)
        xo = [r[j]["xoT"] for j in range(8)]
        x = np.ascontiguousarray(np.concatenate([a[:, :1024] for a in xo], axis=1).T)
        ctxv = np.ascontiguousarray(np.concatenate([a[:, 1024:] for a in xo], axis=1).T)
    return x[None].astype(np.float32)
```

```python
import math
import numpy as np
import contextlib
import concourse.bass as bass
import concourse.mybir as mybir
from concourse.bass_utils import run_bass_kernel_spmd

F32 = mybir.dt.float32
BF16 = mybir.dt.bfloat16
AF = mybir.ActivationFunctionType
ALU = mybir.AluOpType
AX = mybir.AxisListType


class View:
    def __init__(self, buf, ap):
        self.buf = buf
        self.ap = ap


class Buf:
    def __init__(self, t, name):
        self.t = t
        self.name = name
        self.wr = {}
        self.rd = {}

    def __getitem__(self, idx):
        return View(self, self.t[idx])

    def v(self, ap):
        return View(self, ap)

    def re(self, pat, **kw):
        return ReView(self, self.t.rearrange(pat, **kw))


class ReView:
    def __init__(self, buf, ap):
        self.buf = buf
        self.ap = ap

    def __getitem__(self, idx):
        return View(self.buf, self.ap[idx])


def _aps(x):
    return x.ap if isinstance(x, View) else x


class Prog:
    NDMASEM = 8

    def __init__(self, nc, stack):
        self.nc = nc
        self.stack = stack
        self.streams = ['pe', 'dve', 'act', 'pool', 'sp']
        self.sems = {}
        self.cnt = {}
        for k in ['pe', 'dve', 'act', 'pool']:
            self.sems[k] = stack.enter_context(nc.semaphore('s_' + k))
            self.cnt[k] = 0
        self.dq = {}
        for q in ['sp', 'act', 'pool']:
            keys = []
            for i in range(self.NDMASEM):
                k = 'd_%s_%d' % (q, i)
                self.sems[k] = stack.enter_context(nc.semaphore(k))
                self.cnt[k] = 0
                keys.append(k)
            self.dq[q] = [keys, 0]
        self.seen = {e: {} for e in self.streams}
        self.rec = {e: [] for e in self.streams}
        self.nbuf = 0
        self.dmarr = 0
        self.prefix = ''

    def sbuf(self, shape, dt, name=None):
        self.nbuf += 1
        name = self.prefix + (name or ('sb%d' % self.nbuf))
        t = self.stack.enter_context(self.nc.sbuf_tensor(name, list(shape), dt))
        return Buf(t, name)

    def psum(self, shape, dt, name=None):
        self.nbuf += 1
        name = self.prefix + (name or ('ps%d' % self.nbuf))
        t = self.stack.enter_context(self.nc.psum_tensor(name, list(shape), dt))
        return Buf(t, name)

    def dram(self, name, shape, dt, kind):
        t = self.nc.dram_tensor(name, list(shape), dt, kind=kind)
        return Buf(t.ap(), name)

    def _waits(self, e, reads, writes, extra=(), nowaw=False):
        need = {}

        def add(dep):
            if dep is None:
                return
            k, c = dep
            if need.get(k, 0) < c:
                need[k] = c
        raw_self = 0
        for b in reads:
            for k, c in b.wr.items():
                add((k, c))
                if k == e:
                    raw_self = max(raw_self, c)
        for b in writes:
            if not nowaw:
                for k, c in b.wr.items():
                    add((k, c))
            for k, c in b.rd.items():
                add((k, c))
        for d in extra:
            add(d)
        ws = []
        if e in need:
            del need[e]
        if raw_self > 0 and e != 'pe':
            need[e] = raw_self
        for k, c in need.items():
            if self.seen[e].get(k, 0) >= c:
                continue
            ws.append((self.sems[k], c))
            self.seen[e][k] = c
        return ws

    def _mark(self, k, c, reads, writes, nowaw):
        for b in reads:
            b.rd[k] = c
        for b in writes:
            if nowaw:
                b.wr[k] = c
            else:
                b.wr = {k: c}
                b.rd = {}

    def op(self, e, fn, reads=(), writes=(), nowaw=False):
        reads = [r.buf if isinstance(r, View) else r for r in reads if r is not None and not isinstance(r, (int, float))]
        writes = [w.buf if isinstance(w, View) else w for w in writes]
        ws = self._waits(e, reads, writes, nowaw=nowaw)
        self.cnt[e] += 1
        self.rec[e].append((ws, fn, self.sems[e], 1))
        self._mark(e, self.cnt[e], reads, writes, nowaw)

    def dma(self, out, in_, q=None, nowaw=False, **kw):
        if q is None:
            q = 'sp'
        reads = [in_.buf]
        writes = [out.buf]
        keys, idx = self.dq[q]
        k = keys[idx % len(keys)]
        self.dq[q][1] += 1
        prev = (k, self.cnt[k]) if self.cnt[k] > 0 else None
        ws = self._waits(q, reads, writes, extra=(prev,) if prev else (), nowaw=nowaw)
        self.cnt[k] += 16
        oa, ia = out.ap, in_.ap
        self.rec[q].append((ws, (lambda e: e.dma_start(out=oa, in_=ia, **kw)), self.sems[k], 16))
        self._mark(k, self.cnt[k], reads, writes, nowaw)

    def mm(self, out, lhsT, rhs, start=True, stop=True):
        o, l, r = out.ap, lhsT.ap, rhs.ap
        self.op('pe', lambda e: e.matmul(o, lhsT=l, rhs=r, start=start, stop=stop), reads=[lhsT, rhs], writes=[out])

    def transpose(self, out, in_, ident):
        o, i, d = out.ap, in_.ap, ident.ap
        self.op('pe', lambda e: e.transpose(o, i, d), reads=[in_, ident], writes=[out])

    def act(self, out, in_, func, scale=1.0, bias=None, eng='act', accum_out=None, nowaw=False):
        o, i = out.ap, in_.ap
        s = _aps(scale)
        b = _aps(bias)
        kw = {}
        if bias is not None:
            kw['bias'] = b
        if accum_out is not None:
            kw['accum_out'] = accum_out.ap
        wr = [out] + ([accum_out] if accum_out is not None else [])
        self.op('act', lambda e: e.activation(out=o, in_=i, func=func, scale=s, **kw),
                reads=[in_, scale if isinstance(scale, View) else None, bias if isinstance(bias, View) else None], writes=wr, nowaw=nowaw)

    def tt(self, out, in0, in1, op, eng='dve'):
        o, a, b = out.ap, in0.ap, in1.ap
        self.op(eng, lambda e: e.tensor_tensor(out=o, in0=a, in1=b, op=op), reads=[in0, in1], writes=[out])

    def ts(self, out, in0, s1, op0, s2=None, op1=None, eng='dve', accum_out=None, nowaw=False):
        o, a = out.ap, in0.ap
        x1, x2 = _aps(s1), _aps(s2)
        kw = {}
        if op1 is not None:
            kw['op1'] = op1
        if accum_out is not None:
            kw['accum_out'] = accum_out.ap
        wr = [out] + ([accum_out] if accum_out is not None else [])
        self.op(eng, lambda e: e.tensor_scalar(out=o, in0=a, scalar1=x1, scalar2=x2, op0=op0, **kw),
                reads=[in0, s1 if isinstance(s1, View) else None, s2 if isinstance(s2, View) else None], writes=wr, nowaw=nowaw)

    def stt(self, out, in0, scalar, in1, op0, op1):
        o, a, b = out.ap, in0.ap, in1.ap
        s = _aps(scalar)
        self.op('dve', lambda e: e.scalar_tensor_tensor(out=o, in0=a, scalar=s, in1=b, op0=op0, op1=op1),
                reads=[in0, in1, scalar if isinstance(scalar, View) else None], writes=[out])

    def copy(self, out, in_, eng='dve', nowaw=False):
        o, i = out.ap, in_.ap
        self.op(eng, lambda e: e.tensor_copy(out=o, in_=i), reads=[in_], writes=[out], nowaw=nowaw)

    def recip(self, out, in_):
        o, i = out.ap, in_.ap
        self.op('dve', lambda e: e.reciprocal(out=o, in_=i), reads=[in_], writes=[out])

    def memset(self, out, val, eng='dve'):
        o = out.ap
        self.op(eng, lambda e: e.memset(o, val), reads=[], writes=[out])

    def finish(self, bufs, e='sp'):
        ws = self._waits(e, bufs, [])
        self.rec[e].append((ws, None, None, 0))
        rec = self.rec

        def replay(lst):
            def f(eng):
                for ws, fn, sem, inc in lst:
                    for (s_, c_) in ws:
                        eng.wait_ge(s_, c_)
                    if fn is not None:
                        fn(eng).then_inc(sem, inc)
            return f
        with self.nc.Block() as block:
            if rec['sp']:
                block.sync(replay(rec['sp']))
            if rec['pe']:
                block.tensor(replay(rec['pe']))
            if rec['dve']:
                block.vector(replay(rec['dve']))
            if rec['act']:
                block.scalar(replay(rec['act']))
            if rec['pool']:
                block.gpsimd(replay(rec['pool']))
        self.rec = {e_: [] for e_ in self.streams}


_FUSE = {'nc': None, 'P': None}


def new_prog():
    if _FUSE['nc'] is not None:
        P = _FUSE['P']
        inner = contextlib.ExitStack()
        P.stack = inner
        if hasattr(P, '_epsb'):
            del P._epsb
        return _FUSE['nc'], inner, P
    nc = bass.Bass("TRN2", target_bir_lowering=False)
    st = contextlib.ExitStack()
    return nc, st, Prog(nc, st)


def build_mix():
    nc = bass.Bass("TRN2", target_bir_lowering=False)
    st = contextlib.ExitStack()
    with st:
        P = Prog(nc, st)
        _FUSE['nc'], _FUSE['P'] = nc, P
        try:
            for pre_, fn_ in (('a_', build_k2), ('b_', build_k3), ('c_', build_k4)):
                P.prefix = pre_
                fn_()
        finally:
            _FUSE['nc'], _FUSE['P'] = None, None
    return nc


D = 2048
NT = 1056
NL = 1024
IN_COLS = 5904
EPS = 1e-6


def build_k0():
    nc, st, P = new_prog()
    with st:
        cc = P.dram("cc", [128, 32], F32, "ExternalInput")
        w = P.dram("w", [2048, 3072], F32, "ExternalInput")
        b2 = P.dram("b2", [2, 3072], F32, "ExternalInput")
        mod = P.dram("mod", [2, 3072], F32, "ExternalOutput")
        cs = P.sbuf([128, 32], F32)
        ca = P.sbuf([128, 32], F32)
        bs = P.sbuf([2, 3072], F32)
        ms = P.sbuf([2, 3072], F32)
        wb = [P.sbuf([128, 3072], F32) for _ in range(4)]
        ps = [P.psum([128, 512], F32) for _ in range(6)]
        P.dma(cs[:], cc[:])
        P.dma(bs[:], b2[:])
        P.act(ca[:], cs[:], AF.Silu)
        for kc in range(16):
            wt = wb[kc % 4]
            P.dma(wt[:], w[kc * 128:(kc + 1) * 128, :])
            for g in range(6):
                P.mm(ps[g][0:2, :], ca[:, 2 * kc:2 * kc + 2], wt[:, g * 512:(g + 1) * 512], start=(kc == 0), stop=(kc == 15))
        for g in range(6):
            P.tt(ms[0:2, g * 512:(g + 1) * 512], ps[g][0:2, :], bs[0:2, g * 512:(g + 1) * 512], ALU.add)
        P.dma(mod[:], ms[:])
        P.finish([mod])
    return nc


def k1_groups():
    g = []
    for m in range(34):
        c0 = m * 128
        kind = None
        if m < 4:
            kind = 'gq_q'
        elif m == 4:
            kind = 'gq_k'
        g.append((c0, 128, kind))
    g.append((4352, 16, None))
    for m in range(12):
        c0 = 4368 + m * 128
        g.append((c0, 128, 'rope' if m < 8 else None))
    return g


def build_k1():
    nc, st, P = new_prog()
    with st:
        xT = P.dram("xT", [D, NT], F32, "ExternalInput")
        modT = P.dram("modT", [128, 96 * 2], F32, "ExternalInput")
        gain = P.dram("gain", [128, 16], F32, "ExternalInput")
        w_in = P.dram("w_in", [D, IN_COLS], F32, "ExternalInput")
        qkg = P.dram("qkg", [128, 2], F32, "ExternalInput")
        cosT = P.dram("cosT", [128, NL], F32, "ExternalInput")
        sinT = P.dram("sinT", [128, NL], F32, "ExternalInput")
        cmat = P.dram("cmat", [128, 3 * 128], F32, "ExternalInput")
        pT = P.dram("pT", [IN_COLS, NT], F32, "ExternalOutput")

        xs = P.sbuf([128, 16, NT], F32, "xs")
        hT = P.sbuf([128, 16, NT], BF16, "hT")
        mods = P.sbuf([128, 96, 2], F32, "mods")
        gs = P.sbuf([128, 16], F32, "gs")
        qk = P.sbuf([128, 2], F32, "qk")
        cs_ = P.sbuf([128, NL], F32, "cos")
        sn_ = P.sbuf([128, NL], F32, "sin")
        cm = P.sbuf([128, 384], F32, "cm")
        A = P.sbuf([128, 16, 2], F32, "A")
        rstd = P.sbuf([128, NT], F32, "rstd")
        tmp = [P.sbuf([128, NT], F32, "tmp%d" % i) for i in range(2)]
        banks = [P.psum([128, 512], F32, "bank%d" % i) for i in range(8)]

        xTr = xT.re("(kc k) n -> k kc n", k=128)
        for kc in range(16):
            P.dma(xs[:, kc, :], xTr[:, kc, :], nowaw=True)
        P.dma(mods[:], modT.re("k (c r) -> k c r", r=2)[:, :, :])
        P.dma(gs[:], gain[:])
        P.dma(qk[:], qkg[:])
        P.dma(cs_[:], cosT[:])
        P.dma(sn_[:], sinT[:])
        P.dma(cm[:], cmat[:])
        ones = cm[:, 0:128]
        bo = cm[:, 128:256]
        RT = cm[:, 256:384]

        for r in range(2):
            P.ts(A[:, :, r], mods[:, 16:32, r], 1.0, ALU.add)
            P.tt(A[:, :, r], A[:, :, r], gs[:], ALU.mult)
        for kc in range(16):
            sq = tmp[kc % 2]
            P.act(sq[:], xs[:, kc, :], AF.Square)
            for tg in range(3):
                P.mm(banks[tg][:, 0:352], ones, sq[:, tg * 352:(tg + 1) * 352], start=(kc == 0), stop=(kc == 15))
        for tg in range(3):
            P.act(rstd[:, tg * 352:(tg + 1) * 352], banks[tg][:, 0:352], AF.Sqrt, scale=1.0 / D, bias=EPSB(P))
        P.recip(rstd[:], rstd[:])
        for kc in range(16):
            t = tmp[kc % 2]
            P.tt(t[:], xs[:, kc, :], rstd[:], ALU.mult)
            P.act(hT[:, kc, 0:NL], t[:, 0:NL], AF.Identity, scale=A[:, kc, 0:1], bias=mods[:, kc, 0:1])
            P.ts(hT[:, kc, NL:NT], t[:, NL:NT], A[:, kc, 1:2], ALU.mult, mods[:, kc, 1:2], ALU.add)

        wst = [P.sbuf([128, 16, 256], F32, "wst%d" % i) for i in range(2)]
        wbf = [P.sbuf([128, 16, 256], BF16, "wbf%d" % i) for i in range(2)]
        pout = [P.sbuf([128, NT], F32, "pout%d" % i) for i in range(3)]
        w_r = w_in.re("(kc k) c -> k kc c", k=128)
        groups = k1_groups()
        slabs = []
        i = 0
        while i < len(groups):
            c0, n, _ = groups[i]
            if n == 128 and i + 1 < len(groups) and groups[i + 1][1] == 128 and groups[i + 1][0] == c0 + 128:
                slabs.append((c0, 256, [groups[i], groups[i + 1]]))
                i += 2
            else:
                slabs.append((c0, n, [groups[i]]))
                i += 1
        bi = 0
        gi = 0
        for si, (c0, wn, grs) in enumerate(slabs):
            ws_, wb_ = wst[si % 2], wbf[si % 2]
            for kh in range(2):
                P.dma(ws_[:, kh * 8:(kh + 1) * 8, 0:wn], w_r[:, kh * 8:(kh + 1) * 8, c0:c0 + wn], nowaw=True)
            P.copy(wb_[:, :, 0:wn], ws_[:, :, 0:wn], eng=('dve' if si % 2 == 0 else 'pool'))
            for (gc0, gn, kind) in grs:
                off = gc0 - c0
                po = pout[gi % 3]
                gi += 1
                for tg in range(3):
                    bk = banks[3 + (bi % 5)]
                    bi += 1
                    for kc in range(16):
                        P.mm(bk[0:gn, 0:352], wb_[:, kc, off:off + gn], hT[:, kc, tg * 352:(tg + 1) * 352], start=(kc == 0), stop=(kc == 15))
                    P.act(po[0:gn, tg * 352:(tg + 1) * 352], bk[0:gn, 0:352], AF.Copy)
                if kind in ('gq_q', 'gq_k'):
                    gcol = qk[:, 0:1] if kind == 'gq_q' else qk[:, 1:2]
                    sq = tmp[0]
                    P.act(sq[:], po[:], AF.Square)
                    r2 = tmp[1]
                    for tg in range(3):
                        P.mm(banks[tg][:, 0:352], bo, sq[:, tg * 352:(tg + 1) * 352])
                        P.act(r2[:, tg * 352:(tg + 1) * 352], banks[tg][:, 0:352], AF.Sqrt, scale=1.0 / 64, bias=EPSB(P))
                    P.recip(r2[:], r2[:])
                    P.stt(po[:], po[:], gcol, r2[:], ALU.mult, ALU.mult)
                if kind is not None:
                    t1 = tmp[0]
                    for hh in range(2):
                        P.mm(banks[hh][:, 0:512], RT, po[:, hh * 512:(hh + 1) * 512])
                    P.tt(t1[:, 0:NL], po[:, 0:NL], cs_[:], ALU.mult)
                    for hh in range(2):
                        P.tt(po[:, hh * 512:(hh + 1) * 512], banks[hh][:, 0:512], sn_[:, hh * 512:(hh + 1) * 512], ALU.mult)
                    P.tt(po[:, 0:NL], po[:, 0:NL], t1[:, 0:NL], ALU.add)
                P.dma(pT[gc0:gc0 + gn, :], po[0:gn, :])
        P.finish([pT])
    return nc


def EPSB(P):
    if not hasattr(P, '_epsb'):
        P._epsb = P.sbuf([128, 1], F32, "epsb")
        P.memset(P._epsb[:], EPS)
    return P._epsb[:]


def rope_tables():
    rows = 8192 // 64
    row = np.repeat(np.arange(rows, dtype=np.float32), 64)
    col = np.tile(np.arange(64, dtype=np.float32), rows)
    n_freq = 16
    inv = (np.float32(10000.0) ** (-np.arange(n_freq, dtype=np.float32) / n_freq)).astype(np.float32)
    ang = np.concatenate([row[:, None] * inv, col[:, None] * inv], axis=-1).astype(np.float32)
    return np.cos(ang).astype(np.float32), np.sin(ang).astype(np.float32)


def const_mats():
    ones = np.ones((128, 128), np.float32)
    bo = np.zeros((128, 128), np.float32)
    bo[:64, :64] = 1
    bo[64:, 64:] = 1
    RT = np.zeros((128, 128), np.float32)
    for m in range(128):
        if (m % 64) < 32:
            RT[m + 32, m] = -1.0
        else:
            RT[m - 32, m] = 1.0
    return np.concatenate([ones, bo, RT], axis=1)


NT = 1056
NL = 1024
NK = 8448
KT = 66


def build_k2():
    nc, st, P = new_prog()
    with st:
        qT = P.dram("qT", [16, 64, NT], F32, "ExternalInput")
        kT = P.dram("kT", [10, 64, NK], F32, "ExternalInput")
        vv = P.dram("vv", [2, 128, KT * 64], F32, "ExternalInput")
        vd = P.dram("vd", [4, 128, KT * 128], F32, "ExternalInput")
        lamv = P.dram("lamv", [128, 256], F32, "ExternalInput")
        misc = P.dram("misc", [128, 4], F32, "ExternalInput")
        onesd = P.dram("ones", [128, 128], F32, "ExternalInput")
        yaT = P.dram("yaT", [512, NT], F32, "ExternalOutput")
        ydT = P.dram("ydT", [512, NT], F32, "ExternalOutput")

        stage = [P.sbuf([128, 2112], F32, "stage%d" % i) for i in range(2)]
        kbf = [P.sbuf([64, NK], BF16, "kbf%d" % i) for i in range(2)]
        qbf = [P.sbuf([64, NT], BF16, "qbf%d" % i) for i in range(2)]
        vbf = [P.sbuf([128, KT * 128], BF16, "vbf%d" % i) for i in range(2)]
        pb = [P.sbuf([128, 1024], BF16, "pb%d" % i) for i in range(3)]
        ones_f = P.sbuf([128, 128], F32, "ones_f")
        ones_b = P.sbuf([128, 128], BF16, "ones_b")
        lv = P.sbuf([128, 256], F32, "lv")
        ms = P.sbuf([128, 4], F32, "ms")
        lam = P.sbuf([128, 4], F32, "lam")
        rd = P.sbuf([128, 512], F32, "rd")
        accA = P.sbuf([128, 512], F32, "accA")
        accB = P.sbuf([128, 512], F32, "accB")
        osb = [P.sbuf([128, NT], F32, "osb%d" % i) for i in range(3)]
        sq = P.sbuf([128, NT], F32, "sq")
        sbank = [P.psum([128, 1024], F32, "sbank%d" % i) for i in range(2)]
        obank = [P.psum([128, 512], F32, "obank%d" % i) for i in range(2)]
        dbank = [P.psum([128, 512], F32, "dbank%d" % i) for i in range(1)]
        nbank = P.psum([128, 512], F32, "nbank")

        P.dma(ones_f[:], onesd[:])
        P.copy(ones_b[:], ones_f[:])
        P.dma(lv[:], lamv[:])
        P.dma(ms[:], misc[:])
        pr = P.sbuf([128, 128], F32, "pr")
        P.tt(pr[:, 0:64], lv[:, 0:64], lv[:, 64:128], ALU.mult)
        P.tt(pr[:, 64:128], lv[:, 128:192], lv[:, 192:256], ALU.mult)
        o0, i0 = lam[:, 0:1].ap, pr[:, 0:64].ap
        P.op('dve', lambda e: e.tensor_reduce(out=o0, in_=i0, axis=AX.X, op=ALU.add), reads=[pr], writes=[lam])
        o1, i1 = lam[:, 1:2].ap, pr[:, 64:128].ap
        P.op('dve', lambda e: e.tensor_reduce(out=o1, in_=i1, axis=AX.X, op=ALU.add), reads=[pr], writes=[lam])
        P.act(lam[:, 0:2], lam[:, 0:2], AF.Exp)
        P.tt(lam[:, 2:3], lam[:, 0:1], lam[:, 1:2], ALU.subtract)
        P.tt(lam[:, 2:3], lam[:, 2:3], ms[:, 1:2], ALU.add)
        P.ts(lam[:, 3:4], lam[:, 2:3], -1.0, ALU.mult)
        gsc = P.sbuf([128, 2], F32, "gsc")
        P.tt(gsc[:, 0:1], ms[:, 0:1], ms[:, 2:3], ALU.mult)

        cnt = {'s': 0, 'p': 0, 'a': 0, 'st': 0, 'o': 0}

        def load_cast(dst_view_fn, src_view_fn, np_, ncols_total, piece=2112):
            c = 0
            while c < ncols_total:
                n = min(piece, ncols_total - c)
                sg = stage[cnt['st'] % 2]
                cnt['st'] += 1
                P.dma(sg[0:np_, 0:n], src_view_fn(c, n))
                P.copy(dst_view_fn(c, n), sg[0:np_, 0:n], eng='pool')
                c += n

        def attend(kb, qb, vb, dv, ob):
            for (q0, qn, nkt) in [(0, 512, KT), (512, 512, KT), (1024, 32, 2)]:
                a = cnt['a'] % 2
                cnt['a'] += 1
                ob_, db_ = obank[a], dbank[0]
                def issue_s(pp):
                    sb__ = sbank[cnt['s'] % 2]
                    cnt['s'] += 1
                    for hh_ in range(2):
                        kt_ = 2 * pp + hh_
                        P.mm(sb__[:, hh_ * 512:hh_ * 512 + qn], kb[:, kt_ * 128:(kt_ + 1) * 128], qb[:, q0:q0 + qn])
                    return sb__
                npair = nkt // 2
                nxt = issue_s(0)
                for pp in range(npair):
                    sb_ = nxt
                    if pp + 1 < npair:
                        nxt = issue_s(pp + 1)
                    pt = pb[cnt['p'] % 3]
                    cnt['p'] += 1
                    if qn == 512:
                        P.act(pt[:, 0:1024], sb_[:, 0:1024], AF.Exp, scale=0.125)
                    else:
                        for hh_ in range(2):
                            P.act(pt[:, hh_ * 512:hh_ * 512 + qn], sb_[:, hh_ * 512:hh_ * 512 + qn], AF.Exp, scale=0.125)
                    for hh_ in range(2):
                        kt = 2 * pp + hh_
                        pv = pt[:, hh_ * 512:hh_ * 512 + qn]
                        P.mm(ob_[0:dv, 0:qn], vb[:, kt * dv:(kt + 1) * dv], pv, start=(kt == 0), stop=(kt == nkt - 1))
                        if kt == 0:
                            P.copy(accA[:, 0:qn], pv)
                        elif kt == 1:
                            P.copy(accB[:, 0:qn], pv, eng='pool')
                        elif kt % 4 != 1:
                            P.tt(accA[:, 0:qn], accA[:, 0:qn], pv, ALU.add)
                        else:
                            P.tt(accB[:, 0:qn], accB[:, 0:qn], pv, ALU.add, eng='pool')
                P.tt(accA[:, 0:qn], accA[:, 0:qn], accB[:, 0:qn], ALU.add)
                P.mm(db_[0:dv, 0:qn], ones_f[:, 0:dv], accA[:, 0:qn])
                P.recip(rd[0:dv, 0:qn], db_[0:dv, 0:qn])
                P.tt(ob[0:dv, q0:q0 + qn], ob_[0:dv, 0:qn], rd[0:dv, 0:qn], ALU.mult)

        units = []

        def mk_gqa(g, hh):
            h = g * 4 + hh
            kb, vb, qb = kbf[g % 2], vbf[g % 2], qbf[h % 2]

            def load():
                if hh == 0:
                    load_cast(lambda c, n: kb[:, c:c + n], lambda c, n: kT[g, :, c:c + n], 64, NK)
                    load_cast(lambda c, n: vb[:, c:c + n], lambda c, n: vv[g, :, c:c + n], 128, KT * 64)
                load_cast(lambda c, n: qb[:, c:c + n], lambda c, n: qT[h, :, c:c + n], 64, NT)

            def comp():
                ob = osb[cnt['o'] % 3]
                cnt['o'] += 1
                attend(kb, qb, vb, 64, ob)
                P.dma(yaT[h * 64:(h + 1) * 64, :], ob[0:64, :])
            return load, comp

        dstate = {}

        def mk_diff(h, m):
            u = h * 2 + m
            vb, kb, qb = vbf[h % 2], kbf[u % 2], qbf[u % 2]

            def load():
                if m == 0:
                    load_cast(lambda c, n: vb[:, c:c + n], lambda c, n: vd[h, :, c:c + n], 128, KT * 128)
                load_cast(lambda c, n: kb[:, c:c + n], lambda c, n: kT[2 + u, :, c:c + n], 64, NK)
                load_cast(lambda c, n: qb[:, c:c + n], lambda c, n: qT[8 + u, :, c:c + n], 64, NT)

            def comp():
                ob = osb[cnt['o'] % 3]
                cnt['o'] += 1
                attend(kb, qb, vb, 128, ob)
                if m == 0:
                    dstate['o0'] = ob
                    return
                o0_, o1_ = dstate['o0'], ob
                P.stt(o0_[:], o1_[:], lam[:, 3:4], o0_[:], ALU.mult, ALU.add)
                P.act(sq[:], o0_[:], AF.Square)
                for tg in range(3):
                    P.mm(nbank[:, 0:352], ones_f[:], sq[:, tg * 352:(tg + 1) * 352])
                    P.act(o1_[:, tg * 352:(tg + 1) * 352], nbank[:, 0:352], AF.Sqrt, scale=1.0 / 128, bias=EPSB(P))
                P.recip(o1_[:], o1_[:])
                P.stt(o0_[:], o0_[:], gsc[:, 0:1], o1_[:], ALU.mult, ALU.mult)
                P.dma(ydT[h * 128:(h + 1) * 128, :], o0_[:])
            return load, comp

        for g in range(2):
            for hh in range(4):
                units.append(mk_gqa(g, hh))
        for h in range(4):
            for m in range(2):
                units.append(mk_diff(h, m))
        units[0][0]()
        for ui in range(len(units)):
            if ui + 1 < len(units):
                units[ui + 1][0]()
            units[ui][1]()
        P.finish([yaT, ydT])
    return nc


LL = 8192
LC = 256


def hy_consts(L, core):
    nb = L // 128
    cb = np.arange(2 * nb)[:, None]
    jp = np.arange(128)[None, :]
    b = cb - nb
    n_f = 128 * b + 127 - jp
    n_b = 128 * (-b) - 127 + jp
    n = np.where(b >= 0, n_f, n_b)
    valid = (n >= 0) & (n < L)
    n = np.where(valid, n, 0).astype(np.float32)
    t = (n / np.float32(L - 1)).astype(np.float32)
    w = (np.float32(2.0 * math.pi) * n / np.float32(L)).astype(np.float32)
    f = np.linspace(1e-4, 15, 16, dtype=np.float32)
    z = np.concatenate([t[..., None], np.cos(f * w[..., None]), -np.sin(f * w[..., None])], axis=-1).astype(np.float32)
    zT = np.ascontiguousarray(z.reshape(2 * nb * 128, 33).T)
    min_decay = math.log(1e-2) / 1.5
    max_decay = math.log(1e-2) / 0.3
    deltas = np.linspace(min_decay, max_decay, 512, dtype=np.float32)[core * 64:(core + 1) * 64]
    win = np.exp(-t[..., None] * np.abs(deltas)).astype(np.float32) * valid[..., None]
    win = np.ascontiguousarray(win.transpose(1, 0, 2).reshape(128, 2 * nb * 64)).astype(np.float32)
    return zT, win


def build_k3():
    nc, st, P = new_prog()
    with st:
        pl = P.dram("pl", [3, 64, LL], F32, "ExternalInput")
        pc = P.dram("pc", [3, 64, LC], F32, "ExternalInput")
        swd = P.dram("sw", [64, 9], F32, "ExternalInput")
        w1d = P.dram("w1", [33, 64], F32, "ExternalInput")
        w23d = P.dram("w23", [64, 128], F32, "ExternalInput")
        w4d = P.dram("w4s", [64, 128], F32, "ExternalInput")
        vecd = P.dram("vec", [64, 8], F32, "ExternalInput")
        zld = P.dram("zl", [33, 2 * 64 * 128], F32, "ExternalInput")
        zcd = P.dram("zc", [33, 2 * 2 * 128], F32, "ExternalInput")
        wld = P.dram("wl", [128, 128 * 64], F32, "ExternalInput")
        wcd = P.dram("wc", [128, 4 * 64], F32, "ExternalInput")
        identd = P.dram("ident", [128, 128], F32, "ExternalInput")
        yb = P.dram("yb", [64, LL + LC], F32, "ExternalOutput")
        upl_h = nc.dram_tensor("upl", [64, LL + 256], BF16, kind="Internal")
        upc_h = nc.dram_tensor("upc", [64, LC + 256], BF16, kind="Internal")
        upl = Buf(upl_h.ap(), "upl")
        upc = Buf(upc_h.ap(), "upc")

        sw = P.sbuf([64, 9], F32, "sw_s")
        w1 = P.sbuf([33, 64], F32, "w1_s")
        w23 = P.sbuf([64, 128], F32, "w23_s")
        w4 = P.sbuf([64, 128], F32, "w4_s")
        vec = P.sbuf([64, 8], F32, "vec_s")
        sc = P.sbuf([64, 4], F32, "sc_s")
        ident = P.sbuf([128, 128], F32, "ident_s")
        zero = P.sbuf([64, 136], BF16, "zero_s")
        for d, s in [(sw, swd), (w1, w1d), (w23, w23d), (w4, w4d), (vec, vecd), (ident, identd)]:
            P.dma(d[:], s[:])
        P.memset(zero[:], 0.0)
        P.ts(sc[:, 0:1], vec[:, 3:4], 1.0 / 3.0, ALU.mult)
        for k in range(3):
            P.tt(sc[:, 1 + k:2 + k], vec[:, k:k + 1], sc[:, 0:1], ALU.mult)

        PIECE = 2048
        pg = P.sbuf([64, 3, PIECE + 2], F32, "pg")
        cg = P.sbuf([64, 3, PIECE], F32, "cg")
        ub = P.sbuf([64, PIECE], BF16, "ub")
        Hm = P.sbuf([128, 64, 128], BF16, "Hm")
        ush = [P.sbuf([128, 65 * 128], BF16, "ush%d" % i) for i in range(2)]
        ytok = P.sbuf([128, 64, 64], F32, "ytok")
        yconv = P.sbuf([64, LL], F32, "yconv")
        zt = [P.sbuf([33, 512], F32, "zt%d" % i) for i in range(2)]
        hs = [P.sbuf([64, 512], F32, "hs%d" % i) for i in range(3)]
        s2 = P.sbuf([64, 512], F32, "s2")
        wp = [P.sbuf([128, 4, 64], F32, "wp%d" % i) for i in range(2)]
        fb = [P.psum([128, 512], F32, "fb%d" % i) for i in range(3)]
        hb = [P.psum([128, 512], F32, "hb%d" % i) for i in range(2)]
        yk = [P.psum([128, 512], F32, "yk%d" % i) for i in range(2)]
        tb = P.psum([128, 512], F32, "tb")

        def conv_piece(src, L, q, piece):
            lo = q * piece - 1
            hi = (q + 1) * piece + 1
            clo, chi = max(lo, 0), min(hi, L)
            if clo != lo or chi != hi:
                P.memset(pg[:], 0.0)
            for g in range(3):
                P.dma(pg[:, g, clo - lo:chi - lo], src[g, :, clo:chi])
            for g in range(3):
                P.ts(cg[:, g, 0:piece], pg[:, g, 1:1 + piece], sw[:, g * 3 + 1:g * 3 + 2], ALU.mult)
                P.stt(cg[:, g, 0:piece], pg[:, g, 0:piece], sw[:, g * 3:g * 3 + 1], cg[:, g, 0:piece], ALU.mult, ALU.add)
                P.stt(cg[:, g, 0:piece], pg[:, g, 2:2 + piece], sw[:, g * 3 + 2:g * 3 + 3], cg[:, g, 0:piece], ALU.mult, ALU.add)
            P.tt(cg[:, 1, 0:piece], cg[:, 1, 0:piece], cg[:, 2, 0:piece], ALU.mult)

        def sin3(out, ps, k):
            P.act(out[:], ps[0:64, :], AF.Sin, scale=sc[:, 0:1], bias=sc[:, 1 + k:2 + k])
            P.tt(s2[:], out[:], out[:], ALU.mult)
            P.ts(s2[:], s2[:], -4.0, ALU.mult, 3.0, ALU.add)
            P.tt(out[:], s2[:], out[:], ALU.mult)

        def run_seq(src, L, up, up_h, zd, wd, out0):
            nb = L // 128
            piece = min(L, PIECE)
            npieces = L // piece
            W = L + 256
            P.dma(up[:, 0:127], zero[:, 0:127])
            P.dma(up[:, 127 + L:W], zero[:, 0:129])
            for q in range(npieces):
                conv_piece(src, L, q, piece)
                P.copy(ub[:, 0:piece], cg[:, 1, 0:piece])
                P.dma(up[:, 127 + q * piece:127 + (q + 1) * piece], ub[:, 0:piece])
            ntile = (2 * nb * 128) // 512
            for ti in range(ntile):
                z_ = zt[ti % 2]
                P.dma(z_[:], zd[:, ti * 512:(ti + 1) * 512])
                P.mm(fb[0][0:64, :], w1[:, :], z_[:, :])
                sin3(hs[0], fb[0], 0)
                P.mm(fb[1][0:64, :], w23[:, 0:64], hs[0][:, :])
                sin3(hs[1], fb[1], 1)
                P.mm(fb[2][0:64, :], w23[:, 64:128], hs[1][:, :])
                sin3(hs[2], fb[2], 2)
                hbk = hb[ti % 2]
                for k in range(4):
                    cbi = ti * 4 + k
                    wcol = 0 if cbi >= nb else 64
                    P.mm(hbk[:, k * 64:(k + 1) * 64], hs[2][:, k * 128:(k + 1) * 128], w4[:, wcol:wcol + 64])
                wp_ = wp[ti % 2]
                P.dma(wp_[:], wd.re("p (b c) -> p b c", c=64)[:, ti * 4:(ti + 1) * 4, :])
                hv = View(hbk, hbk.t[:, 0:256].rearrange("p (k c) -> p k c", k=4))
                ov = View(Hm, Hm.t[:, :, ti * 4:(ti + 1) * 4].rearrange("p c k -> p k c"))
                P.tt(ov, hv, wp_[:], ALU.mult)
            ncol = (nb + 1) * 128
            for c in range(64):
                us = ush[c % 2]
                src_ap = bass.AP(up_h, c * W, [[1, 128], [1, ncol]])
                P.dma(us[:, 0:ncol], View(up, src_ap))
                ykb = yk[(c // 8) % 2]
                c8 = c % 8
                for m in range(nb + 1):
                    P.mm(ykb[:, c8 * nb:(c8 + 1) * nb], us[:, m * 128:(m + 1) * 128], Hm[:, c, nb - m:2 * nb - m],
                         start=(m == 0), stop=(m == nb))
                if c8 == 7:
                    c0 = c - 7
                    iv = View(ykb, ykb.t[:, 0:8 * nb].rearrange("p (c a) -> p c a", c=8))
                    ov = View(ytok, ytok.t[:, 0:nb, c0:c0 + 8].rearrange("p a c -> p c a"))
                    P.act(ov, iv, AF.Copy)
            for a in range(nb):
                k = a % 4
                P.transpose(tb[0:64, k * 128:(k + 1) * 128], ytok[:, a, :], ident[:])
                if k == 3 or a == nb - 1:
                    a0 = a - k
                    P.act(yconv[:, a0 * 128:(a + 1) * 128], tb[0:64, 0:(k + 1) * 128], AF.Copy)
            for q in range(npieces):
                conv_piece(src, L, q, piece)
                P.stt(cg[:, 1, 0:piece], cg[:, 1, 0:piece], vec[:, 4:5], yconv[:, q * piece:(q + 1) * piece], ALU.mult, ALU.add)
                P.tt(cg[:, 0, 0:piece], cg[:, 0, 0:piece], cg[:, 1, 0:piece], ALU.mult)
                P.dma(yb[:, out0 + q * piece:out0 + (q + 1) * piece], cg[:, 0, 0:piece])

        run_seq(pl, LL, upl, upl_h, zld, wld, 0)
        run_seq(pc, LC, upc, upc_h, zcd, wcd, LL)
        P.finish([yb])
    return nc


NS = 8448
NCH = 132


def dn_consts():
    U = np.triu(np.ones((64, 64), np.float32))
    Ls = np.tril(np.ones((64, 64), np.float32), -1)
    cm = np.zeros((128, 512), np.float32)
    cm[:64, 0:64] = U
    cm[:64, 64:128] = Ls
    cm[:, 128:256] = np.eye(128, dtype=np.float32)
    cm[:, 256:384] = 1.0
    return cm


def build_k4():
    nc, st, P = new_prog()
    with st:
        pqkv = P.dram("pqkv", [3, 128, NS], F32, "ExternalInput")
        tapsd = P.dram("taps", [128, 9], F32, "ExternalInput")
        grawd = P.dram("graw", [64, 2 * NCH], F32, "ExternalInput")
        scd = P.dram("scal", [128, 2], F32, "ExternalInput")
        cmd = P.dram("cm", [128, 512], F32, "ExternalInput")
        oT = P.dram("oT", [128, NS], F32, "ExternalOutput")

        cm = P.sbuf([128, 512], F32, "cm_s")
        taps = P.sbuf([128, 9], F32, "taps_s")
        graw = P.sbuf([64, 2 * NCH], F32, "graw_s")
        scl = P.sbuf([128, 2], F32, "scl_s")
        for d, s in [(cm, cmd), (taps, tapsd), (graw, grawd), (scl, scd)]:
            P.dma(d[:], s[:])
        U = cm[0:64, 0:64]
        Ls = cm[0:64, 64:128]
        I64 = cm[0:64, 128:192]
        I128 = cm[:, 128:256]
        ones = cm[:, 256:384]
        ones64 = cm[0:64, 256:384]
        eps = P.sbuf([128, 1], F32, "eps_s")
        P.memset(eps[:], 1e-6)

        qkv = [P.sbuf([128, NS], F32, "qkv%d" % g) for g in range(3)]
        PIECE = 2048
        pg = P.sbuf([128, PIECE + 2], F32, "pg")
        cgt = P.sbuf([128, PIECE], F32, "cgt")
        sg = P.sbuf([128, PIECE], F32, "sg")
        bank = [P.psum([128, 512], F32, "bk%d" % i) for i in range(8)]
        bctr = [0]

        def nb():
            b = bank[bctr[0] % 8]
            bctr[0] += 1
            return b

        segs = [(0, 256)] + [(256 + i * PIECE, PIECE) for i in range(4)]
        for g in range(3):
            for (s0, n) in segs:
                first = (s0 == 0 or s0 == 256)
                last = (s0 + n == 256 or s0 + n == NS)
                lo = s0 - (0 if first else 1)
                hi = s0 + n + (0 if last else 1)
                if first or last:
                    P.memset(pg[:], 0.0)
                P.dma(pg[:, 1 - (s0 - lo):1 + n + (hi - s0 - n)], pqkv[g, :, lo:hi])
                P.ts(cgt[:, 0:n], pg[:, 1:1 + n], taps[:, g * 3 + 1:g * 3 + 2], ALU.mult)
                P.stt(cgt[:, 0:n], pg[:, 0:n], taps[:, g * 3:g * 3 + 1], cgt[:, 0:n], ALU.mult, ALU.add)
                P.stt(cgt[:, 0:n], pg[:, 2:2 + n], taps[:, g * 3 + 2:g * 3 + 3], cgt[:, 0:n], ALU.mult, ALU.add)
                P.act(qkv[g][:, s0:s0 + n], cgt[:, 0:n], AF.Silu)
                if g < 2:
                    P.act(sg[:, 0:n], qkv[g][:, s0:s0 + n], AF.Square)
                    for c0 in range(0, n, 512):
                        w = min(512, n - c0)
                        b = nb()
                        P.mm(b[:, 0:w], ones, sg[:, c0:c0 + w])
                        P.act(sg[:, c0:c0 + w], b[:, 0:w], AF.Sqrt, bias=eps[:])
                    P.recip(sg[:, 0:n], sg[:, 0:n])
                    if g == 0:
                        P.stt(qkv[g][:, s0:s0 + n], qkv[g][:, s0:s0 + n], 128.0 ** -0.5, sg[:, 0:n], ALU.mult, ALU.mult)
                    else:
                        P.tt(qkv[g][:, s0:s0 + n], qkv[g][:, s0:s0 + n], sg[:, 0:n], ALU.mult)
        qT, kT, vT = qkv

        beta = P.sbuf([64, NCH], F32, "beta")
        nbeta = P.sbuf([64, NCH], F32, "nbeta")
        gg = P.sbuf([64, NCH], F32, "gg")
        ea = P.sbuf([128, 1], F32, "ea")
        P.act(beta[:], graw[:, 0:NCH], AF.Sigmoid)
        P.ts(nbeta[:], beta[:], -1.0, ALU.mult)
        P.act(gg[:], graw[:, NCH:2 * NCH], AF.Exp, bias=scl[0:64, 1:2])
        P.act(gg[:], gg[:], AF.Ln, bias=cm[0:64, 256:257])
        P.act(ea[:], scl[:, 0:1], AF.Exp)
        P.ts(gg[:], gg[:], ea[0:64, 0:1], ALU.mult, -1.0, ALU.mult)
        gc = P.sbuf([64, NCH], F32, "gc")
        egc = P.sbuf([64, NCH], F32, "egc")
        bke = P.sbuf([64, NCH], F32, "bke")
        kde = P.sbuf([64, NCH], F32, "kde")
        lastB = P.sbuf([128, NCH], F32, "lastB")
        b = nb()
        P.mm(b[0:64, 0:NCH], U, gg[:, :])
        P.act(gc[:], b[0:64, 0:NCH], AF.Copy)
        P.act(egc[:], gc[:], AF.Exp)
        P.tt(bke[:], beta[:], egc[:], ALU.mult)
        b = nb()
        P.mm(b[:, 0:NCH], ones64, gg[:, :])
        P.act(lastB[:], b[:, 0:NCH], AF.Exp)
        P.tt(kde[:], b[0:64, 0:NCH], gc[:], ALU.subtract)
        P.act(kde[:], kde[:], AF.Exp)

        S = [P.sbuf([128, 128], F32, "S%d" % i) for i in range(2)]
        P.memset(S[0][:], 0.0)
        oacc = P.sbuf([128, NS], F32, "oacc")

        def T(shape, name, n=2):
            return [P.sbuf(shape, F32, "%s%d" % (name, i)) for i in range(n)]
        gB = T([64, 128], "gB", 4); tt_ = T([128, 64], "tt", 4); a1 = T([64, 64], "a1", 4); gs = T([64, 64], "gs", 4)
        gT_ = T([64, 64], "gT", 4); egr = T([128, 64], "egr", 4)
        XX = [T([64, 64], "Xa%d" % i, 4) for i in range(4)]
        ZZ = [T([64, 64], "Za%d" % i, 4) for i in range(4)]
        WW = [T([64, 64], "Wa%d" % i, 4) for i in range(4)]
        vb = T([64, 128], "vb", 4); kbd = T([64, 128], "kbd", 4); kd = T([64, 128], "kd", 4)
        u = T([64, 128], "u", 4); wT = T([128, 64], "wT", 4); qd = T([128, 64], "qd", 4); qk = T([64, 64], "qk", 4)
        vn = T([64, 128], "vn", 4)
        def pre(c):
            cols = slice(64 * c, 64 * c + 64)
            i2 = c % 4
            i4 = c % 4
            X, Z, W = XX[i2], ZZ[i2], WW[i2]
            P.act(gB[i2][:], ones64, AF.Identity, scale=gg[:, c:c + 1])
            yield
            b1 = nb()
            P.mm(b1[:, 0:64], gB[i2][:], U)
            P.act(tt_[i2][:], b1[:, 0:64], AF.Copy)
            P.act(egr[i2][:], tt_[i2][:], AF.Exp)
            P.ts(a1[i2][:], tt_[i2][0:64, :], gc[:, c:c + 1], ALU.subtract, 0.0, ALU.min)
            P.act(gT_[i2][:], a1[i2][:], AF.Exp)
            yield
            P.tt(gT_[i2][:], gT_[i2][:], U, ALU.mult)
            P.ts(a1[i2][:], tt_[i2][0:64, :], gc[:, c:c + 1], ALU.subtract, -1.0, ALU.mult)
            P.ts(a1[i2][:], a1[i2][:], 0.0, ALU.min)
            yield
            P.act(gs[i2][:], a1[i2][:], AF.Exp)
            yield
            P.tt(gs[i2][:], gs[i2][:], Ls, ALU.mult)
            b2 = nb()
            P.mm(b2[0:64, 0:64], kT[:, cols], kT[:, cols])
            x0 = X[0]
            P.stt(x0[:], b2[0:64, 0:64], nbeta[:, c:c + 1], gs[i2][:], ALU.mult, ALU.mult)
            b3 = nb()
            P.transpose(b3[0:64, 0:64], x0[:], I64)
            yield
            z0 = Z[0]
            P.act(z0[:], b3[0:64, 0:64], AF.Copy)
            yield
            w0 = W[0]
            P.tt(w0[:], z0[:], I64, ALU.add)
            yield
            xk, zk, wk = x0, z0, w0
            for lev in range(5):
                xn, zn, wn = X[(lev + 1) % 4], Z[(lev + 1) % 4], W[(lev + 1) % 4]
                bx = nb()
                P.mm(bx[0:64, 0:64], zk[:], xk[:])
                yield
                P.act(xn[:], bx[0:64, 0:64], AF.Copy)
                yield
                if lev < 4:
                    bz = nb()
                    P.mm(bz[0:64, 0:64], xk[:], zk[:])
                    yield
                    P.copy(zn[:], bz[0:64, 0:64])
                    yield
                bw = nb()
                P.mm(bw[0:64, 0:64], xn[:], wk[:])
                yield
                P.tt(wn[:], bw[0:64, 0:64], wk[:], ALU.add)
                yield
                xk, zk, wk = xn, zn, wn
            Wf = wk
            bv = nb()
            P.transpose(bv[0:64, 0:128], vT[:, cols], I128)
            yield
            P.ts(vb[i4][:], bv[0:64, 0:128], beta[:, c:c + 1], ALU.mult)
            yield
            bk_ = nb()
            P.transpose(bk_[0:64, 0:128], kT[:, cols], I128)
            yield
            P.ts(kbd[i4][:], bk_[0:64, 0:128], bke[:, c:c + 1], ALU.mult)
            yield
            P.ts(kd[i4][:], bk_[0:64, 0:128], kde[:, c:c + 1], ALU.mult)
            yield
            bu = nb()
            P.mm(bu[0:64, 0:128], Wf[:], vb[i4][:])
            yield
            P.act(u[i4][:], bu[0:64, 0:128], AF.Copy)
            yield
            bwt = nb()
            P.mm(bwt[:, 0:64], kbd[i4][:], Wf[:])
            yield
            P.act(wT[i4][:], bwt[:, 0:64], AF.Copy)
            yield
            P.tt(qd[i4][:], qT[:, cols], egr[i2][:], ALU.mult)
            yield
            bq = nb()
            P.mm(bq[0:64, 0:64], kT[:, cols], qT[:, cols])
            yield
            P.tt(qk[i4][:], bq[0:64, 0:64], gT_[i2][:], ALU.mult)
            yield

        def scan(c):
            cols = slice(64 * c, 64 * c + 64)
            i4 = c % 4
            Sc, Sn = S[c % 2], S[(c + 1) % 2]
            bs = nb()
            P.mm(bs[0:64, 0:128], wT[i4][:], Sc[:])
            P.tt(vn[i4][:], u[i4][:], bs[0:64, 0:128], ALU.subtract)
            bo = nb()
            P.mm(bo[:, 0:64], Sc[:], qd[i4][:], start=True, stop=False)
            P.mm(bo[:, 0:64], vn[i4][:], qk[i4][:], start=False, stop=True)
            P.act(oacc[:, cols], bo[:, 0:64], AF.Copy)
            bn = nb()
            P.mm(bn[:, 0:128], kd[i4][:], vn[i4][:])
            P.stt(Sn[:], Sc[:], lastB[:, c:c + 1], bn[:, 0:128], ALU.mult, ALU.add)

        def run_pair(gens):
            live = list(gens)
            while live:
                for g_ in list(live):
                    try:
                        next(g_)
                    except StopIteration:
                        live.remove(g_)
        for cp in range(0, NCH, 4):
            run_pair([pre(cp + i_) for i_ in range(4)])
            for i_ in range(4):
                scan(cp + i_)
        for i in range(4):
            P.dma(oT[:, i * 2112:(i + 1) * 2112], oacc[:, i * 2112:(i + 1) * 2112])
        P.finish([oT])
    return nc


D = 2048
NT = 1056
NL = 1024
DFF = 5632


def sumsq_rstd(P, src, nchunks, n, ones, bank, sqtmp, rstd, dim):
    for kc in range(nchunks):
        sq = sqtmp[kc % 2]
        P.act(sq[:, 0:n], src[:, kc, 0:n], AF.Square)
        P.mm(bank[:, 0:n], ones, sq[:, 0:n], start=(kc == 0), stop=(kc == nchunks - 1))
    P.act(rstd[:, 0:n], bank[:, 0:n], AF.Sqrt, scale=1.0 / dim, bias=EPSB(P))
    P.recip(rstd[:, 0:n], rstd[:, 0:n])


def build_k5a():
    nc, st, P = new_prog()
    with st:
        yT = P.dram("yT", [D, NT], F32, "ExternalInput")
        xT = P.dram("xT", [D, NT], F32, "ExternalInput")
        w_out = P.dram("w_out", [D, D], F32, "ExternalInput")
        modT = P.dram("modT", [128, 192], F32, "ExternalInput")
        gains = P.dram("gains", [128, 32], F32, "ExternalInput")
        onesd = P.dram("ones", [128, 128], F32, "ExternalInput")
        ofT = P.dram("ofT", [512, NT], F32, "ExternalInput")
        obT = P.dram("obT", [512, NT], F32, "ExternalInput")
        gtT = P.dram("gtT", [512, NT], F32, "ExternalInput")
        dngd = P.dram("dng", [128, 1], F32, "ExternalInput")
        xmT = P.dram("xmT", [D, NT], F32, "ExternalOutput")
        hT = P.dram("hT", [D, NT], F32, "ExternalOutput")
        dng = P.sbuf([128, 1], F32, "dng_s")
        P.dma(dng[:], dngd[:])
        dn_o = P.sbuf([128, 352], F32, "dn_o")
        dn_b = P.sbuf([128, 352], F32, "dn_b")
        dn_g = P.sbuf([128, 352], F32, "dn_g")

        wob = P.sbuf([128, 16, D], BF16, "wob")
        wst = [P.sbuf([128, 16, 256], F32, "wst%d" % i) for i in range(2)]
        mods = P.sbuf([128, 96, 2], F32, "mods")
        gs = P.sbuf([128, 32], F32, "gs")
        ones = P.sbuf([128, 128], F32, "ones_s")
        G1 = P.sbuf([128, 16, 2], F32, "G1")
        A2 = P.sbuf([128, 16, 2], F32, "A2")
        yb = P.sbuf([128, 16, 352], BF16, "yb")
        ystage = [P.sbuf([128, 352], F32, "ystage%d" % i) for i in range(2)]
        z = P.sbuf([128, 16, 352], F32, "z")
        xg = P.sbuf([128, 16, 352], F32, "xg")
        sqt = [P.sbuf([128, 352], F32, "sqt%d" % i) for i in range(2)]
        tmp = [P.sbuf([128, 352], F32, "tmp%d" % i) for i in range(2)]
        hout = [P.sbuf([128, 352], F32, "hout%d" % i) for i in range(2)]
        rstd = P.sbuf([128, 352], F32, "rstd")
        banks = [P.psum([128, 512], F32, "bank%d" % i) for i in range(6)]
        nbank = P.psum([128, 512], F32, "nbank")

        P.dma(mods[:], modT.re("k (c r) -> k c r", r=2)[:, :, :])
        P.dma(gs[:], gains[:])
        P.dma(ones[:], onesd[:])
        w_r = w_out.re("(kc k) c -> k kc c", k=128)
        for s in range(8):
            ws_ = wst[s % 2]
            for kh in range(2):
                P.dma(ws_[:, kh * 8:(kh + 1) * 8, :], w_r[:, kh * 8:(kh + 1) * 8, s * 256:(s + 1) * 256], nowaw=True)
            P.copy(wob[:, :, s * 256:(s + 1) * 256], ws_[:], eng=('dve' if s % 2 == 0 else 'pool'))
        for r in range(2):
            P.tt(G1[:, :, r], mods[:, 32:48, r], gs[:, 0:16], ALU.mult)
            P.ts(A2[:, :, r], mods[:, 64:80, r], 1.0, ALU.add)
            P.tt(A2[:, :, r], A2[:, :, r], gs[:, 16:32], ALU.mult)
        yT_r = yT.re("(kc k) n -> k kc n", k=128)
        xT_r = xT.re("(kc k) n -> k kc n", k=128)
        xm_r = xmT.re("(kc k) n -> k kc n", k=128)
        hT_r = hT.re("(kc k) n -> k kc n", k=128)
        bi = 0
        for tg in range(3):
            c0 = tg * 352
            nl = 352 if tg < 2 else 320
            rngs = [(0, nl, 0)] + ([(nl, 352, 1)] if nl < 352 else [])
            for kc in range(16):
                ys_ = ystage[kc % 2]
                if 8 <= kc < 12:
                    hh = kc - 8
                    P.dma(dn_o[:], ofT[hh * 128:(hh + 1) * 128, c0:c0 + 352])
                    P.dma(dn_b[:], obT[hh * 128:(hh + 1) * 128, c0:c0 + 352])
                    P.dma(dn_g[:], gtT[hh * 128:(hh + 1) * 128, c0:c0 + 352])
                    P.tt(dn_o[:], dn_o[:], dn_b[:], ALU.add)
                    P.act(dn_b[:], dn_o[:], AF.Square)
                    P.mm(nbank[:, 0:352], ones[:], dn_b[:])
                    P.act(dn_b[:], nbank[:, 0:352], AF.Sqrt, scale=1.0 / 128, bias=EPSB(P))
                    P.recip(dn_b[:], dn_b[:])
                    P.stt(dn_o[:], dn_o[:], dng[:, 0:1], dn_b[:], ALU.mult, ALU.mult)
                    P.act(dn_g[:], dn_g[:], AF.Silu)
                    P.tt(ys_[:], dn_o[:], dn_g[:], ALU.mult)
                else:
                    P.dma(ys_[:], yT_r[:, kc, c0:c0 + 352])
                P.copy(yb[:, kc, :], ys_[:], eng=('dve' if kc % 2 == 0 else 'pool'))
            P.dma(xg[:, 0:8, :], xT_r[:, 0:8, c0:c0 + 352])
            P.dma(xg[:, 8:16, :], xT_r[:, 8:16, c0:c0 + 352])
            for m in range(16):
                bk = banks[bi % 6]
                bi += 1
                for kc in range(16):
                    P.mm(bk[:, 0:352], wob[:, kc, m * 128:(m + 1) * 128], yb[:, kc, :], start=(kc == 0), stop=(kc == 15))
                P.act(z[:, m, :], bk[:, 0:352], AF.Copy)
            sumsq_rstd(P, z, 16, 352, ones[:], nbank, sqt, rstd, D)
            for kc in range(16):
                t = tmp[kc % 2]
                P.tt(t[:], z[:, kc, :], rstd[:], ALU.mult)
                for (a, b, r) in rngs:
                    P.stt(xg[:, kc, a:b], t[:, a:b], G1[:, kc, r:r + 1], xg[:, kc, a:b], ALU.mult, ALU.add)
            sumsq_rstd(P, xg, 16, 352, ones[:], nbank, sqt, rstd, D)
            for kc in range(16):
                t = tmp[kc % 2]
                ho = hout[kc % 2]
                P.tt(t[:], xg[:, kc, :], rstd[:], ALU.mult)
                for (a, b, r) in rngs:
                    P.ts(ho[:, a:b], t[:, a:b], A2[:, kc, r:r + 1], ALU.mult, mods[:, 48 + kc, r:r + 1], ALU.add)
                P.dma(hT_r[:, kc, c0:c0 + 352], ho[:])
            P.dma(xm_r[:, 0:8, c0:c0 + 352], xg[:, 0:8, :])
            P.dma(xm_r[:, 8:16, c0:c0 + 352], xg[:, 8:16, :])
        P.finish([xmT, hT])
    return nc


NP5 = 1060


def build_k5b():
    nc, st, P = new_prog()
    with st:
        hp = P.dram("hp", [D, NP5], F32, "ExternalInput")
        xmT = P.dram("xmT", [D, NT], F32, "ExternalInput")
        w_up = P.dram("w_up", [D, 2 * DFF], F32, "ExternalInput")
        wcv = P.dram("wcv", [128, 88 * 3], F32, "ExternalInput")
        w_dn = P.dram("w_dn", [DFF, D], F32, "ExternalInput")
        modT = P.dram("modT", [128, 192], F32, "ExternalInput")
        gains = P.dram("gains", [128, 16], F32, "ExternalInput")
        onesd = P.dram("ones", [128, 128], F32, "ExternalInput")
        xoT = P.dram("xoT", [D, NT], F32, "ExternalOutput")

        stf = [P.sbuf([128, 5632], F32, "stf%d" % i) for i in range(2)]
        stb = [P.sbuf([128, 5632], BF16, "stb%d" % i) for i in range(2)]
        mods = P.sbuf([128, 96, 2], F32, "mods")
        gs = P.sbuf([128, 16], F32, "gs")
        wc = P.sbuf([128, 88, 3], F32, "wc")
        ones = P.sbuf([128, 128], F32, "ones_s")
        G2 = P.sbuf([128, 16, 2], F32, "G2")
        hg = P.sbuf([128, 16, 376], BF16, "hg")
        hst = [P.sbuf([128, 376], F32, "hst%d" % i) for i in range(2)]
        gT = P.sbuf([128, 44, 372], BF16, "gT")
        dn = P.sbuf([128, 16, 372], F32, "dn")
        xg = P.sbuf([128, 16, 372], F32, "xg")
        ca = [P.sbuf([128, 372], F32, "ca%d" % i) for i in range(2)]
        cb = [P.sbuf([128, 372], F32, "cb%d" % i) for i in range(2)]
        sqt = [P.sbuf([128, 372], F32, "sqt%d" % i) for i in range(2)]
        tmp = [P.sbuf([128, 372], F32, "tmp%d" % i) for i in range(2)]
        rstd = P.sbuf([128, 372], F32, "rstd")
        banks = [P.psum([128, 512], F32, "bank%d" % i) for i in range(6)]
        nbank = P.psum([128, 512], F32, "nbank")

        P.dma(mods[:], modT.re("k (c r) -> k c r", r=2)[:, :, :])
        P.dma(gs[:], gains[:])
        P.dma(ones[:], onesd[:])
        P.dma(wc[:], wcv.re("k (c t) -> k c t", t=3)[:, :, :])
        for r in range(2):
            P.tt(G2[:, :, r], mods[:, 80:96, r], gs[:], ALU.mult)
        hp_r = hp.re("(kc k) n -> k kc n", k=128)
        xm_r = xmT.re("(kc k) n -> k kc n", k=128)
        xo_r = xoT.re("(kc k) n -> k kc n", k=128)
        wu_r = w_up.re("(kc k) c -> k kc c", k=128)
        wd_r = w_dn.re("(f k) c -> k f c", k=128)
        groups = [(0, 344, [(0, 342)], 0, 342, 342),
                  (342, 344, [(0, 342)], 342, 342, 342),
                  (684, 376, [(0, 340), (342, 32)], 684, 372, 340)]
        si = 0
        bi = 0
        for (u0, un, segs, xc0, gn, nl) in groups:
            for kc in range(16):
                hs_ = hst[kc % 2]
                P.dma(hs_[:, 0:un], hp_r[:, kc, u0:u0 + un])
                P.copy(hg[:, kc, 0:un], hs_[:, 0:un], eng=('dve' if kc % 2 == 0 else 'pool'))
            P.dma(xg[:, 0:8, 0:gn], xm_r[:, 0:8, xc0:xc0 + gn])
            P.dma(xg[:, 8:16, 0:gn], xm_r[:, 8:16, xc0:xc0 + gn])
            for f in range(44):
                sf, sb_ = stf[si % 2], stb[si % 2]
                si += 1
                sfv = sf.v(sf.t[:, 0:4096].rearrange("p (a b c) -> p a b c", a=16, b=2))
                sbv = sb_.v(sb_.t[:, 0:4096].rearrange("p (a b c) -> p a b c", a=16, b=2))
                P.dma(View(sf, sfv.ap[:, :, 0, :]), wu_r[:, :, f * 128:(f + 1) * 128], nowaw=True)
                P.dma(View(sf, sfv.ap[:, :, 1, :]), wu_r[:, :, DFF + f * 128:DFF + (f + 1) * 128], nowaw=True)
                P.copy(sb_[:, 0:4096], sf[:, 0:4096], eng='pool')
                bka = banks[bi % 6]
                bkb = banks[(bi + 1) % 6]
                bi += 2
                for kc in range(16):
                    P.mm(bka[:, 0:un], View(sb_, sbv.ap[:, kc, 0, :]), hg[:, kc, 0:un], start=(kc == 0), stop=(kc == 15))
                for kc in range(16):
                    P.mm(bkb[:, 0:un], View(sb_, sbv.ap[:, kc, 1, :]), hg[:, kc, 0:un], start=(kc == 0), stop=(kc == 15))
                ca_, cb_ = ca[f % 2], cb[f % 2]
                goff = 0
                for (lo, n) in segs:
                    for (cc_, bk, ch) in [(ca_, bka, f), (cb_, bkb, 44 + f)]:
                        P.ts(cc_[:, goff:goff + n], bk[:, lo + 1:lo + 1 + n], wc[:, ch, 1:2], ALU.mult)
                        P.stt(cc_[:, goff:goff + n], bk[:, lo:lo + n], wc[:, ch, 0:1], cc_[:, goff:goff + n], ALU.mult, ALU.add)
                        P.stt(cc_[:, goff:goff + n], bk[:, lo + 2:lo + 2 + n], wc[:, ch, 2:3], cc_[:, goff:goff + n], ALU.mult, ALU.add)
                    goff += n
                P.act(ca_[:, 0:gn], ca_[:, 0:gn], AF.Silu)
                P.tt(gT[:, f, 0:gn], ca_[:, 0:gn], cb_[:, 0:gn], ALU.mult)
            for m in range(16):
                sf, sb_ = stf[si % 2], stb[si % 2]
                si += 1
                sfv = sf.v(sf.t[:, :].rearrange("p (f c) -> p f c", f=44))
                sbv = sb_.v(sb_.t[:, :].rearrange("p (f c) -> p f c", f=44))
                P.dma(View(sf, sfv.ap[:, 0:22, :]), wd_r[:, 0:22, m * 128:(m + 1) * 128], nowaw=True)
                P.dma(View(sf, sfv.ap[:, 22:44, :]), wd_r[:, 22:44, m * 128:(m + 1) * 128], nowaw=True)
                P.copy(sb_[:], sf[:], eng='pool')
                bk = banks[bi % 6]
                bi += 1
                for f in range(44):
                    P.mm(bk[:, 0:gn], View(sb_, sbv.ap[:, f, :]), gT[:, f, 0:gn], start=(f == 0), stop=(f == 43))
                P.act(dn[:, m, 0:gn], bk[:, 0:gn], AF.Copy)
            sumsq_rstd(P, dn, 16, gn, ones[:], nbank, sqt, rstd, D)
            rngs = [(0, nl, 0)] + ([(nl, gn, 1)] if nl < gn else [])
            for kc in range(16):
                t = tmp[kc % 2]
                P.tt(t[:, 0:gn], dn[:, kc, 0:gn], rstd[:, 0:gn], ALU.mult)
                for (a, b, r) in rngs:
                    P.stt(xg[:, kc, a:b], t[:, a:b], G2[:, kc, r:r + 1], xg[:, kc, a:b], ALU.mult, ALU.add)
            P.dma(xo_r[:, 0:8, xc0:xc0 + gn], xg[:, 0:8, 0:gn])
            P.dma(xo_r[:, 8:16, xc0:xc0 + gn], xg[:, 8:16, 0:gn])
        P.finish([xoT])
    return nc


_PROGS = {}


def _prog(name, fn):
    if name not in _PROGS:
        _PROGS[name] = fn()
    return _PROGS[name]


def _run(name, fn, maps):
    nc = fn()
    res = run_bass_kernel_spmd(nc, maps, core_ids=list(range(8)))
    return res.results


def _c(a):
    return np.ascontiguousarray(a, dtype=np.float32)


def kernel(x, c, ctx, c_ctx, w_ada, b_ada, norm_mix_pre, norm_mix_post, norm_ffn_pre,
           norm_ffn_post, w_in, w_out, attn_q_norm, attn_k_norm, hy_short, hy_w1, hy_b1,
           hy_w2, hy_b2, hy_w3, hy_b3, hy_w4, hy_freq, hy_skip, dn_short, dn_a_log,
           dn_dt_bias, dn_norm, df_lambda, df_norm, ffn_up, ffn_conv, ffn_down):
    f = lambda a: np.asarray(a, dtype=np.float32)
    x = f(x)[0]; ctxv = f(ctx)[0]; c = f(c); c_ctx = f(c_ctx)
    w_ada = f(w_ada); b_ada = f(b_ada); w_in = f(w_in); w_out = f(w_out)
    ffn_up = f(ffn_up); ffn_conv = f(ffn_conv); ffn_down = f(ffn_down)
    ones = np.ones((128, 128), np.float32)
    ident = np.eye(128, dtype=np.float32)
    cc = np.stack([c[0], c_ctx], axis=-1).reshape(16, 128, 2).transpose(1, 0, 2).reshape(128, 32)
    maps = []
    for j in range(8):
        l, q = j // 4, j % 4
        maps.append({"cc": _c(cc), "w": _c(w_ada[l][:, q * 3072:(q + 1) * 3072]),
                     "b2": _c(np.broadcast_to(b_ada[l][None, q * 3072:(q + 1) * 3072], (2, 3072)))})
    r = _run('k0', build_k0, maps)
    mod = np.zeros((2, 2, 12288), np.float32)
    for j in range(8):
        l, q = j // 4, j % 4
        mod[l][:, q * 3072:(q + 1) * 3072] = r[j]["mod"]
    cos, sin = rope_tables()
    cm1 = const_mats()
    cm4 = dn_consts()
    hyc = [(hy_consts(LL, j), hy_consts(LC, j)) for j in range(8)]

    def shard_T(lat, cx, j):
        return _c(np.concatenate([lat[j * 1024:(j + 1) * 1024], cx[j * 32:(j + 1) * 32]], axis=0).T)

    for L in range(2):
        modT = _c(mod[L].reshape(2, 96, 128).transpose(2, 1, 0).reshape(128, 192))
        gain = _c(f(norm_mix_pre)[L].reshape(16, 128).T)
        qkg = _c(np.stack([np.tile(f(attn_q_norm)[L], 2), np.tile(f(attn_k_norm)[L], 2)], axis=1))
        maps = []
        for j in range(8):
            cj = np.tile(cos[j * 1024:(j + 1) * 1024].T, (4, 1))
            sj = np.tile(sin[j * 1024:(j + 1) * 1024].T, (4, 1))
            maps.append({"xT": shard_T(x, ctxv, j), "modT": modT, "gain": gain, "w_in": _c(w_in[L]), "qkg": qkg,
                         "cosT": _c(cj), "sinT": _c(sj), "cmat": cm1})
        r = _run('k1', build_k1, maps)
        pT = [r[j]["pT"] for j in range(8)]
        lat = np.concatenate([p[:, :1024] for p in pT], axis=1)
        cxp = np.concatenate([p[:, 1024:] for p in pT], axis=1)
        full = np.concatenate([cxp, lat], axis=1)
        kT = np.concatenate([full[512:640].reshape(2, 64, 8448), full[4880:5392].reshape(8, 64, 8448)], axis=0)

        def vt(rows, nh, dv):
            v = full[rows].T.reshape(66, 128, nh, dv)
            return _c(v.transpose(2, 1, 0, 3).reshape(nh, 128, 66 * dv))
        vv = vt(slice(640, 768), 2, 64)
        vd = vt(slice(5392, 5904), 4, 128)
        lam_init = 0.8 - 0.6 * math.exp(-0.3 * L)
        misc = np.zeros((128, 4), np.float32)
        misc[:, 0] = f(df_norm)[L]; misc[:, 1] = lam_init; misc[:, 2] = 1.0 - lam_init
        lamv = _c(np.broadcast_to(f(df_lambda)[L].reshape(1, 256), (128, 256)))
        maps = []
        for j in range(8):
            q = np.concatenate([pT[j][0:512].reshape(8, 64, 1056), pT[j][4368:4880].reshape(8, 64, 1056)], axis=0)
            maps.append({"qT": _c(q), "kT": _c(kT), "vv": vv, "vd": vd, "lamv": lamv, "misc": misc, "ones": ones})
        maps_mix = maps
        maps = []
        hs_ = f(hy_short)[L]
        for j in range(8):
            rows = [768 + g * 512 + 64 * j for g in range(3)]
            pl = np.stack([lat[r0:r0 + 64] for r0 in rows])
            pc = np.stack([cxp[r0:r0 + 64] for r0 in rows])
            sw = np.stack([hs_[:, g * 512 + 64 * j:g * 512 + 64 * j + 64].T for g in range(3)], axis=1).reshape(64, 9)
            vec = np.zeros((64, 8), np.float32)
            vec[:, 0] = f(hy_b1)[L]; vec[:, 1] = f(hy_b2)[L]; vec[:, 2] = f(hy_b3)[L]; vec[:, 3] = f(hy_freq)[L]
            vec[:, 4] = f(hy_skip)[L][64 * j:64 * j + 64]
            w4 = f(hy_w4)[L]
            w4s = np.concatenate([w4[:, 64 * j:64 * j + 64], w4[:, 512 + 64 * j:512 + 64 * j + 64]], axis=1)
            (zl, wl), (zc, wc) = hyc[j]
            maps.append({"pl": _c(pl), "pc": _c(pc), "sw": _c(sw), "w1": _c(f(hy_w1)[L]),
                         "w23": _c(np.concatenate([f(hy_w2)[L], f(hy_w3)[L]], axis=1)), "w4s": _c(w4s), "vec": vec,
                         "zl": zl, "zc": zc, "wl": wl, "wc": wc, "ident": ident})
        for j in range(8):
            maps_mix[j].update(maps[j])
        maps = []
        ds_ = f(dn_short)[L]
        for j in range(8):
            h, d = j % 4, j // 4
            rows = [2304 + g * 512 + h * 128 for g in range(3)]
            seq = full
            if d == 1:
                seq = np.concatenate([cxp[:, ::-1], lat[:, ::-1]], axis=1)
            pq = np.stack([seq[r0:r0 + 128] for r0 in rows])
            braw = seq[4352 + d * 4 + h].reshape(132, 64).T
            araw = seq[4352 + 8 + d * 4 + h].reshape(132, 64).T
            taps = np.stack([ds_[:, g * 512 + h * 128:g * 512 + (h + 1) * 128].T for g in range(3)], axis=1)
            if d == 1:
                taps = taps[:, :, ::-1]
            scal = np.zeros((128, 2), np.float32)
            scal[:, 0] = f(dn_a_log)[L, d, h]; scal[:, 1] = f(dn_dt_bias)[L, d, h]
            maps.append({"pqkv": _c(pq), "taps": _c(taps.reshape(128, 9)), "graw": _c(np.concatenate([braw, araw], axis=1)),
                         "scal": scal, "cm": cm4})
        for j in range(8):
            maps_mix[j].update(maps[j])
        r = _run('mix', build_mix, maps_mix)
        yaT = [r[j]["yaT"] for j in range(8)]
        ydT = [r[j]["ydT"] for j in range(8)]
        ybf = np.concatenate([r[j]["yb"] for j in range(8)], axis=0)
        yb_lat, yb_ctx = ybf[:, :8192], ybf[:, 8192:]
        of_full = np.concatenate([r[j]["oT"] for j in range(4)], axis=0)
        ob_s = [r[4 + j]["oT"] for j in range(4)]
        ob_full = np.concatenate([np.concatenate([o[:, :256][:, ::-1], o[:, 256:][:, ::-1]], axis=1) for o in ob_s], axis=0)
        gate_lat, gate_ctx = lat[3840:4352], cxp[3840:4352]
        gains = _c(np.concatenate([f(norm_mix_post)[L].reshape(16, 128).T, f(norm_ffn_pre)[L].reshape(16, 128).T], axis=1))
        dng = _c(f(dn_norm)[L].reshape(128, 1))
        maps = []
        for j in range(8):
            sl, sc_ = slice(j * 1024, (j + 1) * 1024), slice(j * 32, (j + 1) * 32)
            ybj = np.concatenate([yb_lat[:, sl], yb_ctx[:, sc_]], axis=1)
            yT = np.concatenate([yaT[j], ybj, np.zeros((512, 1056), np.float32), ydT[j]], axis=0)
            ofj = np.concatenate([of_full[:, 256:][:, sl], of_full[:, :256][:, sc_]], axis=1)
            obj = np.concatenate([ob_full[:, 256:][:, sl], ob_full[:, :256][:, sc_]], axis=1)
            gtj = np.concatenate([gate_lat[:, sl], gate_ctx[:, sc_]], axis=1)
            maps.append({"yT": _c(yT), "xT": shard_T(x, ctxv, j), "w_out": _c(w_out[L]), "modT": modT, "gains": gains,
                         "ones": ones, "ofT": _c(ofj), "obT": _c(obj), "gtT": _c(gtj), "dng": dng})
        r = _run('k5a', build_k5a, maps)
        xm = [r[j]["xmT"] for j in range(8)]
        hh = [r[j]["hT"] for j in range(8)]
        h_lat = np.concatenate([a[:, :1024] for a in hh], axis=1).T
        h_ctx = np.concatenate([a[:, 1024:] for a in hh], axis=1).T
        z1 = np.zeros((1, 2048), np.float32)

        def hp(j):
            la = np.concatenate([h_lat[j * 1024 - 1:j * 1024] if j > 0 else z1, h_lat[j * 1024:(j + 1) * 1024],
                                 h_lat[(j + 1) * 1024:(j + 1) * 1024 + 1] if j < 7 else z1], axis=0)
            cx_ = np.concatenate([h_ctx[j * 32 - 1:j * 32] if j > 0 else z1, h_ctx[j * 32:(j + 1) * 32],
                                  h_ctx[(j + 1) * 32:(j + 1) * 32 + 1] if j < 7 else z1], axis=0)
            return _c(np.concatenate([la, cx_], axis=0).T)
        wcv = _c(ffn_conv[L].reshape(3, 88, 128).transpose(2, 1, 0).reshape(128, 264))
        g5 = _c(f(norm_ffn_post)[L].reshape(16, 128).T)
        maps = [{"hp": hp(j), "xmT": xm[j], "w_up": _c(ffn_up[L]), "wcv": wcv, "w_dn": _c(ffn_down[L]), "modT": modT,
                 "gains": g5, "ones": ones} for j in range(8)]
        r = _run('k5b', build_k5b, maps)
        xo = [r[j]["xoT"] for j in range(8)]
        x = np.ascontiguousarray(np.concatenate([a[:, :1024] for a in xo], axis=1).T)
        ctxv = np.ascontiguousarray(np.concatenate([a[:, 1024:] for a in xo], axis=1).T)
    return x[None].astype(np.float32)
```
